# Optimizing a Trainium2 kernel written in Bass

```python
import math
import jax
import jax.numpy as jnp
from jax import lax
import numpy as np

D_MODEL = 1024
BATCH = 8
SEQ = 2048
DEPTH = 2

D_MIX = D_MODEL
HEAD_DIM = 64
RWKV_WIDTH = D_MIX // 2
RWKV_HEADS = RWKV_WIDTH // HEAD_DIM
MOBA_WIDTH = D_MIX - RWKV_WIDTH
MOBA_HEADS = MOBA_WIDTH // HEAD_DIM
DECAY_LORA = 64
ICLR_LORA = 64
GATE_LORA = 128
RWKV_PROJ = 3 * RWKV_WIDTH + DECAY_LORA + ICLR_LORA + GATE_LORA
MOBA_PROJ = 3 * MOBA_WIDTH
D_PROJ = RWKV_PROJ + MOBA_PROJ
LNX_EPS = 64e-5
MOBA_BLOCK = 256
MOBA_TOPK = 3
MOBA_Q_CHUNK = 16
REL_BUCKETS = 32
REL_MAX_DISTANCE = 1024
MEM_LEN = 256
XATTN_HEADS = 4
XATTN_HEAD_DIM = D_MODEL // XATTN_HEADS
D_FF = ((8 * D_MODEL // 3 + 127) // 128) * 128
FFN_RES_WEIGHT = 0.5
NORM_EPS = 1e-6

kernel_name = "hybrid_rwkv7_moba_macaron"


def rms_norm(x, g):
    xf = x.astype(jnp.float32)
    y = xf * lax.rsqrt(jnp.mean(xf * xf, axis=-1, keepdims=True) + NORM_EPS)
    return (y * g.astype(jnp.float32)).astype(x.dtype)


def swiglu_ffn(h, w_in, w_out):
    gate, up = jnp.split(h @ w_in, 2, axis=-1)
    return (jax.nn.silu(gate) * up) @ w_out


def token_shift(z):
    return jnp.pad(z, ((0, 0), (1, 0), (0, 0)))[:, :-1]


def rel_bucket(dist):
    n = jnp.maximum(dist, 0)
    max_exact = REL_BUCKETS // 2
    nf = jnp.maximum(n, 1).astype(jnp.float32)
    large = max_exact + (jnp.log(nf / max_exact) / math.log(REL_MAX_DISTANCE / max_exact)
                         * (REL_BUCKETS - max_exact)).astype(jnp.int32)
    large = jnp.minimum(large, REL_BUCKETS - 1)
    return jnp.where(n < max_exact, n, large)


def rwkv7_time_mix(p, mu, w0, w_up, a0, a_up, g_up, k_k, k_a, r_k, ln_g, ln_b):
    B, S, _ = p.shape
    H, Dh, W = RWKV_HEADS, HEAD_DIM, RWKV_WIDTH
    dt = p.dtype
    f32 = jnp.float32
    p = p + (token_shift(p) - p) * mu
    r, k, v, w_lo, a_lo, g_lo = jnp.split(
        p, [W, 2 * W, 3 * W, 3 * W + DECAY_LORA, 3 * W + DECAY_LORA + ICLR_LORA], axis=-1)
    w_log = -jax.nn.softplus(-(w0 + jnp.tanh(w_lo) @ w_up).astype(f32)) - 0.5
    decay = jnp.exp(-jnp.exp(w_log))
    a = jax.nn.sigmoid(a0 + a_lo @ a_up)
    g = jax.nn.sigmoid(g_lo) @ g_up
    kk = k * k_k
    k = k * (1 + (a - 1) * k_a)
    heads = lambda z: z.reshape(B, S, H, Dh).astype(f32)
    r, k, v, a, decay, kk = (heads(z) for z in (r, k, v, a, decay, kk))
    kk = kk / jnp.maximum(jnp.sqrt(jnp.sum(kk * kk, axis=-1, keepdims=True)), 1e-12)
    b = kk * a

    def step(state, inp):
        r_t, w_t, k_t, v_t, kk_t, b_t = inp
        sa = jnp.einsum('bhij,bhj->bhi', state, -kk_t)
        state = (state * w_t[:, :, None, :] + sa[..., None] * b_t[:, :, None, :]
                 + v_t[..., None] * k_t[:, :, None, :])
        return state, jnp.einsum('bhij,bhj->bhi', state, r_t)

    tm = lambda z: jnp.swapaxes(z, 0, 1)
    state0 = jnp.zeros((B, H, Dh, Dh), f32)
    _, y = lax.scan(step, state0, tuple(tm(z) for z in (r, decay, k, v, kk, b)))
    y = tm(y)
    mean = jnp.mean(y, axis=-1, keepdims=True)
    var = jnp.mean(jnp.square(y - mean), axis=-1, keepdims=True)
    y = ((y - mean) * lax.rsqrt(var + LNX_EPS)).reshape(B, S, W) * ln_g + ln_b
    bonus = jnp.sum(r * k * r_k, axis=-1, keepdims=True) * v
    y = y + bonus.reshape(B, S, W)
    return (y * g).astype(dt)


def moba_attention(q, k, v, rel_bias):
    B, S, _ = q.shape
    H, Dh, BLK, QC = MOBA_HEADS, HEAD_DIM, MOBA_BLOCK, MOBA_Q_CHUNK
    f32 = jnp.float32
    nb = -(-S // BLK)
    pad = nb * BLK - S
    to_heads = lambda z: z.reshape(B, S, H, Dh).transpose(0, 2, 1, 3)
    q = to_heads(q) * (Dh ** -0.5)
    k = jnp.pad(to_heads(k), ((0, 0), (0, 0), (0, pad), (0, 0)))
    v = jnp.pad(to_heads(v), ((0, 0), (0, 0), (0, pad), (0, 0)))
    kb = k.reshape(B, H, nb, BLK, Dh)
    vb = v.reshape(B, H, nb, BLK, Dh)
    k_mean = jnp.mean(kb.astype(f32), axis=3).astype(q.dtype)
    topk = min(MOBA_TOPK, nb)
    n_chunks = S // QC
    qc = q.reshape(B, H, n_chunks, QC, Dh).transpose(2, 0, 1, 3, 4)
    b_ix = jnp.arange(B)[:, None, None, None]
    h_ix = jnp.arange(H)[None, :, None, None]
    blk_ar = jnp.arange(BLK)

    def chunk_fn(args):
        q_c, c = args
        q_start = c * QC
        q_pos = q_start + jnp.arange(QC)
        q_blk = q_start // BLK
        gate = jnp.einsum('bhqd,bhnd->bhqn', q_c, k_mean)
        gate = jnp.where(jnp.arange(nb) < q_blk, gate, -jnp.inf)
        _, idx = lax.top_k(gate, topk)
        valid = idx < q_blk
        k_sel = kb[b_ix, h_ix, idx]
        v_sel = vb[b_ix, h_ix, idx]
        k_pos_sel = idx[..., None] * BLK + blk_ar
        bias_sel = rel_bias[h_ix[..., None], rel_bucket(q_pos[None, None, :, None, None] - k_pos_sel)]
        s_sel = jnp.einsum('bhqd,bhqtkd->bhqtk', q_c, k_sel).astype(f32) + bias_sel.astype(f32)
        s_sel = jnp.where(valid[..., None], s_sel, -jnp.inf)
        k_own = lax.dynamic_index_in_dim(kb, q_blk, axis=2, keepdims=False)
        v_own = lax.dynamic_index_in_dim(vb, q_blk, axis=2, keepdims=False)
        own_pos = q_blk * BLK + blk_ar
        bias_own = rel_bias[:, rel_bucket(q_pos[:, None] - own_pos[None, :])]
        s_own = jnp.einsum('bhqd,bhkd->bhqk', q_c, k_own).astype(f32) + bias_own.astype(f32)[None]
        s_own = jnp.where(own_pos[None, :] <= q_pos[:, None], s_own, -jnp.inf)
        logits = jnp.concatenate([s_sel.reshape(B, H, QC, topk * BLK), s_own], axis=-1)
        probs = jax.nn.softmax(logits, axis=-1).astype(v.dtype)
        p_sel = probs[..., :topk * BLK].reshape(B, H, QC, topk, BLK)
        p_own = probs[..., topk * BLK:]
        return (jnp.einsum('bhqtk,bhqtkd->bhqd', p_sel, v_sel)
                + jnp.einsum('bhqk,bhkd->bhqd', p_own, v_own))

    out = lax.map(chunk_fn, (qc, jnp.arange(n_chunks)))
    return out.transpose(1, 0, 3, 2, 4).reshape(B, S, H * Dh)


def memory_cross_attention(h, mem_n, w_q, w_kv, w_o):
    B, S, D = h.shape
    M = mem_n.shape[1]
    q = (h @ w_q).reshape(B, S, XATTN_HEADS, XATTN_HEAD_DIM)
    k, v = jnp.split(mem_n @ w_kv, 2, axis=-1)
    k = k.reshape(B, M, XATTN_HEADS, XATTN_HEAD_DIM)
    v = v.reshape(B, M, XATTN_HEADS, XATTN_HEAD_DIM)
    logits = jnp.einsum('bshd,bmhd->bhsm', q, k).astype(jnp.float32) * (XATTN_HEAD_DIM ** -0.5)
    probs = jax.nn.softmax(logits, axis=-1).astype(v.dtype)
    o = jnp.einsum('bhsm,bmhd->bshd', probs, v).reshape(B, S, D)
    return o @ w_o


def setup_inputs(seed: int = 0) -> dict:
    key = jax.random.key(seed)
    ks = iter(jax.random.split(key, 40))
    f32 = jnp.float32

    def nrm(shape, scale):
        return scale * jax.random.normal(next(ks), shape, f32)

    def dense(fan_in, fan_out):
        return nrm((DEPTH, fan_in, fan_out), fan_in ** -0.5)

    def gain(n):
        return 1.0 + nrm((DEPTH, n), 0.02)

    W = RWKV_WIDTH
    return {
        "x": nrm((BATCH, SEQ, D_MODEL), 1.0),
        "mem": nrm((BATCH, MEM_LEN, D_MODEL), 1.0),
        "rel_bias": nrm((MOBA_HEADS, REL_BUCKETS), 0.5),
        "final_norm_g": 1.0 + nrm((D_MODEL,), 0.02),
        "ffn1_norm_g": gain(D_MODEL),
        "ffn1_w_in": dense(D_MODEL, 2 * D_FF),
        "ffn1_w_out": dense(D_FF, D_MODEL),
        "mix_norm_g": gain(D_MODEL),
        "w_mix_in": dense(D_MODEL, D_PROJ),
        "w_mix_out": dense(D_MIX, D_MODEL),
        "rwkv_mu": jax.random.uniform(next(ks), (DEPTH, RWKV_PROJ), f32),
        "rwkv_w0": jnp.linspace(-6.0, -1.0, W, dtype=f32)[None, :] + nrm((DEPTH, W), 0.1),
        "rwkv_w_up": nrm((DEPTH, DECAY_LORA, W), 0.5 * DECAY_LORA ** -0.5),
        "rwkv_a0": nrm((DEPTH, W), 0.1),
        "rwkv_a_up": dense(ICLR_LORA, W),
        "rwkv_g_up": dense(GATE_LORA, W),
        "rwkv_k_k": 0.85 + nrm((DEPTH, W), 0.02),
        "rwkv_k_a": 1.0 + nrm((DEPTH, W), 0.02),
        "rwkv_r_k": nrm((DEPTH, RWKV_HEADS, HEAD_DIM), 0.1),
        "rwkv_ln_g": gain(W),
        "rwkv_ln_b": nrm((DEPTH, W), 0.02),
        "xattn_norm_g": gain(D_MODEL),
        "mem_norm_g": gain(D_MODEL),
        "xattn_w_q": dense(D_MODEL, D_MODEL),
        "xattn_w_kv": dense(D_MODEL, 2 * D_MODEL),
        "xattn_w_o": dense(D_MODEL, D_MODEL),
        "ffn2_norm_g": gain(D_MODEL),
        "ffn2_w_in": dense(D_MODEL, 2 * D_FF),
        "ffn2_w_out": dense(D_FF, D_MODEL),
    }


def reference(x, mem, rel_bias, final_norm_g, ffn1_norm_g, ffn1_w_in, ffn1_w_out, mix_norm_g,
              w_mix_in, w_mix_out, rwkv_mu, rwkv_w0, rwkv_w_up, rwkv_a0, rwkv_a_up, rwkv_g_up,
              rwkv_k_k, rwkv_k_a, rwkv_r_k, rwkv_ln_g, rwkv_ln_b, xattn_norm_g, mem_norm_g,
              xattn_w_q, xattn_w_kv, xattn_w_o, ffn2_norm_g, ffn2_w_in, ffn2_w_out):
    for l in range(DEPTH):
        x = x + FFN_RES_WEIGHT * swiglu_ffn(rms_norm(x, ffn1_norm_g[l]), ffn1_w_in[l], ffn1_w_out[l])
        proj = rms_norm(x, mix_norm_g[l]) @ w_mix_in[l]
        p_rwkv = proj[..., :RWKV_PROJ]
        q_m, k_m, v_m = jnp.split(proj[..., RWKV_PROJ:], 3, axis=-1)
        y_rwkv = rwkv7_time_mix(p_rwkv, rwkv_mu[l], rwkv_w0[l], rwkv_w_up[l], rwkv_a0[l],
                                rwkv_a_up[l], rwkv_g_up[l], rwkv_k_k[l], rwkv_k_a[l], rwkv_r_k[l],
                                rwkv_ln_g[l], rwkv_ln_b[l])
        y_moba = moba_attention(q_m, k_m, v_m, rel_bias)
        x = x + jnp.concatenate([y_rwkv, y_moba], axis=-1) @ w_mix_out[l]
        x = x + memory_cross_attention(rms_norm(x, xattn_norm_g[l]), rms_norm(mem, mem_norm_g[l]),
                                       xattn_w_q[l], xattn_w_kv[l], xattn_w_o[l])
        x = x + FFN_RES_WEIGHT * swiglu_ffn(rms_norm(x, ffn2_norm_g[l]), ffn2_w_in[l], ffn2_w_out[l])
    return rms_norm(x, final_norm_g)
```

```python
import numpy as np
from contextlib import ExitStack
import concourse.bass as bass
import concourse.mybir as mybir
from concourse.bass_utils import run_bass_kernel_spmd

F32 = mybir.dt.float32
BF16 = mybir.dt.bfloat16
AF = mybir.ActivationFunctionType
ALU = mybir.AluOpType
AX = mybir.AxisListType

D = 1024
S = 2048
DEPTH = 2
DFF = 2816
RW = 512
RPROJ = 1792
DPROJ = 3328
MEM = 256
NORM_EPS = 1e-6
LNX_EPS = 64e-5


class Buf:
    def __init__(self, ap, name="", psum=False):
        self.ap = ap
        self.name = name
        self.psum = psum
        self.w = None
        self.r = {}
        self.dsem = None
        self.dcnt = 0

    def __getitem__(self, idx):
        return View([self], self.ap[idx])

    @property
    def v(self):
        return View([self], self.ap)


class View:
    def __init__(self, bufs, ap):
        self.bufs = bufs
        self.ap = ap

    @property
    def v(self):
        return self

    def __getitem__(self, idx):
        return View(self.bufs, self.ap[idx])


def raw(ap):
    return View([], ap)


class Instr:
    __slots__ = ("fn", "deps", "signal", "semval", "dma_inc")

    def __init__(self, fn, deps, dma_inc=None):
        self.fn = fn
        self.deps = deps
        self.signal = False
        self.semval = None
        self.dma_inc = dma_inc


ENGS = ["pe", "act", "dve", "pool", "sp"]


class Prog:
    def __init__(self, nc):
        self.nc = nc
        self.q = {e: [] for e in ENGS}
        self.stack = ExitStack()
        self.esem = {}
        self.n_dsem = 0
        self.all_dma_bufs = []

    def sbuf(self, name, shape, dtype):
        return self.stack.enter_context(self.nc.sbuf_tensor(name, list(shape), dtype))

    def psum(self, name, shape, dtype):
        return self.stack.enter_context(self.nc.psum_tensor(name, list(shape), dtype))

    def _get_dsem(self, buf):
        if buf.dsem is None:
            buf.dsem = self.stack.enter_context(self.nc.semaphore("d%d" % self.n_dsem))
            self.n_dsem += 1
            self.all_dma_bufs.append(buf)
        return buf.dsem

    def _deps(self, eng, ins, outs):
        deps = []
        for v in ins:
            for b in v.bufs:
                if b.w is not None:
                    deps.append(b.w)
                if b.psum:
                    for k, t in b.r.items():
                        if k != eng:
                            deps.append(t)
        for v in outs:
            for b in v.bufs:
                if b.w is not None:
                    deps.append(b.w)
                deps.extend(b.r.values())
        if eng == "pe":
            deps = [d for d in deps if not (d[0] == "E" and d[1] == "pe")]
        return deps

    def _mark(self, tok, key, ins, outs):
        for v in ins:
            for b in v.bufs:
                old = b.r.get(key)
                if old is None or old[2] < tok[2]:
                    b.r[key] = tok
        for v in outs:
            for b in v.bufs:
                b.w = tok
                b.r = {}

    def op(self, eng, fn, ins=(), outs=()):
        ins = [x.v if isinstance(x, Buf) else x for x in ins]
        outs = [x.v if isinstance(x, Buf) else x for x in outs]
        deps = self._deps(eng, ins, outs)
        idx = len(self.q[eng])
        self.q[eng].append(Instr(fn, deps))
        tok = ("E", eng, idx)
        self._mark(tok, eng, ins, outs)
        return tok

    def dma(self, out, in_, queue="sp", **kw):
        out = out.v if isinstance(out, Buf) else out
        in_ = in_.v if isinstance(in_, Buf) else in_
        owner = None
        for v in (out, in_):
            for b in v.bufs:
                owner = b
                break
            if owner is not None:
                break
        assert owner is not None
        sem = self._get_dsem(owner)
        owner.dcnt += 1
        val = 16 * owner.dcnt
        deps = self._deps(queue, [in_], [out])
        oap, iap = out.ap, in_.ap
        self.q[queue].append(Instr(lambda e: e.dma_start(out=oap, in_=iap, **kw), deps, dma_inc=sem))
        tok = ("S", sem, val)
        self._mark(tok, ("S", id(sem)), [in_], [out])
        return tok

    def _last_real(self, e):
        for i in range(len(self.q[e]) - 1, -1, -1):
            ins = self.q[e][i]
            if ins.fn is not None and ins.dma_inc is None:
                return ("E", e, i)
        return None

    def _all_toks(self):
        toks = []
        for e in ENGS:
            t = self._last_real(e)
            if t is not None:
                toks.append(t)
        for b in self.all_dma_bufs:
            toks.append(("S", b.dsem, 16 * b.dcnt))
        return toks

    def barrier(self):
        toks = self._all_toks()
        for e in ENGS:
            deps = [t for t in toks if not (t[0] == "E" and t[1] == e)]
            self.q[e].append(Instr(None, deps))

    def emit(self):
        nc = self.nc
        for e in ENGS:
            self.esem[e] = self.stack.enter_context(nc.semaphore("e_" + e))
        self.q["sp"].append(Instr(None, [t for t in self._all_toks() if not (t[0] == "E" and t[1] == "sp")]))
        for e in ENGS:
            for ins in self.q[e]:
                for d in ins.deps:
                    if d[0] == "E":
                        self.q[d[1]][d[2]].signal = True
        for e in ENGS:
            c = 0
            for ins in self.q[e]:
                if ins.signal:
                    assert ins.fn is not None and ins.dma_inc is None
                    c += 1
                    ins.semval = c
        handles = {"pe": "tensor", "act": "scalar", "dve": "vector", "pool": "gpsimd", "sp": "sync"}
        stats = {}
        with nc.Block() as block:
            for e in ENGS:
                def body(eh, e=e):
                    known = {}
                    nw = 0
                    for ins in self.q[e]:
                        waits = {}
                        for d in ins.deps:
                            if d[0] == "E":
                                sem = self.esem[d[1]]
                                val = self.q[d[1]][d[2]].semval
                            else:
                                sem, val = d[1], d[2]
                            k = id(sem)
                            if known.get(k, 0) >= val:
                                continue
                            if k not in waits or waits[k][1] < val:
                                waits[k] = (sem, val)
                        wl = list(waits.values())
                        for k, (sem, val) in waits.items():
                            known[k] = val
                        nw += len(wl)
                        if ins.fn is None:
                            for sem, val in wl:
                                eh.wait_ge(sem, val)
                            continue
                        for sem, val in wl[:-1]:
                            eh.wait_ge(sem, val)
                        bi = ins.fn(eh)
                        if wl:
                            bi._wait_ge(wl[-1][0], wl[-1][1])
                        if ins.dma_inc is not None:
                            bi.then_inc(ins.dma_inc, 16)
                        elif ins.signal:
                            bi.then_inc(self.esem[e], 1)
                    stats[e] = (len(self.q[e]), nw)
                getattr(block, handles[e])(body)
        self.stats = stats
        return stats


PARAM_SPECS = [
    ("x", [S, D]), ("mem", [MEM, D]), ("rel_bias", [8, 32]), ("final_norm_g", [D]),
    ("ffn1_norm_g", [DEPTH, D]), ("ffn1_w_in", [DEPTH, D, 2 * DFF]), ("ffn1_w_out", [DEPTH, DFF, D]),
    ("mix_norm_g", [DEPTH, D]), ("w_mix_in", [DEPTH, D, DPROJ]), ("w_mix_out", [DEPTH, D, D]),
    ("rwkv_mu", [DEPTH, RPROJ]), ("rwkv_w0", [DEPTH, RW]), ("rwkv_w_up", [DEPTH, 64, RW]),
    ("rwkv_a0", [DEPTH, RW]), ("rwkv_a_up", [DEPTH, 64, RW]), ("rwkv_g_up", [DEPTH, 128, RW]),
    ("rwkv_k_k", [DEPTH, RW]), ("rwkv_k_a", [DEPTH, RW]), ("rwkv_r_k", [DEPTH, 8, 64]),
    ("rwkv_ln_g", [DEPTH, RW]), ("rwkv_ln_b", [DEPTH, RW]),
    ("xattn_norm_g", [DEPTH, D]), ("mem_norm_g", [DEPTH, D]),
    ("xattn_w_q", [DEPTH, D, D]), ("xattn_w_kv", [DEPTH, D, 2 * D]), ("xattn_w_o", [DEPTH, D, D]),
    ("ffn2_norm_g", [DEPTH, D]), ("ffn2_w_in", [DEPTH, D, 2 * DFF]), ("ffn2_w_out", [DEPTH, DFF, D]),
]

GAIN_NAMES = ["ffn1_norm_g", "mix_norm_g", "xattn_norm_g", "mem_norm_g", "ffn2_norm_g"]


class Carver:
    def __init__(self, big, nwords):
        self.big = big
        self.n = nwords
        self.o = 0

    def f32(self, n, pat=None, **kw):
        ap = self.big[:, self.o:self.o + n]
        self.o += n
        assert self.o <= self.n, "scratch overflow"
        return ap.rearrange(pat, **kw) if pat else ap

    def b16(self, n, pat=None, **kw):
        w = (n + 1) // 2
        ap = self.big[:, self.o:self.o + w].bitcast(BF16)
        self.o += w
        assert self.o <= self.n, "scratch overflow"
        return ap.rearrange(pat, **kw) if pat else ap


class K:
    def __init__(self, stages, final_norm=True, debug=False):
        self.stages = stages
        self.final_norm = final_norm
        nc = bass.Bass("TRN2", target_bir_lowering=False)
        self.nc = nc
        self.P = Prog(nc)
        self.dram = {}
        for name, shape in PARAM_SPECS:
            self.dram[name] = nc.dram_tensor(name, shape, F32, kind="ExternalInput").ap()
        self.dram["oh_tab"] = nc.dram_tensor("oh_tab", [32, 2048], F32, kind="ExternalInput").ap()
        self.ebrep_t = nc.dram_tensor("ebrep", [8, 128, 2048], BF16, kind="Internal")
        self.out_ap = nc.dram_tensor("out", [S, D], F32, kind="ExternalOutput").ap()
        self.debug = debug
        if debug:
            self.dbg_ap = nc.dram_tensor("dbg", [128, 16 * 512], F32, kind="ExternalOutput").ap()
            self.DBG = Buf(self.dbg_ap, "DBG")
            self.dbg_n = 0
        self.pb_i = 0
        self.bank_set = list(range(8))
        self.ring_i = 0
        self.ev_i = 0

    def alloc(self):
        P = self.P
        self.xT_t = P.sbuf("xT", [128, 8, S], F32)
        self.xb = [[Buf(self.xT_t[:, c, tb * 512:(tb + 1) * 512], "x%d_%d" % (c, tb)) for tb in range(4)]
                   for c in range(8)]
        self.ident_f = Buf(P.sbuf("ident_f", [128, 128], F32)[:], "ident_f")
        self.ident_b = Buf(P.sbuf("ident_b", [128, 128], BF16)[:], "ident_b")
        self.ones_b = Buf(P.sbuf("ones_b", [128, 128], BF16)[:], "ones_b")
        self.pt_stage = Buf(P.sbuf("pt_stage", [128, 128], F32)[:], "pt_stage")
        self.PT = Buf(P.sbuf("PT", [128, 256], F32)[:], "PT")
        self.ps_t = P.psum("ps", [128, 8, 512], F32)
        self.banks = [Buf(self.ps_t[:, i, :], "bank%d" % i, psum=True) for i in range(8)]
        self.NSLOT = 3
        self.ring_t = P.sbuf("ring", [128, self.NSLOT, 22 * 128], BF16)
        self.ring = [Buf(self.ring_t[:, i, :], "ring%d" % i) for i in range(self.NSLOT)]
        self.BIGW = 27 * 1024
        self.big = P.sbuf("big", [128, self.BIGW], F32)

    def carver(self):
        return Carver(self.big, self.BIGW)

    def dump(self, view, n, name=""):
        if not self.debug:
            return
        if not hasattr(self, "dbg_stg"):
            self.dbg_stg = [Buf(self.P.sbuf("dbgs%d" % i, [128, 512], F32)[:], "dbgs%d" % i) for i in range(2)]
        st = self.dbg_stg[self.dbg_n % 2]
        np_ = view.ap.shape[0]
        self.P.op("pool", lambda e: e.memset(st.ap, 0.0), outs=[st.v])
        bp = 0
        self.copy("dve", st[bp:bp + np_, 0:n], view)
        self.P.dma(self.DBG[:, self.dbg_n * 512:(self.dbg_n + 1) * 512], st.v)
        print("dump slot", self.dbg_n, name)
        self.dbg_n += 1

    def act(self, out, in_, func, bias=None, scale=None):
        oap, iap = out.ap, in_.ap
        ins = [in_]
        kw = {}
        if bias is not None:
            if isinstance(bias, (View, Buf)):
                ins.append(bias)
                kw["bias"] = bias.ap
            else:
                kw["bias"] = bias
        if scale is not None:
            if isinstance(scale, (View, Buf)):
                ins.append(scale)
                kw["scale"] = scale.ap
            else:
                kw["scale"] = scale
        self.P.op("act", lambda e: e.activation(out=oap, in_=iap, func=func, **kw), ins=ins, outs=[out])

    def tt(self, eng, out, in0, in1, op):
        oap, a, b = out.ap, in0.ap, in1.ap
        self.P.op(eng, lambda e: e.tensor_tensor(out=oap, in0=a, in1=b, op=op), ins=[in0, in1], outs=[out])

    def ts(self, eng, out, in0, s1, op0, s2=None, op1=None):
        oap, a = out.ap, in0.ap
        ins = [in0]
        v1 = s1
        if isinstance(s1, (View, Buf)):
            ins.append(s1)
            v1 = s1.ap
        v2 = s2
        if isinstance(s2, (View, Buf)):
            ins.append(s2)
            v2 = s2.ap
        if op1 is None:
            self.P.op(eng, lambda e: e.tensor_scalar(out=oap, in0=a, scalar1=v1, scalar2=None, op0=op0), ins=ins, outs=[out])
        else:
            self.P.op(eng, lambda e: e.tensor_scalar(out=oap, in0=a, scalar1=v1, scalar2=v2, op0=op0, op1=op1),
                      ins=ins, outs=[out])

    def stt(self, out, in0, scalar, in1, op0, op1):
        oap, a, b = out.ap, in0.ap, in1.ap
        ins = [in0, in1]
        sv = scalar
        if isinstance(scalar, (View, Buf)):
            ins.append(scalar)
            sv = scalar.ap
        self.P.op("dve", lambda e: e.scalar_tensor_tensor(out=oap, in0=a, scalar=sv, in1=b, op0=op0, op1=op1),
                  ins=ins, outs=[out])

    def recip(self, out, in_):
        oap, iap = out.ap, in_.ap
        self.P.op("dve", lambda e: e.reciprocal(out=oap, in_=iap), ins=[in_], outs=[out])

    def memset(self, eng, out, val):
        oap = out.ap
        self.P.op(eng, lambda e: e.memset(oap, val), outs=[out])

    def bank(self):
        bs = self.bank_set
        b = self.banks[bs[self.pb_i % len(bs)]]
        self.pb_i += 1
        return b

    def evac_eng(self):
        self.ev_i += 1
        return "act" if self.ev_i % 2 == 0 else "dve"

    def copy(self, eng, out, in_):
        oap, iap = out.ap, in_.ap
        if eng == "act":
            self.P.op("act", lambda e: e.activation(out=oap, in_=iap, func=AF.Copy), ins=[in_], outs=[out])
        else:
            self.P.op(eng, lambda e: e.tensor_copy(out=oap, in_=iap), ins=[in_], outs=[out])

    def mm(self, out, lhsT, rhs, start, stop):
        oap, lap, rap = out.ap, lhsT.ap, rhs.ap
        self.P.op("pe", lambda e: e.matmul(oap, lhsT=lap, rhs=rap, start=start, stop=stop),
                  ins=[lhsT, rhs], outs=[out])

    def transpose(self, out, in_, ident):
        oap, iap, idap = out.ap, in_.ap, ident.ap
        self.P.op("pe", lambda e: e.transpose(out=oap, in_=iap, identity=idap), ins=[in_, ident], outs=[out])

    def load_w(self, w_ap, kc, ncols):
        slot = self.ring[self.ring_i % self.NSLOT]
        self.ring_i += 1
        dst = View([slot], slot.ap[:, 0:kc * ncols].rearrange("p (k n) -> p k n", k=kc))
        src = raw(w_ap.rearrange("(k p) n -> p k n", p=128))
        self.P.dma(dst, src, queue="pool")
        return dst

    def xview(self, cs, tb):
        bufs = [self.xb[c][tb] for c in range(cs.start, cs.stop)]
        return View(bufs, self.xT_t[:, cs, tb * 512:(tb + 1) * 512])

    def setup(self):
        P = self.P
        idf, idb, ones = self.ident_f, self.ident_b, self.ones_b
        P.op("pool", lambda e: e.memset(idf.ap, 0.0), outs=[idf.v])
        P.op("pool", lambda e: e.affine_select(out=idf.ap, in_=idf.ap, compare_op=ALU.not_equal, fill=1.0,
                                               base=0, pattern=[[-1, 128]], channel_multiplier=1),
             ins=[idf.v], outs=[idf.v])
        self.copy("dve", idb.v, idf.v)
        P.op("pool", lambda e: e.memset(ones.ap, 1.0), outs=[ones.v])
        self.pcol = {}
        col = 0
        groups = []
        rows = 0
        cur = []
        plist = [(n, DEPTH * 8) for n in GAIN_NAMES] + [("final_norm_g", 8)]
        plist += [("rwkv_mu", DEPTH * 14)] + [(n, DEPTH * 4) for n in
                                              ["rwkv_w0", "rwkv_a0", "rwkv_k_k", "rwkv_k_a", "rwkv_r_k",
                                               "rwkv_ln_g", "rwkv_ln_b"]]
        for name, nrow in plist:
            if rows + nrow > 128:
                groups.append(cur)
                cur = []
                rows = 0
            cur.append((name, nrow, rows))
            rows += nrow
        groups.append(cur)
        for grp in groups:
            st = self.pt_stage
            P.op("pool", lambda e: e.memset(st.ap, 0.0), outs=[st.v])
            tot = 0
            for name, nrow, r0 in grp:
                ap = self.dram[name]
                if name == "final_norm_g":
                    src = ap.rearrange("(c p) -> c p", p=128)
                elif name == "rwkv_r_k":
                    src = ap.rearrange("l (m two) d -> (l m) (two d)", two=2)
                else:
                    src = ap.rearrange("l (c p) -> (l c) p", p=128)
                P.dma(st[r0:r0 + nrow, :], raw(src))
                self.pcol[name] = col + r0
                tot = r0 + nrow
            bk = self.bank()
            self.transpose(bk[:, 0:128], st.v, self.ident_f)
            self.copy("dve", self.PT[:, col:col + tot], bk[:, 0:tot])
            col += tot
        assert col <= 256
        cv = self.carver()
        stg = [Buf(cv.f32(1024), "xstg%d" % i) for i in range(3)]
        for tt in range(16):
            st = stg[tt % 3]
            P.dma(st.v, raw(self.dram["x"][tt * 128:(tt + 1) * 128, :]))
            tb = tt // 4
            t0 = (tt % 4) * 128
            for h in range(2):
                bk = self.bank()
                for c4 in range(4):
                    c = h * 4 + c4
                    self.transpose(bk[:, c4 * 128:(c4 + 1) * 128], st[:, c * 128:(c + 1) * 128], self.ident_f)
                cs = slice(h * 4, h * 4 + 4)
                dst = View([self.xb[c][tb] for c in range(cs.start, cs.stop)],
                           self.xT_t[:, cs, tb * 512 + t0: tb * 512 + t0 + 128])
                self.copy(self.evac_eng(), dst, View([bk], bk.ap.rearrange("p (c t) -> p c t", c=4)))
        P.barrier()

    def gcol(self, name, l):
        return self.pcol[name] + l * 8

    def rmsnorm(self, tb, gc, hT_view, sq, rstd, out_dtype_bf16=True):
        P = self.P
        xv = self.xview(slice(0, 8), tb)
        xap, sqap = xv.ap, sq.ap
        P.op("act", lambda e: e.activation(out=sqap, in_=xap, func=AF.Square), ins=[xv], outs=[sq.v])
        bk = self.bank()
        for c in range(8):
            self.mm(bk.v, self.ones_b.v, sq[:, c, :], start=(c == 0), stop=(c == 7))
        rap, bap = rstd.ap, bk.ap
        P.op("act", lambda e: e.activation(out=rap, in_=bap, func=AF.Sqrt, bias=self.eps_ap(NORM_EPS), scale=1.0 / D),
             ins=[bk.v], outs=[rstd.v])
        P.op("dve", lambda e: e.reciprocal(out=rap, in_=rap), ins=[rstd.v], outs=[rstd.v])
        for c in range(8):
            xin = self.xb[c][tb]
            o = hT_view[:, c, :]
            oap, iap, gap = o.ap, xin.ap, self.PT.ap[:, gc + c: gc + c + 1]
            eng = "dve"
            P.op(eng, lambda e, oap=oap, iap=iap, gap=gap: e.scalar_tensor_tensor(
                out=oap, in0=iap, scalar=gap, in1=rap, op0=ALU.mult, op1=ALU.mult),
                ins=[xin.v, self.PT.v, rstd.v], outs=[o])

    def eps_ap(self, val):
        return self.eps_tiles[val].ap

    def make_eps(self):
        self.eps_tiles = {}
        for i, val in enumerate([NORM_EPS, LNX_EPS]):
            b = Buf(self.P.sbuf("eps%d" % i, [128, 1], F32)[:], "eps%d" % i)
            self.P.op("pool", lambda e, b=b, val=val: e.memset(b.ap, val), outs=[b.v])
            self.eps_tiles[val] = b

    def ffn(self, l, which):
        P = self.P
        w_in = self.dram["ffn%d_w_in" % which][l]
        w_out = self.dram["ffn%d_w_out" % which][l]
        gc = self.gcol("ffn%d_norm_g" % which, l)
        cv = self.carver()
        hT = [Buf(cv.b16(4096, "p (c t) -> p c t", c=8), "hT%d" % i) for i in range(2)]
        aT = [[Buf(cv.b16(512), "aT%d_%d" % (j, i)) for i in range(2)] for j in range(22)]
        sq = Buf(cv.b16(4096, "p (c t) -> p c t", c=8), "sq")
        rstd = Buf(cv.f32(512), "rstd")
        sg = [Buf(cv.f32(512), "sg%d" % i) for i in range(4)]
        for half in range(2):
            for i in range(2):
                self.rmsnorm(2 * half + i, gc, hT[i].v, sq, rstd)
            for j in range(22):
                wg = self.load_w(w_in[:, j * 128:(j + 1) * 128], 8, 128)
                wu = self.load_w(w_in[:, DFF + j * 128:DFF + (j + 1) * 128], 8, 128)
                for i in range(2):
                    pg = self.bank()
                    pu = self.bank()
                    for k in range(8):
                        self.mm(pg.v, wg[:, k, :], hT[i][:, k, :], start=(k == 0), stop=(k == 7))
                    for k in range(8):
                        self.mm(pu.v, wu[:, k, :], hT[i][:, k, :], start=(k == 0), stop=(k == 7))
                    s = sg[(j * 2 + i) % 4]
                    sap, pgap, puap, aap = s.ap, pg.ap, pu.ap, aT[j][i].ap
                    P.op("act", lambda e, sap=sap, pgap=pgap: e.activation(out=sap, in_=pgap, func=AF.Silu),
                         ins=[pg.v], outs=[s.v])
                    P.op("dve", lambda e, sap=sap, puap=puap, aap=aap: e.tensor_tensor(
                        out=aap, in0=sap, in1=puap, op=ALU.mult), ins=[s.v, pu.v], outs=[aT[j][i].v])
            for c in range(8):
                wo = self.load_w(w_out[:, c * 128:(c + 1) * 128], 22, 128)
                for i in range(2):
                    po = self.bank()
                    for k in range(22):
                        self.mm(po.v, wo[:, k, :], aT[k][i].v, start=(k == 0), stop=(k == 21))
                    xb = self.xb[c][2 * half + i]
                    xap, poap = xb.ap, po.ap
                    P.op("dve", lambda e, xap=xap, poap=poap: e.scalar_tensor_tensor(
                        out=xap, in0=poap, scalar=0.5, in1=xap, op0=ALU.mult, op1=ALU.add),
                        ins=[po.v, xb.v], outs=[xb.v])
        P.barrier()

    def mem_setup(self):
        P = self.P
        self.memT = Buf(P.sbuf("memT", [128, 8, MEM], F32)[:], "memT")
        cv = self.carver()
        stg = [Buf(cv.f32(1024), "mstg%d" % i) for i in range(2)]
        for mt in range(2):
            P.dma(stg[mt].v, raw(self.dram["mem"][mt * 128:(mt + 1) * 128, :]))
            for h in range(2):
                bk = self.bank()
                for c4 in range(4):
                    c = h * 4 + c4
                    self.transpose(bk[:, c4 * 128:(c4 + 1) * 128], stg[mt][:, c * 128:(c + 1) * 128], self.ident_f)
                self.copy(self.evac_eng(), self.memT[:, h * 4:h * 4 + 4, mt * 128:(mt + 1) * 128],
                          View([bk], bk.ap.rearrange("p (c t) -> p c t", c=4)))

    def xattn(self, l):
        P = self.P
        cv = self.carver()
        a16 = cv.b16
        hT = Buf(a16(4096, "p (c t) -> p c t", c=8), "hT")
        sq = Buf(a16(4096, "p (c t) -> p c t", c=8), "sq")
        qT = Buf(a16(4096, "p (c t) -> p c t", c=8), "qT")
        oT = Buf(a16(4096, "p (c t) -> p c t", c=8), "oT")
        mnT = Buf(a16(2048, "p (c t) -> p c t", c=8), "mnT")
        kT = Buf(a16(2048, "p (c t) -> p c t", c=8), "kT")
        vtm = Buf(a16(2048, "p (m n) -> p m n", m=2), "vtm")
        pT = [Buf(a16(512), "pT%d" % i) for i in range(4)]
        rstd = Buf(cv.f32(512), "rstd")
        rden = [Buf(cv.f32(512), "rden%d" % i) for i in range(2)]
        msq = Buf(a16(2048, "p (c t) -> p c t", c=8), "msq")
        mrs = Buf(cv.f32(256), "mrs")
        w_q = self.dram["xattn_w_q"][l]
        w_kv = self.dram["xattn_w_kv"][l]
        w_o = self.dram["xattn_w_o"][l]
        self.dump(self.memT[:, 0, :], 256, "memT0")
        self.dump(self.memT[:, 7, :], 256, "memT7")
        gm = self.gcol("mem_norm_g", l)
        mT = self.memT
        P.op("act", lambda e: e.activation(out=msq.ap, in_=mT.ap, func=AF.Square), ins=[mT.v], outs=[msq.v])
        bk = self.bank()
        for c in range(8):
            self.mm(bk[:, 0:MEM], self.ones_b.v, msq[:, c, :], start=(c == 0), stop=(c == 7))
        bkap = bk.ap[:, 0:MEM]
        P.op("act", lambda e, bkap=bkap: e.activation(out=mrs.ap, in_=bkap, func=AF.Sqrt, bias=self.eps_ap(NORM_EPS),
                                                      scale=1.0 / D), ins=[bk.v], outs=[mrs.v])
        P.op("dve", lambda e: e.reciprocal(out=mrs.ap, in_=mrs.ap), ins=[mrs.v], outs=[mrs.v])
        for c in range(8):
            oap, iap, gap = mnT.ap[:, c, :], mT.ap[:, c, :], self.PT.ap[:, gm + c:gm + c + 1]
            P.op("dve", lambda e, oap=oap, iap=iap, gap=gap: e.scalar_tensor_tensor(
                out=oap, in0=iap, scalar=gap, in1=mrs.ap, op0=ALU.mult, op1=ALU.mult),
                ins=[mT.v, self.PT.v, mrs.v], outs=[mnT.v])
        self.dump(mrs.v, 256, "mrs")
        self.dump(mnT[:, 0, :], 256, "mnT0")
        for c2 in range(4):
            wk = self.load_w(w_kv[:, c2 * 256:(c2 + 1) * 256], 8, 256)
            for cc in range(2):
                c = c2 * 2 + cc
                bk = self.bank()
                for k in range(8):
                    self.mm(bk[:, 0:MEM], wk[:, k, cc * 128:(cc + 1) * 128], mnT[:, k, :], start=(k == 0), stop=(k == 7))
                self.copy(self.evac_eng(), kT[:, c, :], bk[:, 0:MEM])
        for n4 in range(4):
            wv = self.load_w(w_kv[:, D + n4 * 256:D + (n4 + 1) * 256], 8, 256)
            for mt in range(2):
                bk = self.bank()
                for k in range(8):
                    self.mm(bk[:, 0:256], mnT[:, k, mt * 128:(mt + 1) * 128], wv[:, k, :], start=(k == 0), stop=(k == 7))
                self.copy(self.evac_eng(), vtm[:, mt, n4 * 256:(n4 + 1) * 256], bk[:, 0:256])
        self.dump(kT[:, 0, :], 256, "kT0")
        self.dump(vtm[:, 0, 0:512], 512, "vtm0")
        gx = self.gcol("xattn_norm_g", l)
        for tb in range(4):
            self.rmsnorm(tb, gx, hT.v, sq, rstd)
            for c2 in range(4):
                wq = self.load_w(w_q[:, c2 * 256:(c2 + 1) * 256], 8, 256)
                for cc in range(2):
                    c = c2 * 2 + cc
                    bk = self.bank()
                    for k in range(8):
                        self.mm(bk.v, wq[:, k, cc * 128:(cc + 1) * 128], hT[:, k, :], start=(k == 0), stop=(k == 7))
                    self.copy(self.evac_eng(), qT[:, c, :], bk.v)
            for h in range(4):
                pts = []
                for mt in range(2):
                    bk = self.bank()
                    for dc in range(2):
                        c = 2 * h + dc
                        self.mm(bk.v, kT[:, c, mt * 128:(mt + 1) * 128], qT[:, c, :], start=(dc == 0), stop=(dc == 1))
                    p = pT[(h * 2 + mt) % 4]
                    pap, bap = p.ap, bk.ap
                    P.op("act", lambda e, pap=pap, bap=bap: e.activation(out=pap, in_=bap, func=AF.Exp, scale=1.0 / 16.0),
                         ins=[bk.v], outs=[p.v])
                    pts.append(p)
                bden = self.bank()
                for mt in range(2):
                    self.mm(bden.v, self.ones_b.v, pts[mt].v, start=(mt == 0), stop=(mt == 1))
                rd = rden[h % 2]
                rdap, bdap = rd.ap, bden.ap
                P.op("dve", lambda e, rdap=rdap, bdap=bdap: e.reciprocal(out=rdap, in_=bdap), ins=[bden.v], outs=[rd.v])
                for dc in range(2):
                    c = 2 * h + dc
                    bo = self.bank()
                    for mt in range(2):
                        self.mm(bo.v, vtm[:, mt, c * 128:(c + 1) * 128], pts[mt].v, start=(mt == 0), stop=(mt == 1))
                    oap, boap = oT.ap[:, c, :], bo.ap
                    P.op("dve", lambda e, oap=oap, boap=boap, rdap=rdap: e.tensor_tensor(
                        out=oap, in0=boap, in1=rdap, op=ALU.mult), ins=[bo.v, rd.v], outs=[oT[:, c, :]])
            if tb == 0:
                self.dump(qT[:, 0, :], 512, "qT0")
                self.dump(pT[0].v, 512, "pT0")
                self.dump(rden[0].v, 512, "rden0")
                self.dump(oT[:, 0, :], 512, "oT0")
            for c2 in range(4):
                wo = self.load_w(w_o[:, c2 * 256:(c2 + 1) * 256], 8, 256)
                for cc in range(2):
                    c = c2 * 2 + cc
                    bk = self.bank()
                    for k in range(8):
                        self.mm(bk.v, wo[:, k, cc * 128:(cc + 1) * 128], oT[:, k, :], start=(k == 0), stop=(k == 7))
                    xb = self.xb[c][tb]
                    xap, bap = xb.ap, bk.ap
                    P.op("dve", lambda e, xap=xap, bap=bap: e.tensor_tensor(out=xap, in0=bap, in1=xap, op=ALU.add),
                         ins=[bk.v, xb.v], outs=[xb.v])
        P.barrier()

    def moba_setup(self):
        P = self.P
        cv = self.carver()
        rbs = Buf(cv.f32(32)[0:8, :], "rbs")
        erbT = Buf(cv.f32(8)[0:32, :], "erbT")
        oh = Buf(cv.f32(2048)[0:32, :], "oh")
        ebrow = Buf(cv.b16(2048)[0:8, :], "ebrow")
        self.EBREP = Buf(self.ebrep_t.ap(), "EBREP")
        P.dma(rbs.v, raw(self.dram["rel_bias"]))
        P.dma(oh.v, raw(self.dram["oh_tab"]))
        self.act(rbs.v, rbs.v, AF.Exp)
        bk = self.bank()
        self.transpose(bk[0:32, 0:8], rbs.v, self.ident_f[0:8, 0:8])
        self.copy("dve", erbT.v, bk[0:32, 0:8])
        for c4 in range(4):
            bk = self.bank()
            self.mm(bk[0:8, :], erbT.v, oh[:, c4 * 512:(c4 + 1) * 512], start=True, stop=True)
            self.copy("dve", ebrow[:, c4 * 512:(c4 + 1) * 512], bk[0:8, :])
        srcb = View([ebrow], ebrow.ap.rearrange("h (o c) -> h o c", o=1).to_broadcast([8, 128, 2048]))
        P.dma(self.EBREP.v, srcb)
        self.RB31 = Buf(P.sbuf("RB31", [128, 8], F32)[:], "RB31")
        P.dma(self.RB31.v, raw(self.dram["rel_bias"][:, 31:32].rearrange("h o -> o h").partition_broadcast(128)),
              allow_slow_non_contiguous=True)
        self.IND128 = Buf(P.sbuf("IND", [128, 2048], BF16)[:], "IND128")
        self.IND = View([self.IND128], self.IND128.ap[0:8, :])
        ind = self.IND
        self.memset("pool", ind.v, 1.0)
        P.op("pool", lambda e: e.affine_select(out=ind.ap, in_=ind.ap, compare_op=ALU.is_ge, fill=0.0, base=0,
                                               pattern=[[1, 2048]], channel_multiplier=-256), ins=[ind], outs=[ind])
        P.op("pool", lambda e: e.affine_select(out=ind.ap, in_=ind.ap, compare_op=ALU.is_ge, fill=0.0, base=255,
                                               pattern=[[-1, 2048]], channel_multiplier=256), ins=[ind], outs=[ind])
        P.dma(self.IND128[64:72, :], self.IND128[0:8, :])
        self.ELIG = Buf(P.sbuf("ELIG", [128, 8, 8], F32)[:], "ELIG")
        self.OWN = Buf(P.sbuf("OWN", [128, 8, 8], F32)[:], "OWN")
        self.memset("pool", self.ELIG.v, 0.0)
        self.memset("pool", self.OWN.v, 0.0)
        for qb in range(8):
            self.memset("pool", self.ELIG[:, qb, qb:8], -1e30)
            self.memset("pool", self.OWN[:, qb, qb:qb + 1], 1.0)

    def moba(self, l):
        P = self.P
        cv = self.carver()
        yrw = Buf(cv.b16(8192, "p (c t) -> p c t", c=4), "yrw")
        KT = Buf(cv.b16(8192, "p (c t) -> p c t", c=4), "KT")
        VTM = Buf(cv.b16(8192, "p (k n) -> p k n", k=16), "VTM")
        hT = Buf(cv.b16(4096, "p (c t) -> p c t", c=8), "hT")
        sq = Buf(cv.b16(4096, "p (c t) -> p c t", c=8), "sq")
        rstd = Buf(cv.f32(512), "rstd")
        qs = Buf(cv.b16(512), "qs")
        qf = Buf(cv.f32(512), "qf")
        ymo = Buf(cv.b16(2048, "p (c t) -> p c t", c=4), "ymo")
        bands = [Buf(cv.b16(1920), "band%d" % i) for i in range(2)]
        eT = [Buf(cv.f32(512), "eT%d" % i) for i in range(2)]
        pT = [Buf(cv.b16(512), "pT%d" % i) for i in range(3)]
        rden = [Buf(cv.f32(512), "rden%d" % i) for i in range(2)]
        mbT = Buf(cv.b16(1024, "p (a t) -> p a t", a=2), "mbT")
        selp = Buf(cv.f32(4 * 72, "p (a n) -> p a n", a=4), "selp")
        kmT = Buf(cv.f32(64, "p (c n) -> p c n", c=4), "kmT")
        gm = Buf(cv.f32(64, "p (a n) -> p a n", a=8), "gm")
        top8 = Buf(cv.f32(64, "p (a n) -> p a n", a=8), "top8")
        sel = Buf(cv.f32(64, "p (a n) -> p a n", a=8), "sel")
        w_in = self.dram["w_mix_in"][l]
        w_out = self.dram["w_mix_out"][l]
        gc = self.gcol("mix_norm_g", l)
        self.memset("pool", kmT.v, 0.0)
        self.memset("pool", selp.v, 0.0)
        if (l, "rwkv") not in self.stages:
            self.memset("pool", yrw.v, 0.0)
        import os
        if int(os.environ.get("MOBA_LV", "9")) < 3:
            self.memset("pool", ymo.v, 0.0)
        self.bank_set = [0, 1, 2, 3]
        acc_i = 0
        band_i = 0
        e_i = 0
        p_i = 0
        QOFF = RPROJ
        for tb in range(4):
            self.rmsnorm(tb, gc, hT.v, sq, rstd)
            for m in range(4):
                wq = self.load_w(w_in[:, QOFF + m * 128:QOFF + (m + 1) * 128], 8, 128)
                wk = self.load_w(w_in[:, QOFF + 512 + m * 128:QOFF + 512 + (m + 1) * 128], 8, 128)
                wv = self.load_w(w_in[:, QOFF + 1024 + m * 128:QOFF + 1024 + (m + 1) * 128], 8, 128)
                import os
                SK = os.environ.get("MOBA_SKIP", "").split(",")
                bq = self.bank()
                for k in range(8):
                    self.mm(bq.v, wq[:, k, :], hT[:, k, :], start=(k == 0), stop=(k == 7))
                if "qf" not in SK:
                    self.copy("act", qf.v, bq.v)
                if "qs" not in SK:
                    self.ts("dve", qs.v, bq.v, 0.125, ALU.mult)
                bkk = self.bank()
                for k in range(8):
                    self.mm(bkk.v, wk[:, k, :], hT[:, k, :], start=(k == 0), stop=(k == 7))
                self.copy("act", KT[:, m, tb * 512:(tb + 1) * 512], bkk.v)
                for par in range(2):
                    pr = slice(par * 64, (par + 1) * 64)
                    kin = View([bkk], bkk.ap[pr, :].rearrange("p (a t) -> p a t", a=2))
                    kout = kmT[pr, m, par * 8 + 2 * tb:par * 8 + 2 * tb + 2]
                    P.op("dve", lambda e, o=kout.ap, i=kin.ap: e.tensor_reduce(out=o, in_=i, axis=AX.X, op=ALU.add),
                         ins=[kin], outs=[kout])
                bv = self.bank()
                if "v" not in SK:
                    for tt in range(4):
                        for k in range(8):
                            self.mm(bv[:, tt * 128:(tt + 1) * 128], hT[:, k, tt * 128:(tt + 1) * 128], wv[:, k, :],
                                    start=(k == 0), stop=(k == 7))
                    self.copy("dve", VTM[:, tb * 4:tb * 4 + 4, m * 128:(m + 1) * 128],
                              View([bv], bv.ap.rearrange("p (a n) -> p a n", a=4)))
                import os
                LV = int(os.environ.get("MOBA_LV", "9"))
                if LV < 2:
                    continue
                bg = self.bank()
                for tt in range(4):
                    self.mm(bg[:, tt * 16:(tt + 1) * 16], qf[:, tt * 128:(tt + 1) * 128], kmT[:, m, :],
                            start=True, stop=True)
                for qq in range(2):
                    qb = 2 * tb + qq
                    el = View([self.ELIG], self.ELIG.ap[:, qb:qb + 1, :].to_broadcast([128, 4, 8]))
                    self.tt("dve", gm[:, qq * 4:(qq + 1) * 4, :],
                            View([bg], bg.ap[:, qq * 32:(qq + 1) * 32].rearrange("p (a n) -> p a n", a=4)), el, ALU.add)
                GLV = int(os.environ.get("GATE_LV", "9"))
                if GLV < 2:
                    continue
                for a in range(8):
                    P.op("dve", lambda e, o=top8.ap[:, a, :], i=gm.ap[:, a, :]: e.max(out=o, in_=i),
                         ins=[gm.v], outs=[top8.v])
                thr = View([top8], top8.ap[:, :, 2:3].to_broadcast([128, 8, 8]))
                self.tt("dve", sel.v, gm.v, thr, ALU.is_ge)
                for qq in range(2):
                    qb = 2 * tb + qq
                    ow = View([self.OWN], self.OWN.ap[:, qb:qb + 1, :].to_broadcast([128, 4, 8]))
                    self.tt("dve", sel[:, qq * 4:(qq + 1) * 4, :], sel[:, qq * 4:(qq + 1) * 4, :], ow, ALU.max)
                self.ts("dve", sel.v, sel.v, 30000.0, ALU.mult, -30000.0, ALU.add)
                if GLV < 3:
                    continue
                bt = self.bank()
                for tt in range(4):
                    self.transpose(bt[0:8, tt * 128:(tt + 1) * 128], sel[:, tt * 2, :], self.ident_f)
                self.copy("act", mbT[0:8, 0, :], bt[0:8, :])
                self.copy("dve", selp[:, :, 64:72], View([sel], sel.ap.rearrange("p (t two) n -> p t two n", two=2)[:, :, 1, :]))
                bt = self.bank()
                for tt in range(4):
                    self.transpose(bt[0:72, tt * 128:(tt + 1) * 128], selp[:, tt, :], self.ident_f)
                self.copy("act", mbT[64:72, 1, :], bt[64:72, :])
                if LV < 3:
                    continue
                nkt = 4 * (tb + 1)
                for par in range(2):
                    h = 2 * m + par
                    pr = slice(par * 64, (par + 1) * 64)
                    band = bands[band_i % 2]
                    band_i += 1
                    src = bass.AP(self.ebrep_t, h * 128 * 2048 + 128, [[2047, 128], [1, 1920]])
                    P.dma(band.v, View([self.EBREP], src))
                    bo = self.banks[4 + 2 * (acc_i % 2)]
                    bd = self.banks[5 + 2 * (acc_i % 2)]
                    acc_i += 1
                    for kt in range(nkt):
                        bs = self.bank()
                        self.mm(bs.v, KT[pr, m, kt * 128:(kt + 1) * 128], qs[pr, :], start=True, stop=False)
                        mr = slice(par * 64, par * 64 + 8)
                        self.mm(bs.v, self.IND128[mr, kt * 128:(kt + 1) * 128], mbT[mr, par, :], start=False, stop=True)
                        delta = tb * 512 - kt * 128
                        p = pT[p_i % 3]
                        p_i += 1
                        if delta >= 1024:
                            self.act(p.v, bs.v, AF.Exp, bias=self.RB31[:, h:h + 1])
                        else:
                            et = eT[e_i % 2]
                            e_i += 1
                            self.act(et.v, bs.v, AF.Exp)
                            self.tt("dve", p.v, et.v, band[:, delta + 384:delta + 384 + 512], ALU.mult)
                        self.mm(bo.v, VTM[:, kt, m * 128:(m + 1) * 128], p.v, start=(kt == 0), stop=(kt == nkt - 1))
                        self.mm(bd.v, self.ones_b.v, p.v, start=(kt == 0), stop=(kt == nkt - 1))
                    rd = rden[par]
                    self.recip(rd[pr, :], bd[pr, :])
                    self.tt("dve", ymo[pr, m, :], bo[pr, :], rd[pr, :], ALU.mult)
            for c2 in range(4):
                wo = self.load_w(w_out[:, c2 * 256:(c2 + 1) * 256], 8, 256)
                for cc in range(2):
                    c = c2 * 2 + cc
                    bk = self.bank()
                    for k in range(8):
                        rhs = yrw[:, k, tb * 512:(tb + 1) * 512] if k < 4 else ymo[:, k - 4, :]
                        self.mm(bk.v, wo[:, k, cc * 128:(cc + 1) * 128], rhs, start=(k == 0), stop=(k == 7))
                    xb = self.xb[c][tb]
                    self.tt("dve", xb.v, bk.v, xb.v, ALU.add)
        self.bank_set = list(range(8))
        P.barrier()

    def rwkv_setup(self):
        P = self.P
        self.BONES = Buf(P.sbuf("BONES", [128, 128], BF16)[:], "BONES")
        self.memset("pool", self.BONES.v, 0.0)
        self.memset("pool", self.BONES[0:64, 0:64], 1.0)
        self.memset("pool", self.BONES[64:128, 64:128], 1.0)
        self.MUs = Buf(P.sbuf("MUs", [64, 64], F32)[:], "MUs")
        self.MUi = Buf(P.sbuf("MUi", [64, 64], F32)[:], "MUi")
        self.MLs = Buf(P.sbuf("MLs", [64, 64], F32)[:], "MLs")
        for mk, op, pat, cm in [(self.MUs, ALU.is_gt, [[1, 64]], -1), (self.MUi, ALU.is_ge, [[1, 64]], -1),
                                (self.MLs, ALU.is_gt, [[-1, 64]], 1)]:
            self.memset("pool", mk.v, 1.0)
            P.op("pool", lambda e, mk=mk, op=op, pat=pat, cm=cm: e.affine_select(
                out=mk.ap, in_=mk.ap, compare_op=op, fill=0.0, base=0, pattern=pat, channel_multiplier=cm),
                ins=[mk], outs=[mk])
        self.RMASK = Buf(P.sbuf("RMASK", [128, 512], F32)[:], "RMASK")
        self.memset("pool", self.RMASK.v, 1.0)
        self.memset("pool", View([self.RMASK], self.RMASK.ap.rearrange("p (c t) -> p c t", t=64)[:, :, 0:1]), 0.0)

    def rwkv(self, l):
        P = self.P
        CD = 0.6065306597126334
        cv = self.carver()
        yrw = Buf(cv.b16(8192, "p (c t) -> p c t", c=4), "yrw")
        hT = Buf(cv.b16(4096, "p (c t) -> p c t", c=8), "hT")
        ra_o = cv.o
        RA = Buf(cv.f32(2048), "RA")
        sq = View([RA], RA.ap.bitcast(BF16).rearrange("p (c t) -> p c t", c=8))
        rstd = Buf(cv.f32(512), "rstd")
        waup = Buf(cv.b16(512), "waup")
        gup = Buf(cv.b16(512), "gup")
        lo12 = Buf(cv.b16(512), "lo12")
        sgl = Buf(cv.b16(512), "sgl")
        carry = Buf(cv.f32(16), "carry")
        Hf = Buf(cv.f32(512, "p (m i) -> p m i", m=4), "Hf")
        Hb = Buf(cv.b16(512, "p (m i) -> p m i", m=4), "Hb")
        praw = [Buf(cv.f32(516), "praw%d" % i) for i in range(2)]
        f32t = {}
        f32o = {}
        for nm in ["rf", "kf", "lerp", "sig", "asig", "Lr", "eL", "eLm", "eX", "t1", "kkn", "kp", "bb", "bonus"]:
            f32o[nm] = cv.o
            f32t[nm] = Buf(cv.f32(512), nm)
        f32t["ys"] = f32t["lerp"]
        b16t = {}
        for nm in ["rt", "kt", "bt", "at", "vT", "kh", "bh", "gt", "tmpb"]:
            b16t[nm] = Buf(cv.b16(512), nm)
        khT = Buf(cv.b16(1024, "p (c j) -> p c j", c=8)[0:64], "khT")
        bhT = Buf(cv.b16(1024, "p (c j) -> p c j", c=8)[0:64], "bhT")
        vTM = Buf(cv.b16(1024, "p (c j) -> p c j", c=8)[0:64], "vTM")
        mats = {}
        for nm in ["TTb", "AkT", "ArbT", "ArkT", "ZC"]:
            mats[nm] = Buf(cv.b16(1024, "p (a t) -> p a t", a=16)[0:64], nm)
        inv = {}
        for nm in ["M", "N"]:
            inv[nm] = Buf(cv.f32(1024, "p (a t) -> p a t", a=16)[0:64], nm)

        def reg(o):
            return self.big[0:64, o:o + 1024].rearrange("p (a t) -> p a t", a=16)
        inv["M2"] = View([RA], reg(ra_o))
        inv["N2"] = View([RA], reg(ra_o + 1024))
        inv["P2"] = View([f32t["rf"], f32t["kf"]], reg(f32o["rf"]))
        inv["Pm"] = View([f32t["sig"], f32t["asig"]], reg(f32o["sig"]))
        Zs = Buf(cv.b16(128)[0:64], "Zs")
        Us = Buf(cv.b16(128)[0:64], "Us")
        w_in = self.dram["w_mix_in"][l]
        gc = self.gcol("mix_norm_g", l)
        pc = self.pcol
        PT = self.PT

        def pcolv(name, idx, n_per_layer):
            c = pc[name] + l * n_per_layer + idx
            return PT[:, c:c + 1]
        P.dma(waup[0:64, :], raw(self.dram["rwkv_w_up"][l]), queue="pool")
        P.dma(waup[64:128, :], raw(self.dram["rwkv_a_up"][l]), queue="pool")
        P.dma(gup.v, raw(self.dram["rwkv_g_up"][l]), queue="pool")
        import os
        RLV = int(os.environ.get("RWKV_LV", "9"))
        if RLV < 9:
            self.memset("pool", yrw.v, 0.0)
        self.memset("pool", carry.v, 0.0)
        self.memset("pool", Hf.v, 0.0)
        self.memset("pool", Hb.v, 0.0)
        self.bank_set = [0, 1, 2, 3, 4]
        BY = self.banks[5]
        pri = 0

        def project_lerp(j, out_view, tb):
            nonlocal pri
            w = self.load_w(w_in[:, j * 128:(j + 1) * 128], 8, 128)
            bk = self.bank()
            for k in range(8):
                self.mm(bk.v, w[:, k, :], hT[:, k, :], start=(k == 0), stop=(k == 7))
            pr_ = praw[pri % 2]
            pri += 1
            self.copy("dve", pr_[:, 0:1], carry[:, j:j + 1])
            self.copy("act", pr_[:, 1:513], bk.v)
            self.copy("dve", carry[:, j:j + 1], pr_[:, 512:513])
            d = f32t["t1"]
            self.tt("dve", d.v, pr_[:, 0:512], pr_[:, 1:513], ALU.subtract)
            self.stt(out_view, d.v, pcolv("rwkv_mu", j, 14), pr_[:, 1:513], ALU.mult, ALU.add)

        for tb in range(4):
            self.rmsnorm(tb, gc, hT.v, sq, rstd)
            lerp = f32t["lerp"]
            project_lerp(12, lerp.v, tb)
            self.act(lo12[0:64, :], lerp[0:64, :], AF.Tanh)
            self.copy("dve", lo12[64:128, :], lerp[64:128, :])
            project_lerp(13, lerp.v, tb)
            self.act(sgl.v, lerp.v, AF.Sigmoid)
            for m in range(4):
                rf, kf, sig, asig, Lr, eL, eLm, eX = (f32t[n] for n in ["rf", "kf", "sig", "asig", "Lr", "eL", "eLm", "eX"])
                t1, kkn, kp, bb, ys, bonus = (f32t[n] for n in ["t1", "kkn", "kp", "bb", "ys", "bonus"])
                rt, kt, bt, at, vT, kh, bh, gt, tmpb = (b16t[n] for n in ["rt", "kt", "bt", "at", "vT", "kh", "bh", "gt", "tmpb"])
                project_lerp(m, rf.v, tb)
                project_lerp(4 + m, kf.v, tb)
                project_lerp(8 + m, lerp.v, tb)
                self.copy("act", vT.v, lerp.v)
                bw = self.bank()
                self.mm(bw.v, waup[0:64, m * 128:(m + 1) * 128], lo12[0:64, :], start=True, stop=True)
                self.act(sig.v, bw.v, AF.Sigmoid, bias=pcolv("rwkv_w0", m, 4))
                ba = self.bank()
                self.mm(ba.v, waup[64:128, m * 128:(m + 1) * 128], lo12[64:128, :], start=True, stop=True)
                self.act(asig.v, ba.v, AF.Sigmoid, bias=pcolv("rwkv_a0", m, 4))
                bgt = self.bank()
                self.mm(bgt.v, gup[:, m * 128:(m + 1) * 128], sgl.v, start=True, stop=True)
                self.copy("act", gt.v, bgt.v)
                P.op("dve", lambda e, o=Lr.ap, d0=self.RMASK.ap, d1=sig.ap: e.tensor_tensor_scan(
                    out=o, data0=d0, data1=d1, initial=0.0, op0=ALU.mult, op1=ALU.add),
                    ins=[self.RMASK, sig], outs=[Lr])
                self.act(eL.v, Lr.v, AF.Exp, scale=-CD)
                self.act(eLm.v, Lr.v, AF.Exp, scale=CD)
                self.ts("dve", t1.v, kf.v, pcolv("rwkv_k_k", m, 4), ALU.mult)
                self.act(tmpb.v, t1.v, AF.Square)
                bss = self.bank()
                self.mm(bss.v, self.BONES.v, tmpb.v, start=True, stop=True)
                self.act(kkn.v, bss.v, AF.Sqrt)
                self.ts("dve", kkn.v, kkn.v, 1e-12, ALU.max)
                self.recip(kkn.v, kkn.v)
                self.tt("dve", kkn.v, t1.v, kkn.v, ALU.mult)
                self.ts("dve", t1.v, asig.v, -1.0, ALU.add, pcolv("rwkv_k_a", m, 4), ALU.mult)
                self.stt(kp.v, t1.v, 1.0, kf.v, ALU.add, ALU.mult)
                self.tt("dve", bb.v, kkn.v, asig.v, ALU.mult)
                self.stt(tmpb.v, rf.v, pcolv("rwkv_r_k", m, 4), kp.v, ALU.mult, ALU.mult)
                bbn = self.bank()
                self.mm(bbn.v, self.BONES.v, tmpb.v, start=True, stop=True)
                self.tt("dve", bonus.v, bbn.v, vT.v, ALU.mult)
                self.tt("dve", rt.v, rf.v, eL.v, ALU.mult)
                self.tt("dve", kt.v, kp.v, eLm.v, ALU.mult)
                self.tt("dve", bt.v, bb.v, eLm.v, ALU.mult)
                self.tt("dve", t1.v, Lr.v, sig.v, ALU.subtract)
                self.act(eX.v, t1.v, AF.Exp, scale=-CD)
                self.stt(at.v, kkn.v, -1.0, eX.v, ALU.mult, ALU.mult)
                lrc = View([Lr], Lr.ap.rearrange("p (c t) -> p c t", t=64)[:, :, 63:64].to_broadcast([128, 8, 64]))
                self.tt("dve", View([t1], t1.ap.rearrange("p (c t) -> p c t", t=64)),
                        View([Lr], Lr.ap.rearrange("p (c t) -> p c t", t=64)), lrc, ALU.subtract)
                self.act(eX.v, t1.v, AF.Exp, scale=CD)
                self.tt("dve", kh.v, kp.v, eX.v, ALU.mult)
                self.tt("dve", bh.v, bb.v, eX.v, ALU.mult)
                if RLV < 2:
                    continue
                for srcb, dstb in [(kh, khT), (bh, bhT), (vT, vTM)]:
                    bk = self.bank()
                    bkb = View([bk], bk.ap.bitcast(BF16))
                    for c in range(8):
                        self.transpose(bkb[0:64, c * 128:(c + 1) * 128], srcb[:, c * 64:(c + 1) * 64], self.ident_b)
                    self.copy("act", dstb.v, View([bk], bk.ap.bitcast(BF16)[0:64, :].rearrange("p (c j) -> p c j", c=8)))
                def chunk_mats(lhs, rhs, dst, mask, eng):
                    for par in range(2):
                        pr = slice(par * 64, (par + 1) * 64)
                        bk = self.bank()
                        for c in range(8):
                            cs = slice(c * 64, (c + 1) * 64)
                            self.mm(bk[0:64, cs], lhs[pr, cs], rhs[pr, cs], start=True, stop=True)
                        mk = View([mask], mask.ap.rearrange("p (o t) -> p o t", o=1).to_broadcast([64, 8, 64]))
                        self.tt(eng, dst[:, par * 8:(par + 1) * 8, :],
                                View([bk], bk.ap[0:64, :].rearrange("p (c t) -> p c t", c=8)), mk, ALU.mult)
                M, N, Pm, M2, N2, P2 = (inv[n] for n in ["M", "N", "Pm", "M2", "N2", "P2"])
                if RLV < 3:
                    continue
                chunk_mats(bt, at, M, self.MUs, "dve")
                chunk_mats(at, bt, N, self.MLs, "dve")
                chunk_mats(kt, at, mats["AkT"], self.MUs, "dve")
                chunk_mats(bt, rt, mats["ArbT"], self.MUi, "dve")
                chunk_mats(kt, rt, mats["ArkT"], self.MUi, "dve")
                if RLV < 4:
                    continue
                i64 = View([self.ident_f], self.ident_f.ap[0:64, 0:64].rearrange("p (o t) -> p o t", o=1).to_broadcast([64, 16, 64]))
                self.tt("dve", Pm.v, M.v, i64, ALU.add)
                for lvl in range(1, 6):
                    if lvl < 5:
                        for hh in range(2):
                            bk = self.bank()
                            for a8 in range(8):
                                a = hh * 8 + a8
                                self.mm(bk[0:64, a8 * 64:(a8 + 1) * 64], N[:, a, :], M[:, a, :], start=True, stop=True)
                            self.copy("act", M2[:, hh * 8:(hh + 1) * 8, :],
                                      View([bk], bk.ap[0:64, :].rearrange("p (c t) -> p c t", c=8)))
                    for hh in range(2):
                        bk = self.bank()
                        for a8 in range(8):
                            a = hh * 8 + a8
                            self.mm(bk[0:64, a8 * 64:(a8 + 1) * 64], M[:, a, :], N[:, a, :], start=True, stop=True)
                        self.copy("act", N2[:, hh * 8:(hh + 1) * 8, :],
                                  View([bk], bk.ap[0:64, :].rearrange("p (c t) -> p c t", c=8)))
                    for hh in range(2):
                        bk = self.bank()
                        for a8 in range(8):
                            a = hh * 8 + a8
                            self.mm(bk[0:64, a8 * 64:(a8 + 1) * 64], N2[:, a, :], Pm[:, a, :], start=True, stop=True)
                        self.tt("dve", P2[:, hh * 8:(hh + 1) * 8, :],
                                View([bk], bk.ap[0:64, :].rearrange("p (c t) -> p c t", c=8)),
                                Pm[:, hh * 8:(hh + 1) * 8, :], ALU.add)
                    M, M2 = M2, M
                    N, N2 = N2, N
                    Pm, P2 = P2, Pm
                TTb = mats["TTb"]
                self.copy("act", TTb.v, Pm.v)
                AkT, ArbT, ArkT = mats["AkT"], mats["ArbT"], mats["ArkT"]
                if RLV < 5:
                    continue
                ZC = mats["ZC"]
                for par in range(2):
                    pr = slice(par * 64, (par + 1) * 64)
                    bk = self.bank()
                    for c in range(8):
                        self.mm(bk[0:64, c * 64:(c + 1) * 64], AkT[:, par * 8 + c, :], vTM[:, c, pr], start=True, stop=True)
                    self.copy("act", ZC[:, par * 8:(par + 1) * 8, :],
                              View([bk], bk.ap[0:64, :].rearrange("p (c t) -> p c t", c=8)))
                BYH = [self.banks[6], self.banks[7]]
                for c in range(8):
                    cs = slice(c * 64, (c + 1) * 64)
                    zb = [self.banks[(c % 2) * 2], self.banks[(c % 2) * 2 + 1]]
                    for par in range(2):
                        pr = slice(par * 64, (par + 1) * 64)
                        self.mm(zb[par][0:64, 0:64], at[pr, cs], Hb[pr, m, pr], start=True, stop=True)
                    zin = View(zb, self.ps_t[0:64, (c % 2) * 2:(c % 2) * 2 + 2, 0:64])
                    zc = View([ZC], ZC.ap.rearrange("p (h c) t -> p h c t", h=2)[:, :, c, :])
                    self.tt("dve", View([Zs], Zs.ap.rearrange("p (h t) -> p h t", h=2)), zin, zc, ALU.add)
                    bu = self.banks[4]
                    for par in range(2):
                        pr = slice(par * 64, (par + 1) * 64)
                        a = par * 8 + c
                        self.mm(bu[0:64, pr], TTb[:, a, :], Zs[:, pr], start=True, stop=True)
                    self.copy("act", Us.v, bu[0:64, 0:128])
                    for par in range(2):
                        pr = slice(par * 64, (par + 1) * 64)
                        a = par * 8 + c
                        self.mm(BYH[par][pr, cs], Hb[pr, m, pr], rt[pr, cs], start=True, stop=True)
                        self.mm(BY[pr, cs], Us[:, pr], ArbT[:, a, :], start=True, stop=False)
                        self.mm(BY[pr, cs], vTM[:, c, pr], ArkT[:, a, :], start=False, stop=True)
                    bhh = self.banks[4]
                    for par in range(2):
                        pr = slice(par * 64, (par + 1) * 64)
                        self.mm(bhh[pr, pr], khT[:, c, pr], vTM[:, c, pr], start=True, stop=False)
                        self.mm(bhh[pr, pr], bhT[:, c, pr], Us[:, pr], start=False, stop=True)
                    for par in range(2):
                        pr = slice(par * 64, (par + 1) * 64)
                        wc = eL[pr, c * 64 + 63:c * 64 + 64]
                        self.stt(Hf[pr, m, pr], Hf[pr, m, pr], wc, bhh[pr, pr], ALU.mult, ALU.add)
                        self.copy("act", Hb[pr, m, pr], Hf[pr, m, pr])
                if RLV < 6:
                    continue
                self.copy("act", ys.v, BY.v)
                for par in range(2):
                    pr = slice(par * 64, (par + 1) * 64)
                    self.tt("dve", ys[pr, :], ys[pr, :], BYH[par][pr, :], ALU.add)
                self.copy("dve", tmpb.v, ys.v)
                bm = self.bank()
                self.mm(bm.v, self.BONES.v, tmpb.v, start=True, stop=True)
                self.stt(ys.v, bm.v, -1.0 / 64.0, ys.v, ALU.mult, ALU.add)
                self.act(tmpb.v, ys.v, AF.Square)
                bvv = self.bank()
                self.mm(bvv.v, self.BONES.v, tmpb.v, start=True, stop=True)
                self.act(t1.v, bvv.v, AF.Sqrt, bias=self.eps_tiles[LNX_EPS], scale=1.0 / 64.0)
                self.recip(t1.v, t1.v)
                self.tt("dve", ys.v, ys.v, t1.v, ALU.mult)
                self.ts("dve", ys.v, ys.v, pcolv("rwkv_ln_g", m, 4), ALU.mult, pcolv("rwkv_ln_b", m, 4), ALU.add)
                self.tt("dve", ys.v, ys.v, bonus.v, ALU.add)
                self.tt("dve", yrw[:, m, tb * 512:(tb + 1) * 512], ys.v, gt.v, ALU.mult)
        self.bank_set = list(range(8))
        P.barrier()

    def finish(self):
        P = self.P
        cv = self.carver()
        sq = Buf(cv.b16(4096, "p (c t) -> p c t", c=8), "sq")
        rstd = Buf(cv.f32(512), "rstd")
        yT = Buf(cv.f32(4096, "p (c t) -> p c t", c=8), "yT")
        ostg = [Buf(cv.f32(1024), "ostg%d" % i) for i in range(2)]
        self.OUT = Buf(self.out_ap, "OUT")
        gc = self.pcol["final_norm_g"]
        n = 0
        for tb in range(4):
            if self.final_norm:
                self.rmsnorm(tb, gc, yT.v, sq, rstd)
                src = lambda c, t0: yT[:, c, t0:t0 + 128]
            else:
                src = lambda c, t0, tb=tb: View([self.xb[c][tb]], self.xb[c][tb].ap[:, t0:t0 + 128])
            for t4 in range(4):
                st = ostg[n % 2]
                n += 1
                for h in range(2):
                    bk = self.bank()
                    for c4 in range(4):
                        c = h * 4 + c4
                        self.transpose(bk[:, c4 * 128:(c4 + 1) * 128], src(c, t4 * 128), self.ident_f)
                    self.copy(self.evac_eng(), st[:, h * 512:(h + 1) * 512], bk.v)
                r0 = tb * 512 + t4 * 128
                P.dma(self.OUT[r0:r0 + 128, :], st.v)

    def build(self):
        self.alloc()
        self.make_eps()
        self.setup()
        self.mem_setup()
        self.P.barrier()
        self.moba_setup()
        self.rwkv_setup()
        self.P.barrier()
        for l in range(DEPTH):
            for stg in ["ffn1", "rwkv", "moba", "xattn", "ffn2"]:
                if (l, stg) not in self.stages:
                    continue
                if stg == "ffn1":
                    self.ffn(l, 1)
                elif stg == "ffn2":
                    self.ffn(l, 2)
                elif stg == "xattn":
                    self.xattn(l)
                elif stg == "moba":
                    self.moba(l)
                elif stg == "rwkv":
                    self.rwkv(l)
        self.finish()
        st = self.P.emit()
        return self.nc, st


BUCKET_STARTS = [0, 1, 2, 3, 4, 5, 6, 7, 8, 9, 10, 11, 12, 13, 14, 15, 16, 21, 27, 35, 46, 59, 77, 99, 128, 166,
                 216, 280, 363, 470, 609, 790]


def make_oh():
    oh = np.zeros((32, 2048), np.float32)
    for c in range(512, 2048):
        d = c - 512
        b = 0
        for i, st in enumerate(BUCKET_STARTS):
            if d >= st:
                b = i
        oh[b, c] = 1.0
    return oh


ALL_STAGES = [(l, s) for l in range(DEPTH) for s in ["ffn1", "rwkv", "moba", "xattn", "ffn2"]]


def run(inputs, stages, final_norm=True, cores=8, trace=False, debug=False):
    k = K(stages, final_norm, debug)
    nc, st = k.build()
    in_maps = []
    oh = make_oh()
    for b in range(cores):
        m = {}
        for name, shape in PARAM_SPECS:
            a = np.asarray(inputs[name], dtype=np.float32)
            if name in ("x", "mem"):
                a = a[b]
            m[name] = np.ascontiguousarray(a)
        m["oh_tab"] = oh
        in_maps.append(m)
    res = run_bass_kernel_spmd(nc, in_maps, core_ids=list(range(cores)), trace=trace)
    out = np.stack([r["out"] for r in res.results], axis=0)
    if debug:
        return out, res, st, res.results[0]["dbg"]
    return out, res, st


def kernel(**inputs):
    out, _, _ = run(inputs, ALL_STAGES, True, 8)
    return out.astype(np.float32)
```

```python
import numpy as np
from contextlib import ExitStack
import concourse.bass as bass
import concourse.mybir as mybir
from concourse.bass_utils import run_bass_kernel_spmd

F32 = mybir.dt.float32
BF16 = mybir.dt.bfloat16
F32R = mybir.dt.float32r
AF = mybir.ActivationFunctionType
ALU = mybir.AluOpType
AX = mybir.AxisListType

D = 1024
S = 2048
DEPTH = 2
DFF = 2816
RW = 512
RPROJ = 1792
DPROJ = 3328
MEM = 256
NORM_EPS = 1e-6
LNX_EPS = 64e-5


class Buf:
    def __init__(self, ap, name="", psum=False):
        self.ap = ap
        self.name = name
        self.psum = psum
        self.w = None
        self.r = {}
        self.dsem = None
        self.dcnt = 0

    def __getitem__(self, idx):
        return View([self], self.ap[idx])

    @property
    def v(self):
        return View([self], self.ap)


class View:
    def __init__(self, bufs, ap):
        self.bufs = bufs
        self.ap = ap

    @property
    def v(self):
        return self

    def __getitem__(self, idx):
        return View(self.bufs, self.ap[idx])


def raw(ap):
    return View([], ap)


class Instr:
    __slots__ = ("fn", "deps", "signal", "semval", "dma_inc")

    def __init__(self, fn, deps, dma_inc=None):
        self.fn = fn
        self.deps = deps
        self.signal = False
        self.semval = None
        self.dma_inc = dma_inc


ENGS = ["pe", "act", "dve", "pool", "sp"]


class Prog:
    def __init__(self, nc):
        self.nc = nc
        self.q = {e: [] for e in ENGS}
        self.stack = ExitStack()
        self.esem = {}
        self.n_dsem = 0
        self.all_dma_bufs = []

    def sbuf(self, name, shape, dtype):
        return self.stack.enter_context(self.nc.sbuf_tensor(name, list(shape), dtype))

    def psum(self, name, shape, dtype):
        return self.stack.enter_context(self.nc.psum_tensor(name, list(shape), dtype))

    def _get_dsem(self, buf):
        if buf.dsem is None:
            buf.dsem = self.stack.enter_context(self.nc.semaphore("d%d" % self.n_dsem))
            self.n_dsem += 1
            self.all_dma_bufs.append(buf)
        return buf.dsem

    def _deps(self, eng, ins, outs):
        deps = []
        for v in ins:
            for b in v.bufs:
                if b.w is not None:
                    deps.append(b.w)
                if b.psum:
                    for k, t in b.r.items():
                        if k != eng:
                            deps.append(t)
        for v in outs:
            for b in v.bufs:
                if b.w is not None:
                    deps.append(b.w)
                deps.extend(b.r.values())
        if eng == "pe":
            deps = [d for d in deps if not (d[0] == "E" and d[1] == "pe")]
        return deps

    def _mark(self, tok, key, ins, outs):
        for v in ins:
            for b in v.bufs:
                old = b.r.get(key)
                if old is None or old[2] < tok[2]:
                    b.r[key] = tok
        for v in outs:
            for b in v.bufs:
                b.w = tok
                b.r = {}

    def op(self, eng, fn, ins=(), outs=()):
        ins = [x.v if isinstance(x, Buf) else x for x in ins]
        outs = [x.v if isinstance(x, Buf) else x for x in outs]
        deps = self._deps(eng, ins, outs)
        idx = len(self.q[eng])
        self.q[eng].append(Instr(fn, deps))
        tok = ("E", eng, idx)
        self._mark(tok, eng, ins, outs)
        return tok

    def dma(self, out, in_, queue="sp", **kw):
        out = out.v if isinstance(out, Buf) else out
        in_ = in_.v if isinstance(in_, Buf) else in_
        owner = None
        for v in (out, in_):
            for b in v.bufs:
                owner = b
                break
            if owner is not None:
                break
        assert owner is not None
        sem = self._get_dsem(owner)
        owner.dcnt += 1
        val = 16 * owner.dcnt
        deps = self._deps(queue, [in_], [out])
        oap, iap = out.ap, in_.ap
        self.q[queue].append(Instr(lambda e: e.dma_start(out=oap, in_=iap, **kw), deps, dma_inc=sem))
        tok = ("S", sem, val)
        self._mark(tok, ("S", id(sem)), [in_], [out])
        return tok

    def _last_real(self, e):
        for i in range(len(self.q[e]) - 1, -1, -1):
            ins = self.q[e][i]
            if ins.fn is not None and ins.dma_inc is None:
                return ("E", e, i)
        return None

    def _all_toks(self):
        toks = []
        for e in ENGS:
            t = self._last_real(e)
            if t is not None:
                toks.append(t)
        for b in self.all_dma_bufs:
            toks.append(("S", b.dsem, 16 * b.dcnt))
        return toks

    def barrier(self):
        toks = self._all_toks()
        for e in ENGS:
            deps = [t for t in toks if not (t[0] == "E" and t[1] == e)]
            self.q[e].append(Instr(None, deps))

    def emit(self):
        nc = self.nc
        for e in ENGS:
            self.esem[e] = self.stack.enter_context(nc.semaphore("e_" + e))
        self.q["sp"].append(Instr(None, [t for t in self._all_toks() if not (t[0] == "E" and t[1] == "sp")]))
        for e in ENGS:
            for ins in self.q[e]:
                for d in ins.deps:
                    if d[0] == "E":
                        self.q[d[1]][d[2]].signal = True
        for e in ENGS:
            c = 0
            for ins in self.q[e]:
                if ins.signal:
                    assert ins.fn is not None and ins.dma_inc is None
                    c += 1
                    ins.semval = c
        handles = {"pe": "tensor", "act": "scalar", "dve": "vector", "pool": "gpsimd", "sp": "sync"}
        stats = {}
        with nc.Block() as block:
            for e in ENGS:
                def body(eh, e=e):
                    known = {}
                    nw = 0
                    for ins in self.q[e]:
                        waits = {}
                        for d in ins.deps:
                            if d[0] == "E":
                                sem = self.esem[d[1]]
                                val = self.q[d[1]][d[2]].semval
                            else:
                                sem, val = d[1], d[2]
                            k = id(sem)
                            if known.get(k, 0) >= val:
                                continue
                            if k not in waits or waits[k][1] < val:
                                waits[k] = (sem, val)
                        wl = list(waits.values())
                        for k, (sem, val) in waits.items():
                            known[k] = val
                        nw += len(wl)
                        if ins.fn is None:
                            for sem, val in wl:
                                eh.wait_ge(sem, val)
                            continue
                        for sem, val in wl[:-1]:
                            eh.wait_ge(sem, val)
                        bi = ins.fn(eh)
                        if wl:
                            bi._wait_ge(wl[-1][0], wl[-1][1])
                        if ins.dma_inc is not None:
                            bi.then_inc(ins.dma_inc, 16)
                        elif ins.signal:
                            bi.then_inc(self.esem[e], 1)
                    stats[e] = (len(self.q[e]), nw)
                getattr(block, handles[e])(body)
        self.stats = stats
        return stats


PARAM_SPECS = [
    ("x", [S, D]), ("mem", [MEM, D]), ("rel_bias", [8, 32]), ("final_norm_g", [D]),
    ("ffn1_norm_g", [DEPTH, D]), ("ffn1_w_in", [DEPTH, D, 2 * DFF]), ("ffn1_w_out", [DEPTH, DFF, D]),
    ("mix_norm_g", [DEPTH, D]), ("w_mix_in", [DEPTH, D, DPROJ]), ("w_mix_out", [DEPTH, D, D]),
    ("rwkv_mu", [DEPTH, RPROJ]), ("rwkv_w0", [DEPTH, RW]), ("rwkv_w_up", [DEPTH, 64, RW]),
    ("rwkv_a0", [DEPTH, RW]), ("rwkv_a_up", [DEPTH, 64, RW]), ("rwkv_g_up", [DEPTH, 128, RW]),
    ("rwkv_k_k", [DEPTH, RW]), ("rwkv_k_a", [DEPTH, RW]), ("rwkv_r_k", [DEPTH, 8, 64]),
    ("rwkv_ln_g", [DEPTH, RW]), ("rwkv_ln_b", [DEPTH, RW]),
    ("xattn_norm_g", [DEPTH, D]), ("mem_norm_g", [DEPTH, D]),
    ("xattn_w_q", [DEPTH, D, D]), ("xattn_w_kv", [DEPTH, D, 2 * D]), ("xattn_w_o", [DEPTH, D, D]),
    ("ffn2_norm_g", [DEPTH, D]), ("ffn2_w_in", [DEPTH, D, 2 * DFF]), ("ffn2_w_out", [DEPTH, DFF, D]),
]

GAIN_NAMES = ["ffn1_norm_g", "mix_norm_g", "xattn_norm_g", "mem_norm_g", "ffn2_norm_g"]


class Carver:
    def __init__(self, big, nwords):
        self.big = big
        self.n = nwords
        self.o = 0

    def f32(self, n, pat=None, **kw):
        ap = self.big[:, self.o:self.o + n]
        self.o += n
        assert self.o <= self.n, "scratch overflow"
        return ap.rearrange(pat, **kw) if pat else ap

    def b16(self, n, pat=None, **kw):
        w = (n + 1) // 2
        ap = self.big[:, self.o:self.o + w].bitcast(BF16)
        self.o += w
        assert self.o <= self.n, "scratch overflow"
        return ap.rearrange(pat, **kw) if pat else ap


class K:
    def __init__(self, stages, final_norm=True, debug=False):
        self.stages = stages
        self.final_norm = final_norm
        nc = bass.Bass("TRN2", target_bir_lowering=False)
        self.nc = nc
        self.P = Prog(nc)
        self.dram = {}
        for name, shape in PARAM_SPECS:
            self.dram[name] = nc.dram_tensor(name, shape, F32, kind="ExternalInput").ap()
        self.dram["oh_tab"] = nc.dram_tensor("oh_tab", [32, 2048], F32, kind="ExternalInput").ap()
        self.ebrep_t = nc.dram_tensor("ebrep", [8, 128, 2048], BF16, kind="Internal")
        self.out_ap = nc.dram_tensor("out", [S, D], F32, kind="ExternalOutput").ap()
        self.debug = debug
        if debug:
            self.dbg_ap = nc.dram_tensor("dbg", [128, 16 * 512], F32, kind="ExternalOutput").ap()
            self.DBG = Buf(self.dbg_ap, "DBG")
            self.dbg_n = 0
        self.pb_i = 0
        self.bank_set = list(range(8))
        self.ring_i = 0
        self.ev_i = 0

    def alloc(self):
        P = self.P
        self.xT_t = P.sbuf("xT", [128, 8, S], F32)
        self.xb = [[Buf(self.xT_t[:, c, tb * 512:(tb + 1) * 512], "x%d_%d" % (c, tb)) for tb in range(4)]
                   for c in range(8)]
        self.ident_f = Buf(P.sbuf("ident_f", [128, 128], F32)[:], "ident_f")
        self.ident_b = Buf(P.sbuf("ident_b", [128, 128], BF16)[:], "ident_b")
        self.ones_b = Buf(P.sbuf("ones_b", [128, 128], BF16)[:], "ones_b")
        self.pt_stage = Buf(P.sbuf("pt_stage", [128, 128], F32)[:], "pt_stage")
        self.PT = Buf(P.sbuf("PT", [128, 256], F32)[:], "PT")
        self.ps_t = P.psum("ps", [128, 8, 512], F32)
        self.banks = [Buf(self.ps_t[:, i, :], "bank%d" % i, psum=True) for i in range(8)]
        self.NSLOT = 3
        self.ring_t = P.sbuf("ring", [128, self.NSLOT, 22 * 128], BF16)
        self.ring = [Buf(self.ring_t[:, i, :], "ring%d" % i) for i in range(self.NSLOT)]
        self.BIGW = 27 * 1024
        self.big = P.sbuf("big", [128, self.BIGW], F32)

    def carver(self):
        return Carver(self.big, self.BIGW)

    def dump(self, view, n, name=""):
        if not self.debug:
            return
        if not hasattr(self, "dbg_stg"):
            self.dbg_stg = [Buf(self.P.sbuf("dbgs%d" % i, [128, 512], F32)[:], "dbgs%d" % i) for i in range(2)]
        st = self.dbg_stg[self.dbg_n % 2]
        np_ = view.ap.shape[0]
        self.P.op("pool", lambda e: e.memset(st.ap, 0.0), outs=[st.v])
        bp = 0
        self.copy("dve", st[bp:bp + np_, 0:n], view)
        self.P.dma(self.DBG[:, self.dbg_n * 512:(self.dbg_n + 1) * 512], st.v)
        print("dump slot", self.dbg_n, name)
        self.dbg_n += 1

    def act(self, out, in_, func, bias=None, scale=None):
        oap, iap = out.ap, in_.ap
        ins = [in_]
        kw = {}
        if bias is not None:
            if isinstance(bias, (View, Buf)):
                ins.append(bias)
                kw["bias"] = bias.ap
            else:
                kw["bias"] = bias
        if scale is not None:
            if isinstance(scale, (View, Buf)):
                ins.append(scale)
                kw["scale"] = scale.ap
            else:
                kw["scale"] = scale
        self.P.op("act", lambda e: e.activation(out=oap, in_=iap, func=func, **kw), ins=ins, outs=[out])

    def tt(self, eng, out, in0, in1, op):
        oap, a, b = out.ap, in0.ap, in1.ap
        self.P.op(eng, lambda e: e.tensor_tensor(out=oap, in0=a, in1=b, op=op), ins=[in0, in1], outs=[out])

    def ts(self, eng, out, in0, s1, op0, s2=None, op1=None):
        oap, a = out.ap, in0.ap
        ins = [in0]
        v1 = s1
        if isinstance(s1, (View, Buf)):
            ins.append(s1)
            v1 = s1.ap
        v2 = s2
        if isinstance(s2, (View, Buf)):
            ins.append(s2)
            v2 = s2.ap
        if op1 is None:
            self.P.op(eng, lambda e: e.tensor_scalar(out=oap, in0=a, scalar1=v1, scalar2=None, op0=op0), ins=ins, outs=[out])
        else:
            self.P.op(eng, lambda e: e.tensor_scalar(out=oap, in0=a, scalar1=v1, scalar2=v2, op0=op0, op1=op1),
                      ins=ins, outs=[out])

    def stt(self, out, in0, scalar, in1, op0, op1):
        oap, a, b = out.ap, in0.ap, in1.ap
        ins = [in0, in1]
        sv = scalar
        if isinstance(scalar, (View, Buf)):
            ins.append(scalar)
            sv = scalar.ap
        self.P.op("dve", lambda e: e.scalar_tensor_tensor(out=oap, in0=a, scalar=sv, in1=b, op0=op0, op1=op1),
                  ins=ins, outs=[out])

    def rsqrt_act(self, out, in_, scale=1.0, bias=None):
        self.act(out, in_, AF.Ln, bias=bias, scale=scale)
        self.act(out, out, AF.Exp, scale=-0.5)

    def recip_act(self, out, in_):
        self.act(out, in_, AF.Ln)
        self.act(out, out, AF.Exp, scale=-1.0)

    def recip(self, out, in_):
        oap, iap = out.ap, in_.ap
        self.P.op("dve", lambda e: e.reciprocal(out=oap, in_=iap), ins=[in_], outs=[out])

    def memset(self, eng, out, val):
        oap = out.ap
        self.P.op(eng, lambda e: e.memset(oap, val), outs=[out])

    def bank(self):
        bs = self.bank_set
        b = self.banks[bs[self.pb_i % len(bs)]]
        self.pb_i += 1
        return b

    def evac_eng(self):
        self.ev_i += 1
        return "act" if self.ev_i % 2 == 0 else "dve"

    def copy(self, eng, out, in_):
        oap, iap = out.ap, in_.ap
        if eng == "act":
            self.P.op("act", lambda e: e.activation(out=oap, in_=iap, func=AF.Copy), ins=[in_], outs=[out])
        else:
            self.P.op(eng, lambda e: e.tensor_copy(out=oap, in_=iap), ins=[in_], outs=[out])

    def mm(self, out, lhsT, rhs, start, stop, r32=False):
        oap, lap, rap = out.ap, lhsT.ap, rhs.ap
        if r32:
            lap, rap = lap.bitcast(F32R), rap.bitcast(F32R)
        self.P.op("pe", lambda e: e.matmul(oap, lhsT=lap, rhs=rap, start=start, stop=stop),
                  ins=[lhsT, rhs], outs=[out])

    def transpose(self, out, in_, ident):
        oap, iap, idap = out.ap, in_.ap, ident.ap
        self.P.op("pe", lambda e: e.transpose(out=oap, in_=iap, identity=idap), ins=[in_, ident], outs=[out])

    def load_w(self, w_ap, kc, ncols):
        slot = self.ring[self.ring_i % self.NSLOT]
        self.ring_i += 1
        dst = View([slot], slot.ap[:, 0:kc * ncols].rearrange("p (k n) -> p k n", k=kc))
        src = raw(w_ap.rearrange("(k p) n -> p k n", p=128))
        self.P.dma(dst, src, queue="pool")
        return dst

    def xview(self, cs, tb):
        bufs = [self.xb[c][tb] for c in range(cs.start, cs.stop)]
        return View(bufs, self.xT_t[:, cs, tb * 512:(tb + 1) * 512])

    def setup(self):
        P = self.P
        idf, idb, ones = self.ident_f, self.ident_b, self.ones_b
        P.op("pool", lambda e: e.memset(idf.ap, 0.0), outs=[idf.v])
        P.op("pool", lambda e: e.affine_select(out=idf.ap, in_=idf.ap, compare_op=ALU.not_equal, fill=1.0,
                                               base=0, pattern=[[-1, 128]], channel_multiplier=1),
             ins=[idf.v], outs=[idf.v])
        self.copy("dve", idb.v, idf.v)
        P.op("pool", lambda e: e.memset(ones.ap, 1.0), outs=[ones.v])
        self.pcol = {}
        col = 0
        groups = []
        rows = 0
        cur = []
        plist = [(n, DEPTH * 8) for n in GAIN_NAMES] + [("final_norm_g", 8)]
        plist += [("rwkv_mu", DEPTH * 14)] + [(n, DEPTH * 4) for n in
                                              ["rwkv_w0", "rwkv_a0", "rwkv_k_k", "rwkv_k_a", "rwkv_r_k",
                                               "rwkv_ln_g", "rwkv_ln_b"]]
        for name, nrow in plist:
            if rows + nrow > 128:
                groups.append(cur)
                cur = []
                rows = 0
            cur.append((name, nrow, rows))
            rows += nrow
        groups.append(cur)
        for grp in groups:
            st = self.pt_stage
            P.op("pool", lambda e: e.memset(st.ap, 0.0), outs=[st.v])
            tot = 0
            for name, nrow, r0 in grp:
                ap = self.dram[name]
                if name == "final_norm_g":
                    src = ap.rearrange("(c p) -> c p", p=128)
                elif name == "rwkv_r_k":
                    src = ap.rearrange("l (m two) d -> (l m) (two d)", two=2)
                else:
                    src = ap.rearrange("l (c p) -> (l c) p", p=128)
                P.dma(st[r0:r0 + nrow, :], raw(src))
                self.pcol[name] = col + r0
                tot = r0 + nrow
            bk = self.bank()
            self.transpose(bk[:, 0:128], st.v, self.ident_f)
            self.copy("dve", self.PT[:, col:col + tot], bk[:, 0:tot])
            col += tot
        assert col <= 256
        cv = self.carver()
        stg = [Buf(cv.f32(1024), "xstg%d" % i) for i in range(3)]
        for tt in range(16):
            st = stg[tt % 3]
            P.dma(st.v, raw(self.dram["x"][tt * 128:(tt + 1) * 128, :]))
            tb = tt // 4
            t0 = (tt % 4) * 128
            for h in range(2):
                bk = self.bank()
                for c4 in range(4):
                    c = h * 4 + c4
                    self.transpose(bk[:, c4 * 128:(c4 + 1) * 128], st[:, c * 128:(c + 1) * 128], self.ident_f)
                cs = slice(h * 4, h * 4 + 4)
                dst = View([self.xb[c][tb] for c in range(cs.start, cs.stop)],
                           self.xT_t[:, cs, tb * 512 + t0: tb * 512 + t0 + 128])
                self.copy(self.evac_eng(), dst, View([bk], bk.ap.rearrange("p (c t) -> p c t", c=4)))
        P.barrier()

    def gcol(self, name, l):
        return self.pcol[name] + l * 8

    def rmsnorm(self, tb, gc, hT_view, sq, rstd, out_dtype_bf16=True):
        P = self.P
        xv = self.xview(slice(0, 8), tb)
        xap, sqap = xv.ap, sq.ap
        P.op("act", lambda e: e.activation(out=sqap, in_=xap, func=AF.Square), ins=[xv], outs=[sq.v])
        bk = self.bank()
        for c in range(8):
            self.mm(bk.v, self.ones_b.v, sq[:, c, :], start=(c == 0), stop=(c == 7))
        rap = rstd.ap
        self.rsqrt_act(rstd.v, bk.v, scale=1.0 / D, bias=self.eps_tiles[NORM_EPS])
        for c in range(8):
            xin = self.xb[c][tb]
            o = hT_view[:, c, :]
            oap, iap, gap = o.ap, xin.ap, self.PT.ap[:, gc + c: gc + c + 1]
            eng = "dve"
            P.op(eng, lambda e, oap=oap, iap=iap, gap=gap: e.scalar_tensor_tensor(
                out=oap, in0=iap, scalar=gap, in1=rap, op0=ALU.mult, op1=ALU.mult),
                ins=[xin.v, self.PT.v, rstd.v], outs=[o])

    def eps_ap(self, val):
        return self.eps_tiles[val].ap

    def make_eps(self):
        self.eps_tiles = {}
        for i, val in enumerate([NORM_EPS, LNX_EPS]):
            b = Buf(self.P.sbuf("eps%d" % i, [128, 1], F32)[:], "eps%d" % i)
            self.P.op("pool", lambda e, b=b, val=val: e.memset(b.ap, val), outs=[b.v])
            self.eps_tiles[val] = b

    def ffn(self, l, which):
        P = self.P
        w_in = self.dram["ffn%d_w_in" % which][l]
        w_out = self.dram["ffn%d_w_out" % which][l]
        gc = self.gcol("ffn%d_norm_g" % which, l)
        cv = self.carver()
        hT = [Buf(cv.b16(4096, "p (c t) -> p c t", c=8), "hT%d" % i) for i in range(2)]
        aT = [[Buf(cv.b16(512), "aT%d_%d" % (j, i)) for i in range(2)] for j in range(22)]
        sq = Buf(cv.b16(4096, "p (c t) -> p c t", c=8), "sq")
        rstd = Buf(cv.f32(512), "rstd")
        sg = [Buf(cv.f32(512), "sg%d" % i) for i in range(4)]
        for half in range(2):
            for i in range(2):
                self.rmsnorm(2 * half + i, gc, hT[i].v, sq, rstd)
            for j in range(22):
                wg = self.load_w(w_in[:, j * 128:(j + 1) * 128], 8, 128)
                wu = self.load_w(w_in[:, DFF + j * 128:DFF + (j + 1) * 128], 8, 128)
                for i in range(2):
                    pg = self.bank()
                    pu = self.bank()
                    for k in range(8):
                        self.mm(pg.v, wg[:, k, :], hT[i][:, k, :], start=(k == 0), stop=(k == 7))
                    for k in range(8):
                        self.mm(pu.v, wu[:, k, :], hT[i][:, k, :], start=(k == 0), stop=(k == 7))
                    s = sg[(j * 2 + i) % 4]
                    sap, pgap, puap, aap = s.ap, pg.ap, pu.ap, aT[j][i].ap
                    P.op("act", lambda e, sap=sap, pgap=pgap: e.activation(out=sap, in_=pgap, func=AF.Silu),
                         ins=[pg.v], outs=[s.v])
                    P.op("dve", lambda e, sap=sap, puap=puap, aap=aap: e.tensor_tensor(
                        out=aap, in0=sap, in1=puap, op=ALU.mult), ins=[s.v, pu.v], outs=[aT[j][i].v])
            for c in range(8):
                wo = self.load_w(w_out[:, c * 128:(c + 1) * 128], 22, 128)
                for i in range(2):
                    po = self.bank()
                    for k in range(22):
                        self.mm(po.v, wo[:, k, :], aT[k][i].v, start=(k == 0), stop=(k == 21))
                    xb = self.xb[c][2 * half + i]
                    xap, poap = xb.ap, po.ap
                    P.op("dve", lambda e, xap=xap, poap=poap: e.scalar_tensor_tensor(
                        out=xap, in0=poap, scalar=0.5, in1=xap, op0=ALU.mult, op1=ALU.add),
                        ins=[po.v, xb.v], outs=[xb.v])
        P.barrier()

    def mem_setup(self):
        P = self.P
        self.memT = Buf(P.sbuf("memT", [128, 8, MEM], F32)[:], "memT")
        cv = self.carver()
        stg = [Buf(cv.f32(1024), "mstg%d" % i) for i in range(2)]
        for mt in range(2):
            P.dma(stg[mt].v, raw(self.dram["mem"][mt * 128:(mt + 1) * 128, :]))
            for h in range(2):
                bk = self.bank()
                for c4 in range(4):
                    c = h * 4 + c4
                    self.transpose(bk[:, c4 * 128:(c4 + 1) * 128], stg[mt][:, c * 128:(c + 1) * 128], self.ident_f)
                self.copy(self.evac_eng(), self.memT[:, h * 4:h * 4 + 4, mt * 128:(mt + 1) * 128],
                          View([bk], bk.ap.rearrange("p (c t) -> p c t", c=4)))

    def xattn(self, l):
        P = self.P
        cv = self.carver()
        a16 = cv.b16
        hT = Buf(a16(4096, "p (c t) -> p c t", c=8), "hT")
        sq = Buf(a16(4096, "p (c t) -> p c t", c=8), "sq")
        qT = Buf(a16(4096, "p (c t) -> p c t", c=8), "qT")
        oT = Buf(a16(4096, "p (c t) -> p c t", c=8), "oT")
        mnT = Buf(a16(2048, "p (c t) -> p c t", c=8), "mnT")
        kT = Buf(a16(2048, "p (c t) -> p c t", c=8), "kT")
        vtm = Buf(a16(2048, "p (m n) -> p m n", m=2), "vtm")
        pT = [Buf(a16(512), "pT%d" % i) for i in range(4)]
        rstd = Buf(cv.f32(512), "rstd")
        rden = [Buf(cv.f32(512), "rden%d" % i) for i in range(2)]
        msq = Buf(a16(2048, "p (c t) -> p c t", c=8), "msq")
        mrs = Buf(cv.f32(256), "mrs")
        w_q = self.dram["xattn_w_q"][l]
        w_kv = self.dram["xattn_w_kv"][l]
        w_o = self.dram["xattn_w_o"][l]
        self.dump(self.memT[:, 0, :], 256, "memT0")
        self.dump(self.memT[:, 7, :], 256, "memT7")
        gm = self.gcol("mem_norm_g", l)
        mT = self.memT
        P.op("act", lambda e: e.activation(out=msq.ap, in_=mT.ap, func=AF.Square), ins=[mT.v], outs=[msq.v])
        bk = self.bank()
        for c in range(8):
            self.mm(bk[:, 0:MEM], self.ones_b.v, msq[:, c, :], start=(c == 0), stop=(c == 7))
        self.rsqrt_act(mrs.v, bk[:, 0:MEM], scale=1.0 / D, bias=self.eps_tiles[NORM_EPS])
        for c in range(8):
            oap, iap, gap = mnT.ap[:, c, :], mT.ap[:, c, :], self.PT.ap[:, gm + c:gm + c + 1]
            P.op("dve", lambda e, oap=oap, iap=iap, gap=gap: e.scalar_tensor_tensor(
                out=oap, in0=iap, scalar=gap, in1=mrs.ap, op0=ALU.mult, op1=ALU.mult),
                ins=[mT.v, self.PT.v, mrs.v], outs=[mnT.v])
        self.dump(mrs.v, 256, "mrs")
        self.dump(mnT[:, 0, :], 256, "mnT0")
        for c2 in range(4):
            wk = self.load_w(w_kv[:, c2 * 256:(c2 + 1) * 256], 8, 256)
            for cc in range(2):
                c = c2 * 2 + cc
                bk = self.bank()
                for k in range(8):
                    self.mm(bk[:, 0:MEM], wk[:, k, cc * 128:(cc + 1) * 128], mnT[:, k, :], start=(k == 0), stop=(k == 7))
                self.copy(self.evac_eng(), kT[:, c, :], bk[:, 0:MEM])
        for n4 in range(4):
            wv = self.load_w(w_kv[:, D + n4 * 256:D + (n4 + 1) * 256], 8, 256)
            for mt in range(2):
                bk = self.bank()
                for k in range(8):
                    self.mm(bk[:, 0:256], mnT[:, k, mt * 128:(mt + 1) * 128], wv[:, k, :], start=(k == 0), stop=(k == 7))
                self.copy(self.evac_eng(), vtm[:, mt, n4 * 256:(n4 + 1) * 256], bk[:, 0:256])
        self.dump(kT[:, 0, :], 256, "kT0")
        self.dump(vtm[:, 0, 0:512], 512, "vtm0")
        gx = self.gcol("xattn_norm_g", l)
        for tb in range(4):
            self.rmsnorm(tb, gx, hT.v, sq, rstd)
            for c2 in range(4):
                wq = self.load_w(w_q[:, c2 * 256:(c2 + 1) * 256], 8, 256)
                for cc in range(2):
                    c = c2 * 2 + cc
                    bk = self.bank()
                    for k in range(8):
                        self.mm(bk.v, wq[:, k, cc * 128:(cc + 1) * 128], hT[:, k, :], start=(k == 0), stop=(k == 7))
                    self.copy(self.evac_eng(), qT[:, c, :], bk.v)
            for h in range(4):
                pts = []
                for mt in range(2):
                    bk = self.bank()
                    for dc in range(2):
                        c = 2 * h + dc
                        self.mm(bk.v, kT[:, c, mt * 128:(mt + 1) * 128], qT[:, c, :], start=(dc == 0), stop=(dc == 1))
                    p = pT[(h * 2 + mt) % 4]
                    pap, bap = p.ap, bk.ap
                    P.op("act", lambda e, pap=pap, bap=bap: e.activation(out=pap, in_=bap, func=AF.Exp, scale=1.0 / 16.0),
                         ins=[bk.v], outs=[p.v])
                    pts.append(p)
                bden = self.bank()
                for mt in range(2):
                    self.mm(bden.v, self.ones_b.v, pts[mt].v, start=(mt == 0), stop=(mt == 1))
                rd = rden[h % 2]
                rdap = rd.ap
                self.recip_act(rd.v, bden.v)
                for dc in range(2):
                    c = 2 * h + dc
                    bo = self.bank()
                    for mt in range(2):
                        self.mm(bo.v, vtm[:, mt, c * 128:(c + 1) * 128], pts[mt].v, start=(mt == 0), stop=(mt == 1))
                    oap, boap = oT.ap[:, c, :], bo.ap
                    P.op("dve", lambda e, oap=oap, boap=boap, rdap=rdap: e.tensor_tensor(
                        out=oap, in0=boap, in1=rdap, op=ALU.mult), ins=[bo.v, rd.v], outs=[oT[:, c, :]])
            if tb == 0:
                self.dump(qT[:, 0, :], 512, "qT0")
                self.dump(pT[0].v, 512, "pT0")
                self.dump(rden[0].v, 512, "rden0")
                self.dump(oT[:, 0, :], 512, "oT0")
            for c2 in range(4):
                wo = self.load_w(w_o[:, c2 * 256:(c2 + 1) * 256], 8, 256)
                for cc in range(2):
                    c = c2 * 2 + cc
                    bk = self.bank()
                    for k in range(8):
                        self.mm(bk.v, wo[:, k, cc * 128:(cc + 1) * 128], oT[:, k, :], start=(k == 0), stop=(k == 7))
                    xb = self.xb[c][tb]
                    xap, bap = xb.ap, bk.ap
                    P.op("dve", lambda e, xap=xap, bap=bap: e.tensor_tensor(out=xap, in0=bap, in1=xap, op=ALU.add),
                         ins=[bk.v, xb.v], outs=[xb.v])
        P.barrier()

    def moba_setup(self):
        P = self.P
        cv = self.carver()
        rbs = Buf(cv.f32(32)[0:8, :], "rbs")
        erbT = Buf(cv.f32(8)[0:32, :], "erbT")
        oh = Buf(cv.f32(2048)[0:32, :], "oh")
        ebrow = Buf(cv.b16(2048)[0:8, :], "ebrow")
        self.EBREP = Buf(self.ebrep_t.ap(), "EBREP")
        P.dma(rbs.v, raw(self.dram["rel_bias"]))
        P.dma(oh.v, raw(self.dram["oh_tab"]))
        self.act(rbs.v, rbs.v, AF.Exp)
        bk = self.bank()
        self.transpose(bk[0:32, 0:8], rbs.v, self.ident_f[0:8, 0:8])
        self.copy("dve", erbT.v, bk[0:32, 0:8])
        for c4 in range(4):
            bk = self.bank()
            self.mm(bk[0:8, :], erbT.v, oh[:, c4 * 512:(c4 + 1) * 512], start=True, stop=True)
            self.copy("dve", ebrow[:, c4 * 512:(c4 + 1) * 512], bk[0:8, :])
        srcb = View([ebrow], ebrow.ap.rearrange("h (o c) -> h o c", o=1).to_broadcast([8, 128, 2048]))
        P.dma(self.EBREP.v, srcb)
        self.RB31 = Buf(P.sbuf("RB31", [128, 8], F32)[:], "RB31")
        P.dma(self.RB31.v, raw(self.dram["rel_bias"][:, 31:32].rearrange("h o -> o h").partition_broadcast(128)),
              allow_slow_non_contiguous=True)
        self.IND128 = Buf(P.sbuf("IND", [128, 2048], BF16)[:], "IND128")
        self.IND = View([self.IND128], self.IND128.ap[0:8, :])
        ind = self.IND
        self.memset("pool", ind.v, 1.0)
        P.op("pool", lambda e: e.affine_select(out=ind.ap, in_=ind.ap, compare_op=ALU.is_ge, fill=0.0, base=0,
                                               pattern=[[1, 2048]], channel_multiplier=-256), ins=[ind], outs=[ind])
        P.op("pool", lambda e: e.affine_select(out=ind.ap, in_=ind.ap, compare_op=ALU.is_ge, fill=0.0, base=255,
                                               pattern=[[-1, 2048]], channel_multiplier=256), ins=[ind], outs=[ind])
        P.dma(self.IND128[64:72, :], self.IND128[0:8, :])
        self.ELIG = Buf(P.sbuf("ELIG", [128, 8, 8], F32)[:], "ELIG")
        self.OWN = Buf(P.sbuf("OWN", [128, 8, 8], F32)[:], "OWN")
        self.memset("pool", self.ELIG.v, 0.0)
        self.memset("pool", self.OWN.v, 0.0)
        for qb in range(8):
            self.memset("pool", self.ELIG[:, qb, qb:8], -1e30)
            self.memset("pool", self.OWN[:, qb, qb:qb + 1], 1.0)

    def moba(self, l):
        P = self.P
        cv = self.carver()
        yrw = Buf(cv.b16(8192, "p (c t) -> p c t", c=4), "yrw")
        KT = Buf(cv.b16(8192, "p (c t) -> p c t", c=4), "KT")
        VTM = Buf(cv.b16(8192, "p (k n) -> p k n", k=16), "VTM")
        hT = Buf(cv.b16(4096, "p (c t) -> p c t", c=8), "hT")
        sq = Buf(cv.b16(4096, "p (c t) -> p c t", c=8), "sq")
        rstd = Buf(cv.f32(512), "rstd")
        qs = Buf(cv.b16(512), "qs")
        qf = Buf(cv.f32(512), "qf")
        ymo = Buf(cv.b16(2048, "p (c t) -> p c t", c=4), "ymo")
        bands = [Buf(cv.b16(1920), "band%d" % i) for i in range(2)]
        eT = [Buf(cv.f32(512), "eT%d" % i) for i in range(2)]
        pT = [Buf(cv.b16(512), "pT%d" % i) for i in range(3)]
        rden = [Buf(cv.f32(512), "rden%d" % i) for i in range(2)]
        mbT = Buf(cv.b16(1024, "p (a t) -> p a t", a=2), "mbT")
        selp = Buf(cv.f32(4 * 72, "p (a n) -> p a n", a=4), "selp")
        kmT = Buf(cv.f32(64, "p (c n) -> p c n", c=4), "kmT")
        gm = Buf(cv.f32(64, "p (a n) -> p a n", a=8), "gm")
        top8 = Buf(cv.f32(64, "p (a n) -> p a n", a=8), "top8")
        sel = Buf(cv.f32(64, "p (a n) -> p a n", a=8), "sel")
        w_in = self.dram["w_mix_in"][l]
        w_out = self.dram["w_mix_out"][l]
        gc = self.gcol("mix_norm_g", l)
        self.memset("pool", kmT.v, 0.0)
        self.memset("pool", selp.v, 0.0)
        if (l, "rwkv") not in self.stages:
            self.memset("pool", yrw.v, 0.0)
        import os
        if int(os.environ.get("MOBA_LV", "9")) < 3:
            self.memset("pool", ymo.v, 0.0)
        self.bank_set = [0, 1, 2, 3]
        acc_i = 0
        band_i = 0
        e_i = 0
        p_i = 0
        QOFF = RPROJ
        for tb in range(4):
            self.rmsnorm(tb, gc, hT.v, sq, rstd)
            for m in range(4):
                wq = self.load_w(w_in[:, QOFF + m * 128:QOFF + (m + 1) * 128], 8, 128)
                wk = self.load_w(w_in[:, QOFF + 512 + m * 128:QOFF + 512 + (m + 1) * 128], 8, 128)
                wv = self.load_w(w_in[:, QOFF + 1024 + m * 128:QOFF + 1024 + (m + 1) * 128], 8, 128)
                import os
                SK = os.environ.get("MOBA_SKIP", "").split(",")
                bq = self.bank()
                for k in range(8):
                    self.mm(bq.v, wq[:, k, :], hT[:, k, :], start=(k == 0), stop=(k == 7))
                if "qf" not in SK:
                    self.copy("act", qf.v, bq.v)
                if "qs" not in SK:
                    self.ts("dve", qs.v, bq.v, 0.125, ALU.mult)
                bkk = self.bank()
                for k in range(8):
                    self.mm(bkk.v, wk[:, k, :], hT[:, k, :], start=(k == 0), stop=(k == 7))
                self.copy("act", KT[:, m, tb * 512:(tb + 1) * 512], bkk.v)
                for par in range(2):
                    pr = slice(par * 64, (par + 1) * 64)
                    kin = View([bkk], bkk.ap[pr, :].rearrange("p (a t) -> p a t", a=2))
                    kout = kmT[pr, m, par * 8 + 2 * tb:par * 8 + 2 * tb + 2]
                    P.op("dve", lambda e, o=kout.ap, i=kin.ap: e.tensor_reduce(out=o, in_=i, axis=AX.X, op=ALU.add),
                         ins=[kin], outs=[kout])
                bv = self.bank()
                if "v" not in SK:
                    for tt in range(4):
                        for k in range(8):
                            self.mm(bv[:, tt * 128:(tt + 1) * 128], hT[:, k, tt * 128:(tt + 1) * 128], wv[:, k, :],
                                    start=(k == 0), stop=(k == 7))
                    self.copy("dve", VTM[:, tb * 4:tb * 4 + 4, m * 128:(m + 1) * 128],
                              View([bv], bv.ap.rearrange("p (a n) -> p a n", a=4)))
                import os
                LV = int(os.environ.get("MOBA_LV", "9"))
                if LV < 2:
                    continue
                bg = self.bank()
                for tt in range(4):
                    self.mm(bg[:, tt * 16:(tt + 1) * 16], qf[:, tt * 128:(tt + 1) * 128], kmT[:, m, :],
                            start=True, stop=True)
                for qq in range(2):
                    qb = 2 * tb + qq
                    el = View([self.ELIG], self.ELIG.ap[:, qb:qb + 1, :].to_broadcast([128, 4, 8]))
                    self.tt("dve", gm[:, qq * 4:(qq + 1) * 4, :],
                            View([bg], bg.ap[:, qq * 32:(qq + 1) * 32].rearrange("p (a n) -> p a n", a=4)), el, ALU.add)
                GLV = int(os.environ.get("GATE_LV", "9"))
                if GLV < 2:
                    continue
                for a in range(8):
                    P.op("dve", lambda e, o=top8.ap[:, a, :], i=gm.ap[:, a, :]: e.max(out=o, in_=i),
                         ins=[gm.v], outs=[top8.v])
                thr = View([top8], top8.ap[:, :, 2:3].to_broadcast([128, 8, 8]))
                self.tt("dve", sel.v, gm.v, thr, ALU.is_ge)
                for qq in range(2):
                    qb = 2 * tb + qq
                    ow = View([self.OWN], self.OWN.ap[:, qb:qb + 1, :].to_broadcast([128, 4, 8]))
                    self.tt("dve", sel[:, qq * 4:(qq + 1) * 4, :], sel[:, qq * 4:(qq + 1) * 4, :], ow, ALU.max)
                self.ts("dve", sel.v, sel.v, 30000.0, ALU.mult, -30000.0, ALU.add)
                if GLV < 3:
                    continue
                bt = self.bank()
                for tt in range(4):
                    self.transpose(bt[0:8, tt * 128:(tt + 1) * 128], sel[:, tt * 2, :], self.ident_f)
                self.copy("act", mbT[0:8, 0, :], bt[0:8, :])
                self.copy("dve", selp[:, :, 64:72], View([sel], sel.ap.rearrange("p (t two) n -> p t two n", two=2)[:, :, 1, :]))
                bt = self.bank()
                for tt in range(4):
                    self.transpose(bt[0:72, tt * 128:(tt + 1) * 128], selp[:, tt, :], self.ident_f)
                self.copy("act", mbT[64:72, 1, :], bt[64:72, :])
                if LV < 3:
                    continue
                nkt = 4 * (tb + 1)
                for par in range(2):
                    h = 2 * m + par
                    pr = slice(par * 64, (par + 1) * 64)
                    band = bands[band_i % 2]
                    band_i += 1
                    src = bass.AP(self.ebrep_t, h * 128 * 2048 + 128, [[2047, 128], [1, 1920]])
                    P.dma(band.v, View([self.EBREP], src))
                    bo = self.banks[4 + 2 * (acc_i % 2)]
                    bd = self.banks[5 + 2 * (acc_i % 2)]
                    acc_i += 1
                    def scores(kt):
                        bs = self.bank()
                        self.mm(bs.v, KT[pr, m, kt * 128:(kt + 1) * 128], qs[pr, :], start=True, stop=False)
                        mr = slice(par * 64, par * 64 + 8)
                        self.mm(bs.v, self.IND128[mr, kt * 128:(kt + 1) * 128], mbT[mr, par, :], start=False, stop=True)
                        return bs
                    LOOK = 2
                    pend = [scores(kt) for kt in range(min(LOOK, nkt))]
                    for kt in range(nkt):
                        bs = pend.pop(0)
                        if kt + LOOK < nkt:
                            pend.append(scores(kt + LOOK))
                        delta = tb * 512 - kt * 128
                        p = pT[p_i % 3]
                        p_i += 1
                        if delta >= 1024:
                            self.act(p.v, bs.v, AF.Exp, bias=self.RB31[:, h:h + 1])
                        else:
                            et = eT[e_i % 2]
                            e_i += 1
                            self.act(et.v, bs.v, AF.Exp)
                            self.tt("dve", p.v, et.v, band[:, delta + 384:delta + 384 + 512], ALU.mult)
                        self.mm(bo.v, VTM[:, kt, m * 128:(m + 1) * 128], p.v, start=(kt == 0), stop=(kt == nkt - 1))
                        self.mm(bd.v, self.ones_b.v, p.v, start=(kt == 0), stop=(kt == nkt - 1))
                    rd = rden[par]
                    self.recip_act(rd[pr, :], bd[pr, :])
                    self.tt("dve", ymo[pr, m, :], bo[pr, :], rd[pr, :], ALU.mult)
            for c2 in range(4):
                wo = self.load_w(w_out[:, c2 * 256:(c2 + 1) * 256], 8, 256)
                for cc in range(2):
                    c = c2 * 2 + cc
                    bk = self.bank()
                    for k in range(8):
                        rhs = yrw[:, k, tb * 512:(tb + 1) * 512] if k < 4 else ymo[:, k - 4, :]
                        self.mm(bk.v, wo[:, k, cc * 128:(cc + 1) * 128], rhs, start=(k == 0), stop=(k == 7))
                    xb = self.xb[c][tb]
                    self.tt("dve", xb.v, bk.v, xb.v, ALU.add)
        self.bank_set = list(range(8))
        P.barrier()

    def rwkv_setup(self):
        P = self.P
        self.BONES = Buf(P.sbuf("BONES", [128, 128], BF16)[:], "BONES")
        self.memset("pool", self.BONES.v, 0.0)
        self.memset("pool", self.BONES[0:64, 0:64], 1.0)
        self.memset("pool", self.BONES[64:128, 64:128], 1.0)
        self.MUs = Buf(P.sbuf("MUs", [64, 64], F32)[:], "MUs")
        self.MUi = Buf(P.sbuf("MUi", [64, 64], F32)[:], "MUi")
        self.MLs = Buf(P.sbuf("MLs", [64, 64], F32)[:], "MLs")
        for mk, op, pat, cm in [(self.MUs, ALU.is_gt, [[1, 64]], -1), (self.MUi, ALU.is_ge, [[1, 64]], -1),
                                (self.MLs, ALU.is_gt, [[-1, 64]], 1)]:
            self.memset("pool", mk.v, 1.0)
            P.op("pool", lambda e, mk=mk, op=op, pat=pat, cm=cm: e.affine_select(
                out=mk.ap, in_=mk.ap, compare_op=op, fill=0.0, base=0, pattern=pat, channel_multiplier=cm),
                ins=[mk], outs=[mk])
        self.RMASK = Buf(P.sbuf("RMASK", [128, 512], F32)[:], "RMASK")
        self.memset("pool", self.RMASK.v, 1.0)
        self.memset("pool", View([self.RMASK], self.RMASK.ap.rearrange("p (c t) -> p c t", t=64)[:, :, 0:1]), 0.0)

    def rwkv(self, l):
        P = self.P
        CD = 0.6065306597126334
        cv = self.carver()
        yrw = Buf(cv.b16(8192, "p (c t) -> p c t", c=4), "yrw")
        hT = Buf(cv.b16(4096, "p (c t) -> p c t", c=8), "hT")
        ra_o = cv.o
        RA = Buf(cv.f32(2048), "RA")
        sq = View([RA], RA.ap.bitcast(BF16).rearrange("p (c t) -> p c t", c=8))
        rstd = Buf(cv.f32(512), "rstd")
        waup = Buf(cv.b16(512), "waup")
        gup = Buf(cv.b16(512), "gup")
        lo12 = Buf(cv.b16(512), "lo12")
        sgl = Buf(cv.b16(512), "sgl")
        carry = Buf(cv.f32(16), "carry")
        Hf_ap = cv.f32(512, "p (m i) -> p m i", m=4)
        Hb_ap = cv.b16(512, "p (m i) -> p m i", m=4)
        Hf = Buf(Hf_ap, "Hf")
        Hb = Buf(Hb_ap, "Hb")
        HfB = [[Buf(Hf_ap[p_ * 64:(p_ + 1) * 64, m_, p_ * 64:(p_ + 1) * 64], "Hf%d%d" % (m_, p_)) for p_ in range(2)] for m_ in range(4)]
        HbB = [[Buf(Hb_ap[p_ * 64:(p_ + 1) * 64, m_, p_ * 64:(p_ + 1) * 64], "Hb%d%d" % (m_, p_)) for p_ in range(2)] for m_ in range(4)]
        praw = [Buf(cv.f32(516), "praw%d" % i) for i in range(2)]
        f32t = {}
        f32o = {}
        for nm in ["rf", "kf", "lerp", "sig", "asig", "Lr", "eL", "eLm", "eX", "t1", "kkn", "kp", "bb", "bonus"]:
            f32o[nm] = cv.o
            f32t[nm] = Buf(cv.f32(512), nm)
        f32t["ys"] = f32t["lerp"]
        b16t = {}
        for nm in ["rt", "kt", "bt", "at", "vT", "kh", "bh", "gt", "tmpb"]:
            b16t[nm] = Buf(cv.b16(512), nm)
        khT = Buf(cv.b16(1024, "p (c j) -> p c j", c=8)[0:64], "khT")
        bhT = Buf(cv.b16(1024, "p (c j) -> p c j", c=8)[0:64], "bhT")
        vTM = Buf(cv.b16(1024, "p (c j) -> p c j", c=8)[0:64], "vTM")
        mats = {}
        for nm in ["TTb", "AkT", "ArbT", "ArkT", "ZC"]:
            mats[nm] = Buf(cv.b16(1024, "p (a t) -> p a t", a=16)[0:64], nm)
        inv = {}
        for nm in ["M", "N"]:
            inv[nm] = Buf(cv.b16(1024, "p (a t) -> p a t", a=16)[0:64], nm)

        def reg(o):
            return self.big[0:64, o:o + 512].bitcast(BF16).rearrange("p (a t) -> p a t", a=16)
        inv["M2"] = View([RA], reg(ra_o))
        inv["N2"] = View([RA], reg(ra_o + 512))
        inv["P2"] = View([f32t["rf"]], reg(f32o["rf"]))
        inv["Pm"] = View([f32t["sig"]], reg(f32o["sig"]))
        Zs = Buf(cv.b16(128)[0:64], "Zs")
        Us = Buf(cv.b16(128)[0:64], "Us")
        w_in = self.dram["w_mix_in"][l]
        gc = self.gcol("mix_norm_g", l)
        pc = self.pcol
        PT = self.PT

        def pcolv(name, idx, n_per_layer):
            c = pc[name] + l * n_per_layer + idx
            return PT[:, c:c + 1]
        P.dma(waup[0:64, :], raw(self.dram["rwkv_w_up"][l]), queue="pool")
        P.dma(waup[64:128, :], raw(self.dram["rwkv_a_up"][l]), queue="pool")
        P.dma(gup.v, raw(self.dram["rwkv_g_up"][l]), queue="pool")
        import os
        RLV = int(os.environ.get("RWKV_LV", "9"))
        if RLV < 9:
            self.memset("pool", yrw.v, 0.0)
        self.memset("pool", carry.v, 0.0)
        for m_ in range(4):
            for p_ in range(2):
                self.memset("pool", HfB[m_][p_].v, 0.0)
                self.memset("pool", HbB[m_][p_].v, 0.0)
        self.bank_set = [0, 1, 2, 3, 4]
        BY = self.banks[5]
        pri = 0

        def project_lerp(j, out_view, tb):
            nonlocal pri
            w = self.load_w(w_in[:, j * 128:(j + 1) * 128], 8, 128)
            bk = self.bank()
            for k in range(8):
                self.mm(bk.v, w[:, k, :], hT[:, k, :], start=(k == 0), stop=(k == 7))
            pr_ = praw[pri % 2]
            pri += 1
            self.copy("dve", pr_[:, 0:1], carry[:, j:j + 1])
            self.copy("act", pr_[:, 1:513], bk.v)
            self.copy("dve", carry[:, j:j + 1], pr_[:, 512:513])
            d = f32t["t1"]
            self.tt("dve", d.v, pr_[:, 0:512], pr_[:, 1:513], ALU.subtract)
            self.stt(out_view, d.v, pcolv("rwkv_mu", j, 14), pr_[:, 1:513], ALU.mult, ALU.add)

        for tb in range(4):
            self.rmsnorm(tb, gc, hT.v, sq, rstd)
            lerp = f32t["lerp"]
            project_lerp(12, lerp.v, tb)
            self.act(lo12[0:64, :], lerp[0:64, :], AF.Tanh)
            self.copy("dve", lo12[64:128, :], lerp[64:128, :])
            project_lerp(13, lerp.v, tb)
            self.act(sgl.v, lerp.v, AF.Sigmoid)
            for m in range(4):
                rf, kf, sig, asig, Lr, eL, eLm, eX = (f32t[n] for n in ["rf", "kf", "sig", "asig", "Lr", "eL", "eLm", "eX"])
                t1, kkn, kp, bb, ys, bonus = (f32t[n] for n in ["t1", "kkn", "kp", "bb", "ys", "bonus"])
                rt, kt, bt, at, vT, kh, bh, gt, tmpb = (b16t[n] for n in ["rt", "kt", "bt", "at", "vT", "kh", "bh", "gt", "tmpb"])
                project_lerp(m, rf.v, tb)
                project_lerp(4 + m, kf.v, tb)
                project_lerp(8 + m, lerp.v, tb)
                self.copy("act", vT.v, lerp.v)
                bw = self.bank()
                self.mm(bw.v, waup[0:64, m * 128:(m + 1) * 128], lo12[0:64, :], start=True, stop=True)
                self.act(sig.v, bw.v, AF.Sigmoid, bias=pcolv("rwkv_w0", m, 4))
                ba = self.bank()
                self.mm(ba.v, waup[64:128, m * 128:(m + 1) * 128], lo12[64:128, :], start=True, stop=True)
                self.act(asig.v, ba.v, AF.Sigmoid, bias=pcolv("rwkv_a0", m, 4))
                bgt = self.bank()
                self.mm(bgt.v, gup[:, m * 128:(m + 1) * 128], sgl.v, start=True, stop=True)
                self.copy("act", gt.v, bgt.v)
                P.op("dve", lambda e, o=Lr.ap, d0=self.RMASK.ap, d1=sig.ap: e.tensor_tensor_scan(
                    out=o, data0=d0, data1=d1, initial=0.0, op0=ALU.mult, op1=ALU.add),
                    ins=[self.RMASK, sig], outs=[Lr])
                self.act(eL.v, Lr.v, AF.Exp, scale=-CD)
                self.act(eLm.v, Lr.v, AF.Exp, scale=CD)
                self.ts("dve", t1.v, kf.v, pcolv("rwkv_k_k", m, 4), ALU.mult)
                self.act(tmpb.v, t1.v, AF.Square)
                bss = self.bank()
                self.mm(bss.v, self.BONES.v, tmpb.v, start=True, stop=True)
                self.ts("dve", kkn.v, bss.v, 1e-24, ALU.max)
                self.rsqrt_act(kkn.v, kkn.v)
                self.tt("dve", kkn.v, t1.v, kkn.v, ALU.mult)
                self.ts("dve", t1.v, asig.v, -1.0, ALU.add, pcolv("rwkv_k_a", m, 4), ALU.mult)
                self.stt(kp.v, t1.v, 1.0, kf.v, ALU.add, ALU.mult)
                self.tt("dve", bb.v, kkn.v, asig.v, ALU.mult)
                self.stt(tmpb.v, rf.v, pcolv("rwkv_r_k", m, 4), kp.v, ALU.mult, ALU.mult)
                bbn = self.bank()
                self.mm(bbn.v, self.BONES.v, tmpb.v, start=True, stop=True)
                self.tt("dve", bonus.v, bbn.v, vT.v, ALU.mult)
                self.tt("dve", rt.v, rf.v, eL.v, ALU.mult)
                self.tt("dve", kt.v, kp.v, eLm.v, ALU.mult)
                self.tt("dve", bt.v, bb.v, eLm.v, ALU.mult)
                self.tt("dve", t1.v, Lr.v, sig.v, ALU.subtract)
                self.act(eX.v, t1.v, AF.Exp, scale=-CD)
                self.stt(at.v, kkn.v, -1.0, eX.v, ALU.mult, ALU.mult)
                lrc = View([Lr], Lr.ap.rearrange("p (c t) -> p c t", t=64)[:, :, 63:64].to_broadcast([128, 8, 64]))
                self.tt("dve", View([t1], t1.ap.rearrange("p (c t) -> p c t", t=64)),
                        View([Lr], Lr.ap.rearrange("p (c t) -> p c t", t=64)), lrc, ALU.subtract)
                self.act(eX.v, t1.v, AF.Exp, scale=CD)
                self.tt("dve", kh.v, kp.v, eX.v, ALU.mult)
                self.tt("dve", bh.v, bb.v, eX.v, ALU.mult)
                if RLV < 2:
                    continue
                for srcb, dstb in [(kh, khT), (bh, bhT), (vT, vTM)]:
                    bk = self.bank()
                    bkb = View([bk], bk.ap.bitcast(BF16))
                    for c in range(8):
                        self.transpose(bkb[0:64, c * 128:(c + 1) * 128], srcb[:, c * 64:(c + 1) * 64], self.ident_b)
                    self.copy("act", dstb.v, View([bk], bk.ap.bitcast(BF16)[0:64, :].rearrange("p (c j) -> p c j", c=8)))
                def r32v(v):
                    return v

                def chunk_mats(lhs, rhs, dst, mask, eng, r32=False):
                    for par in range(2):
                        pr = slice(par * 64, (par + 1) * 64)
                        bk = self.bank()
                        for c in range(8):
                            cs = slice(c * 64, (c + 1) * 64)
                            self.mm(bk[0:64, cs], lhs[pr, cs], rhs[pr, cs], start=True, stop=True)
                        mk = View([mask], mask.ap.rearrange("p (o t) -> p o t", o=1).to_broadcast([64, 8, 64]))
                        dv = dst[:, par * 8:(par + 1) * 8, :]
                        self.tt(eng, r32v(dv) if r32 else dv,
                                View([bk], bk.ap[0:64, :].rearrange("p (c t) -> p c t", c=8)), mk, ALU.mult)
                M, N, Pm, M2, N2, P2 = (inv[n] for n in ["M", "N", "Pm", "M2", "N2", "P2"])
                if RLV < 3:
                    continue
                chunk_mats(bt, at, M, self.MUs, "dve", r32=True)
                chunk_mats(at, bt, N, self.MLs, "dve", r32=True)
                chunk_mats(kt, at, mats["AkT"], self.MUs, "dve")
                chunk_mats(bt, rt, mats["ArbT"], self.MUi, "dve")
                chunk_mats(kt, rt, mats["ArkT"], self.MUi, "dve")
                if RLV < 4:
                    continue
                i64 = View([self.ident_b], self.ident_b.ap[0:64, 0:64].rearrange("p (o t) -> p o t", o=1).to_broadcast([64, 16, 64]))
                self.tt("dve", r32v(Pm.v), M.v, i64, ALU.add)
                for lvl in range(1, 6):
                    if lvl < 5:
                        for hh in range(2):
                            bk = self.bank()
                            for a8 in range(8):
                                a = hh * 8 + a8
                                self.mm(bk[0:64, a8 * 64:(a8 + 1) * 64], N[:, a, :], M[:, a, :], start=True, stop=True)
                            self.copy("act", r32v(M2[:, hh * 8:(hh + 1) * 8, :]),
                                      View([bk], bk.ap[0:64, :].rearrange("p (c t) -> p c t", c=8)))
                    for hh in range(2):
                        bk = self.bank()
                        for a8 in range(8):
                            a = hh * 8 + a8
                            self.mm(bk[0:64, a8 * 64:(a8 + 1) * 64], M[:, a, :], N[:, a, :], start=True, stop=True)
                        self.copy("act", r32v(N2[:, hh * 8:(hh + 1) * 8, :]),
                                  View([bk], bk.ap[0:64, :].rearrange("p (c t) -> p c t", c=8)))
                    for hh in range(2):
                        bk = self.bank()
                        for a8 in range(8):
                            a = hh * 8 + a8
                            self.mm(bk[0:64, a8 * 64:(a8 + 1) * 64], N2[:, a, :], Pm[:, a, :], start=True, stop=True)
                        self.tt("dve", r32v(P2[:, hh * 8:(hh + 1) * 8, :]),
                                View([bk], bk.ap[0:64, :].rearrange("p (c t) -> p c t", c=8)),
                                Pm[:, hh * 8:(hh + 1) * 8, :], ALU.add)
                    M, M2 = M2, M
                    N, N2 = N2, N
                    Pm, P2 = P2, Pm
                TTb = mats["TTb"]
                self.copy("act", TTb.v, Pm.v)
                AkT, ArbT, ArkT = mats["AkT"], mats["ArbT"], mats["ArkT"]
                if RLV < 5:
                    continue
                ZC = mats["ZC"]
                for par in range(2):
                    pr = slice(par * 64, (par + 1) * 64)
                    bk = self.bank()
                    for c in range(8):
                        self.mm(bk[0:64, c * 64:(c + 1) * 64], AkT[:, par * 8 + c, :], vTM[:, c, pr], start=True, stop=True)
                    self.copy("act", ZC[:, par * 8:(par + 1) * 8, :],
                              View([bk], bk.ap[0:64, :].rearrange("p (c t) -> p c t", c=8)))
                BYH = [self.banks[6], self.banks[7]]
                for c in range(8):
                    cs = slice(c * 64, (c + 1) * 64)
                    zb = [self.banks[(c % 2) * 2], self.banks[(c % 2) * 2 + 1]]
                    for par in range(2):
                        pr = slice(par * 64, (par + 1) * 64)
                        self.mm(zb[par][0:64, 0:64], at[pr, cs], HbB[m][par].v, start=True, stop=True)
                    zin = View(zb, self.ps_t[0:64, (c % 2) * 2:(c % 2) * 2 + 2, 0:64])
                    zc = View([ZC], ZC.ap.rearrange("p (h c) t -> p h c t", h=2)[:, :, c, :])
                    self.tt("dve", View([Zs], Zs.ap.rearrange("p (h t) -> p h t", h=2)), zin, zc, ALU.add)
                    bu = self.banks[4]
                    for par in range(2):
                        pr = slice(par * 64, (par + 1) * 64)
                        a = par * 8 + c
                        self.mm(bu[0:64, pr], TTb[:, a, :], Zs[:, pr], start=True, stop=True)
                    self.copy("act", Us.v, bu[0:64, 0:128])
                    for par in range(2):
                        pr = slice(par * 64, (par + 1) * 64)
                        a = par * 8 + c
                        self.mm(BYH[par][pr, cs], HbB[m][par].v, rt[pr, cs], start=True, stop=True)
                        self.mm(BY[pr, cs], Us[:, pr], ArbT[:, a, :], start=True, stop=False)
                        self.mm(BY[pr, cs], vTM[:, c, pr], ArkT[:, a, :], start=False, stop=True)
                    bhh = self.banks[4]
                    for par in range(2):
                        pr = slice(par * 64, (par + 1) * 64)
                        self.mm(bhh[pr, pr], khT[:, c, pr], vTM[:, c, pr], start=True, stop=False)
                        self.mm(bhh[pr, pr], bhT[:, c, pr], Us[:, pr], start=False, stop=True)
                    for par in range(2):
                        pr = slice(par * 64, (par + 1) * 64)
                        wc = eL[pr, c * 64 + 63:c * 64 + 64]
                        self.stt(HfB[m][par].v, HfB[m][par].v, wc, bhh[pr, pr], ALU.mult, ALU.add)
                        self.copy("act", HbB[m][par].v, HfB[m][par].v)
                if RLV < 6:
                    continue
                self.copy("act", ys.v, BY.v)
                for par in range(2):
                    pr = slice(par * 64, (par + 1) * 64)
                    self.tt("dve", ys[pr, :], ys[pr, :], BYH[par][pr, :], ALU.add)
                self.copy("dve", tmpb.v, ys.v)
                bm = self.bank()
                self.mm(bm.v, self.BONES.v, tmpb.v, start=True, stop=True)
                self.stt(ys.v, bm.v, -1.0 / 64.0, ys.v, ALU.mult, ALU.add)
                self.act(tmpb.v, ys.v, AF.Square)
                bvv = self.bank()
                self.mm(bvv.v, self.BONES.v, tmpb.v, start=True, stop=True)
                self.rsqrt_act(t1.v, bvv.v, scale=1.0 / 64.0, bias=self.eps_tiles[LNX_EPS])
                self.tt("dve", ys.v, ys.v, t1.v, ALU.mult)
                self.ts("dve", ys.v, ys.v, pcolv("rwkv_ln_g", m, 4), ALU.mult, pcolv("rwkv_ln_b", m, 4), ALU.add)
                self.tt("dve", ys.v, ys.v, bonus.v, ALU.add)
                self.tt("dve", yrw[:, m, tb * 512:(tb + 1) * 512], ys.v, gt.v, ALU.mult)
        self.bank_set = list(range(8))
        P.barrier()

    def finish(self):
        P = self.P
        cv = self.carver()
        sq = Buf(cv.b16(4096, "p (c t) -> p c t", c=8), "sq")
        rstd = Buf(cv.f32(512), "rstd")
        yT = Buf(cv.f32(4096, "p (c t) -> p c t", c=8), "yT")
        ostg = [Buf(cv.f32(1024), "ostg%d" % i) for i in range(2)]
        self.OUT = Buf(self.out_ap, "OUT")
        gc = self.pcol["final_norm_g"]
        n = 0
        for tb in range(4):
            if self.final_norm:
                self.rmsnorm(tb, gc, yT.v, sq, rstd)
                src = lambda c, t0: yT[:, c, t0:t0 + 128]
            else:
                src = lambda c, t0, tb=tb: View([self.xb[c][tb]], self.xb[c][tb].ap[:, t0:t0 + 128])
            for t4 in range(4):
                st = ostg[n % 2]
                n += 1
                for h in range(2):
                    bk = self.bank()
                    for c4 in range(4):
                        c = h * 4 + c4
                        self.transpose(bk[:, c4 * 128:(c4 + 1) * 128], src(c, t4 * 128), self.ident_f)
                    self.copy(self.evac_eng(), st[:, h * 512:(h + 1) * 512], bk.v)
                r0 = tb * 512 + t4 * 128
                P.dma(self.OUT[r0:r0 + 128, :], st.v)

    def build(self):
        self.alloc()
        self.make_eps()
        self.setup()
        self.mem_setup()
        self.P.barrier()
        self.moba_setup()
        self.rwkv_setup()
        self.P.barrier()
        for l in range(DEPTH):
            for stg in ["ffn1", "rwkv", "moba", "xattn", "ffn2"]:
                if (l, stg) not in self.stages:
                    continue
                if stg == "ffn1":
                    self.ffn(l, 1)
                elif stg == "ffn2":
                    self.ffn(l, 2)
                elif stg == "xattn":
                    self.xattn(l)
                elif stg == "moba":
                    self.moba(l)
                elif stg == "rwkv":
                    self.rwkv(l)
        self.finish()
        st = self.P.emit()
        return self.nc, st


BUCKET_STARTS = [0, 1, 2, 3, 4, 5, 6, 7, 8, 9, 10, 11, 12, 13, 14, 15, 16, 21, 27, 35, 46, 59, 77, 99, 128, 166,
                 216, 280, 363, 470, 609, 790]


def make_oh():
    oh = np.zeros((32, 2048), np.float32)
    for c in range(512, 2048):
        d = c - 512
        b = 0
        for i, st in enumerate(BUCKET_STARTS):
            if d >= st:
                b = i
        oh[b, c] = 1.0
    return oh


ALL_STAGES = [(l, s) for l in range(DEPTH) for s in ["ffn1", "rwkv", "moba", "xattn", "ffn2"]]


def run(inputs, stages, final_norm=True, cores=8, trace=False, debug=False):
    k = K(stages, final_norm, debug)
    nc, st = k.build()
    in_maps = []
    oh = make_oh()
    for b in range(cores):
        m = {}
        for name, shape in PARAM_SPECS:
            a = np.asarray(inputs[name], dtype=np.float32)
            if name in ("x", "mem"):
                a = a[b]
            m[name] = np.ascontiguousarray(a)
        m["oh_tab"] = oh
        in_maps.append(m)
    res = run_bass_kernel_spmd(nc, in_maps, core_ids=list(range(cores)), trace=trace)
    out = np.stack([r["out"] for r in res.results], axis=0)
    if debug:
        return out, res, st, res.results[0]["dbg"]
    return out, res, st


def kernel(**inputs):
    out, _, _ = run(inputs, ALL_STAGES, True, 8)
    return out.astype(np.float32)
```

```python
import numpy as np
from contextlib import ExitStack
import concourse.bass as bass
import concourse.mybir as mybir
from concourse.bass_utils import run_bass_kernel_spmd

F32 = mybir.dt.float32
BF16 = mybir.dt.bfloat16
F32R = mybir.dt.float32r
AF = mybir.ActivationFunctionType
ALU = mybir.AluOpType
AX = mybir.AxisListType

D = 1024
S = 2048
DEPTH = 2
DFF = 2816
RW = 512
RPROJ = 1792
DPROJ = 3328
MEM = 256
NORM_EPS = 1e-6
LNX_EPS = 64e-5


class Buf:
    def __init__(self, ap, name="", psum=False):
        self.ap = ap
        self.name = name
        self.psum = psum
        self.w = None
        self.r = {}
        self.dsem = None
        self.dcnt = 0

    def __getitem__(self, idx):
        return View([self], self.ap[idx])

    @property
    def v(self):
        return View([self], self.ap)


class View:
    def __init__(self, bufs, ap):
        self.bufs = bufs
        self.ap = ap

    @property
    def v(self):
        return self

    def __getitem__(self, idx):
        return View(self.bufs, self.ap[idx])


def raw(ap):
    return View([], ap)


class Instr:
    __slots__ = ("fn", "deps", "signal", "semval", "dma_inc")

    def __init__(self, fn, deps, dma_inc=None):
        self.fn = fn
        self.deps = deps
        self.signal = False
        self.semval = None
        self.dma_inc = dma_inc


ENGS = ["pe", "act", "dve", "pool", "sp"]


class Prog:
    def __init__(self, nc):
        self.nc = nc
        self.q = {e: [] for e in ENGS}
        self.stack = ExitStack()
        self.esem = {}
        self.n_dsem = 0
        self.all_dma_bufs = []

    def sbuf(self, name, shape, dtype):
        return self.stack.enter_context(self.nc.sbuf_tensor(name, list(shape), dtype))

    def psum(self, name, shape, dtype):
        return self.stack.enter_context(self.nc.psum_tensor(name, list(shape), dtype))

    def _get_dsem(self, buf):
        if buf.dsem is None:
            buf.dsem = self.stack.enter_context(self.nc.semaphore("d%d" % self.n_dsem))
            self.n_dsem += 1
            self.all_dma_bufs.append(buf)
        return buf.dsem

    def _deps(self, eng, ins, outs):
        deps = []
        for v in ins:
            for b in v.bufs:
                if b.w is not None:
                    deps.append(b.w)
                if b.psum:
                    for k, t in b.r.items():
                        if k != eng:
                            deps.append(t)
        for v in outs:
            for b in v.bufs:
                if b.w is not None:
                    deps.append(b.w)
                deps.extend(b.r.values())
        if eng == "pe":
            deps = [d for d in deps if not (d[0] == "E" and d[1] == "pe")]
        return deps

    def _mark(self, tok, key, ins, outs):
        for v in ins:
            for b in v.bufs:
                old = b.r.get(key)
                if old is None or old[2] < tok[2]:
                    b.r[key] = tok
        for v in outs:
            for b in v.bufs:
                b.w = tok
                b.r = {}

    def op(self, eng, fn, ins=(), outs=()):
        ins = [x.v if isinstance(x, Buf) else x for x in ins]
        outs = [x.v if isinstance(x, Buf) else x for x in outs]
        deps = self._deps(eng, ins, outs)
        idx = len(self.q[eng])
        self.q[eng].append(Instr(fn, deps))
        tok = ("E", eng, idx)
        self._mark(tok, eng, ins, outs)
        return tok

    def dma(self, out, in_, queue="sp", **kw):
        out = out.v if isinstance(out, Buf) else out
        in_ = in_.v if isinstance(in_, Buf) else in_
        owner = None
        for v in (out, in_):
            for b in v.bufs:
                owner = b
                break
            if owner is not None:
                break
        assert owner is not None
        sem = self._get_dsem(owner)
        owner.dcnt += 1
        val = 16 * owner.dcnt
        deps = self._deps(queue, [in_], [out])
        oap, iap = out.ap, in_.ap
        self.q[queue].append(Instr(lambda e: e.dma_start(out=oap, in_=iap, **kw), deps, dma_inc=sem))
        tok = ("S", sem, val)
        self._mark(tok, ("S", id(sem)), [in_], [out])
        return tok

    def _last_real(self, e):
        for i in range(len(self.q[e]) - 1, -1, -1):
            ins = self.q[e][i]
            if ins.fn is not None and ins.dma_inc is None:
                return ("E", e, i)
        return None

    def _all_toks(self):
        toks = []
        for e in ENGS:
            t = self._last_real(e)
            if t is not None:
                toks.append(t)
        for b in self.all_dma_bufs:
            toks.append(("S", b.dsem, 16 * b.dcnt))
        return toks

    def barrier(self):
        toks = self._all_toks()
        for e in ENGS:
            deps = [t for t in toks if not (t[0] == "E" and t[1] == e)]
            self.q[e].append(Instr(None, deps))

    def emit(self):
        nc = self.nc
        for e in ENGS:
            self.esem[e] = self.stack.enter_context(nc.semaphore("e_" + e))
        self.q["sp"].append(Instr(None, [t for t in self._all_toks() if not (t[0] == "E" and t[1] == "sp")]))
        for e in ENGS:
            for ins in self.q[e]:
                for d in ins.deps:
                    if d[0] == "E":
                        self.q[d[1]][d[2]].signal = True
        for e in ENGS:
            c = 0
            for ins in self.q[e]:
                if ins.signal:
                    assert ins.fn is not None and ins.dma_inc is None
                    c += 1
                    ins.semval = c
        handles = {"pe": "tensor", "act": "scalar", "dve": "vector", "pool": "gpsimd", "sp": "sync"}
        stats = {}
        with nc.Block() as block:
            for e in ENGS:
                def body(eh, e=e):
                    known = {}
                    nw = 0
                    for ins in self.q[e]:
                        waits = {}
                        for d in ins.deps:
                            if d[0] == "E":
                                sem = self.esem[d[1]]
                                val = self.q[d[1]][d[2]].semval
                            else:
                                sem, val = d[1], d[2]
                            k = id(sem)
                            if known.get(k, 0) >= val:
                                continue
                            if k not in waits or waits[k][1] < val:
                                waits[k] = (sem, val)
                        wl = list(waits.values())
                        for k, (sem, val) in waits.items():
                            known[k] = val
                        nw += len(wl)
                        if ins.fn is None:
                            for sem, val in wl:
                                eh.wait_ge(sem, val)
                            continue
                        for sem, val in wl[:-1]:
                            eh.wait_ge(sem, val)
                        bi = ins.fn(eh)
                        if wl:
                            bi._wait_ge(wl[-1][0], wl[-1][1])
                        if ins.dma_inc is not None:
                            bi.then_inc(ins.dma_inc, 16)
                        elif ins.signal:
                            bi.then_inc(self.esem[e], 1)
                    stats[e] = (len(self.q[e]), nw)
                getattr(block, handles[e])(body)
        self.stats = stats
        return stats


PARAM_SPECS = [
    ("x", [S, D]), ("mem", [MEM, D]), ("rel_bias", [8, 32]), ("final_norm_g", [D]),
    ("ffn1_norm_g", [DEPTH, D]), ("ffn1_w_in", [DEPTH, D, 2 * DFF]), ("ffn1_w_out", [DEPTH, DFF, D]),
    ("mix_norm_g", [DEPTH, D]), ("w_mix_in", [DEPTH, D, DPROJ]), ("w_mix_out", [DEPTH, D, D]),
    ("rwkv_mu", [DEPTH, RPROJ]), ("rwkv_w0", [DEPTH, RW]), ("rwkv_w_up", [DEPTH, 64, RW]),
    ("rwkv_a0", [DEPTH, RW]), ("rwkv_a_up", [DEPTH, 64, RW]), ("rwkv_g_up", [DEPTH, 128, RW]),
    ("rwkv_k_k", [DEPTH, RW]), ("rwkv_k_a", [DEPTH, RW]), ("rwkv_r_k", [DEPTH, 8, 64]),
    ("rwkv_ln_g", [DEPTH, RW]), ("rwkv_ln_b", [DEPTH, RW]),
    ("xattn_norm_g", [DEPTH, D]), ("mem_norm_g", [DEPTH, D]),
    ("xattn_w_q", [DEPTH, D, D]), ("xattn_w_kv", [DEPTH, D, 2 * D]), ("xattn_w_o", [DEPTH, D, D]),
    ("ffn2_norm_g", [DEPTH, D]), ("ffn2_w_in", [DEPTH, D, 2 * DFF]), ("ffn2_w_out", [DEPTH, DFF, D]),
]

GAIN_NAMES = ["ffn1_norm_g", "mix_norm_g", "xattn_norm_g", "mem_norm_g", "ffn2_norm_g"]


class Carver:
    def __init__(self, big, nwords):
        self.big = big
        self.n = nwords
        self.o = 0

    def f32(self, n, pat=None, **kw):
        ap = self.big[:, self.o:self.o + n]
        self.o += n
        assert self.o <= self.n, "scratch overflow"
        return ap.rearrange(pat, **kw) if pat else ap

    def b16(self, n, pat=None, **kw):
        w = (n + 1) // 2
        ap = self.big[:, self.o:self.o + w].bitcast(BF16)
        self.o += w
        assert self.o <= self.n, "scratch overflow"
        return ap.rearrange(pat, **kw) if pat else ap


class K:
    def __init__(self, stages, final_norm=True, debug=False):
        self.stages = stages
        self.final_norm = final_norm
        nc = bass.Bass("TRN2", target_bir_lowering=False)
        self.nc = nc
        self.P = Prog(nc)
        self.dram = {}
        for name, shape in PARAM_SPECS:
            self.dram[name] = nc.dram_tensor(name, shape, F32, kind="ExternalInput").ap()
        self.dram["oh_tab"] = nc.dram_tensor("oh_tab", [32, 2048], F32, kind="ExternalInput").ap()
        self.ebrep_t = nc.dram_tensor("ebrep", [8, 128, 2048], BF16, kind="Internal")
        self.out_ap = nc.dram_tensor("out", [S, D], F32, kind="ExternalOutput").ap()
        self.debug = debug
        if debug:
            self.dbg_ap = nc.dram_tensor("dbg", [128, 16 * 512], F32, kind="ExternalOutput").ap()
            self.DBG = Buf(self.dbg_ap, "DBG")
            self.dbg_n = 0
        self.pb_i = 0
        self.bank_set = list(range(8))
        self.ring_i = 0
        self.ev_i = 0

    def alloc(self):
        P = self.P
        self.xT_t = P.sbuf("xT", [128, 8, S], F32)
        self.xb = [[Buf(self.xT_t[:, c, tb * 512:(tb + 1) * 512], "x%d_%d" % (c, tb)) for tb in range(4)]
                   for c in range(8)]
        self.ident_f = Buf(P.sbuf("ident_f", [128, 128], F32)[:], "ident_f")
        self.ident_b = Buf(P.sbuf("ident_b", [128, 128], BF16)[:], "ident_b")
        self.ones_b = Buf(P.sbuf("ones_b", [128, 128], BF16)[:], "ones_b")
        self.pt_stage = Buf(P.sbuf("pt_stage", [128, 128], F32)[:], "pt_stage")
        self.PT = Buf(P.sbuf("PT", [128, 256], F32)[:], "PT")
        self.ps_t = P.psum("ps", [128, 8, 512], F32)
        self.banks = [Buf(self.ps_t[:, i, :], "bank%d" % i, psum=True) for i in range(8)]
        self.NSLOT = 3
        self.ring_t = P.sbuf("ring", [128, self.NSLOT, 22 * 128], BF16)
        self.ring = [Buf(self.ring_t[:, i, :], "ring%d" % i) for i in range(self.NSLOT)]
        self.BIGW = 27 * 1024
        self.big = P.sbuf("big", [128, self.BIGW], F32)

    def carver(self):
        return Carver(self.big, self.BIGW)

    def dump(self, view, n, name=""):
        if not self.debug:
            return
        if not hasattr(self, "dbg_stg"):
            self.dbg_stg = [Buf(self.P.sbuf("dbgs%d" % i, [128, 512], F32)[:], "dbgs%d" % i) for i in range(2)]
        st = self.dbg_stg[self.dbg_n % 2]
        np_ = view.ap.shape[0]
        self.P.op("pool", lambda e: e.memset(st.ap, 0.0), outs=[st.v])
        bp = 0
        self.copy("dve", st[bp:bp + np_, 0:n], view)
        self.P.dma(self.DBG[:, self.dbg_n * 512:(self.dbg_n + 1) * 512], st.v)
        print("dump slot", self.dbg_n, name)
        self.dbg_n += 1

    def act(self, out, in_, func, bias=None, scale=None):
        oap, iap = out.ap, in_.ap
        ins = [in_]
        kw = {}
        if bias is not None:
            if isinstance(bias, (View, Buf)):
                ins.append(bias)
                kw["bias"] = bias.ap
            else:
                kw["bias"] = bias
        if scale is not None:
            if isinstance(scale, (View, Buf)):
                ins.append(scale)
                kw["scale"] = scale.ap
            else:
                kw["scale"] = scale
        self.P.op("act", lambda e: e.activation(out=oap, in_=iap, func=func, **kw), ins=ins, outs=[out])

    def tt(self, eng, out, in0, in1, op):
        oap, a, b = out.ap, in0.ap, in1.ap
        self.P.op(eng, lambda e: e.tensor_tensor(out=oap, in0=a, in1=b, op=op), ins=[in0, in1], outs=[out])

    def ts(self, eng, out, in0, s1, op0, s2=None, op1=None):
        oap, a = out.ap, in0.ap
        ins = [in0]
        v1 = s1
        if isinstance(s1, (View, Buf)):
            ins.append(s1)
            v1 = s1.ap
        v2 = s2
        if isinstance(s2, (View, Buf)):
            ins.append(s2)
            v2 = s2.ap
        if op1 is None:
            self.P.op(eng, lambda e: e.tensor_scalar(out=oap, in0=a, scalar1=v1, scalar2=None, op0=op0), ins=ins, outs=[out])
        else:
            self.P.op(eng, lambda e: e.tensor_scalar(out=oap, in0=a, scalar1=v1, scalar2=v2, op0=op0, op1=op1),
                      ins=ins, outs=[out])

    def stt(self, out, in0, scalar, in1, op0, op1):
        oap, a, b = out.ap, in0.ap, in1.ap
        ins = [in0, in1]
        sv = scalar
        if isinstance(scalar, (View, Buf)):
            ins.append(scalar)
            sv = scalar.ap
        self.P.op("dve", lambda e: e.scalar_tensor_tensor(out=oap, in0=a, scalar=sv, in1=b, op0=op0, op1=op1),
                  ins=ins, outs=[out])

    def rsqrt_act(self, out, in_, scale=1.0, bias=None):
        self.act(out, in_, AF.Ln, bias=bias, scale=scale)
        self.act(out, out, AF.Exp, scale=-0.5)

    def recip_act(self, out, in_):
        self.act(out, in_, AF.Ln)
        self.act(out, out, AF.Exp, scale=-1.0)

    def recip(self, out, in_):
        oap, iap = out.ap, in_.ap
        self.P.op("dve", lambda e: e.reciprocal(out=oap, in_=iap), ins=[in_], outs=[out])

    def memset(self, eng, out, val):
        oap = out.ap
        self.P.op(eng, lambda e: e.memset(oap, val), outs=[out])

    def bank(self):
        bs = self.bank_set
        b = self.banks[bs[self.pb_i % len(bs)]]
        self.pb_i += 1
        return b

    def evac_eng(self):
        self.ev_i += 1
        return "act" if self.ev_i % 2 == 0 else "dve"

    def copy(self, eng, out, in_):
        oap, iap = out.ap, in_.ap
        if eng == "act":
            self.P.op("act", lambda e: e.activation(out=oap, in_=iap, func=AF.Copy), ins=[in_], outs=[out])
        else:
            self.P.op(eng, lambda e: e.tensor_copy(out=oap, in_=iap), ins=[in_], outs=[out])

    def mm(self, out, lhsT, rhs, start, stop, r32=False):
        oap, lap, rap = out.ap, lhsT.ap, rhs.ap
        if r32:
            lap, rap = lap.bitcast(F32R), rap.bitcast(F32R)
        self.P.op("pe", lambda e: e.matmul(oap, lhsT=lap, rhs=rap, start=start, stop=stop),
                  ins=[lhsT, rhs], outs=[out])

    def transpose(self, out, in_, ident):
        oap, iap, idap = out.ap, in_.ap, ident.ap
        self.P.op("pe", lambda e: e.transpose(out=oap, in_=iap, identity=idap), ins=[in_, ident], outs=[out])

    def load_w(self, w_ap, kc, ncols):
        slot = self.ring[self.ring_i % self.NSLOT]
        self.ring_i += 1
        dst = View([slot], slot.ap[:, 0:kc * ncols].rearrange("p (k n) -> p k n", k=kc))
        src = raw(w_ap.rearrange("(k p) n -> p k n", p=128))
        self.P.dma(dst, src, queue="pool")
        return dst

    def xview(self, cs, tb):
        bufs = [self.xb[c][tb] for c in range(cs.start, cs.stop)]
        return View(bufs, self.xT_t[:, cs, tb * 512:(tb + 1) * 512])

    def setup(self):
        P = self.P
        idf, idb, ones = self.ident_f, self.ident_b, self.ones_b
        P.op("pool", lambda e: e.memset(idf.ap, 0.0), outs=[idf.v])
        P.op("pool", lambda e: e.affine_select(out=idf.ap, in_=idf.ap, compare_op=ALU.not_equal, fill=1.0,
                                               base=0, pattern=[[-1, 128]], channel_multiplier=1),
             ins=[idf.v], outs=[idf.v])
        self.copy("dve", idb.v, idf.v)
        P.op("pool", lambda e: e.memset(ones.ap, 1.0), outs=[ones.v])
        self.pcol = {}
        col = 0
        groups = []
        rows = 0
        cur = []
        plist = [(n, DEPTH * 8) for n in GAIN_NAMES] + [("final_norm_g", 8)]
        plist += [("rwkv_mu", DEPTH * 14)] + [(n, DEPTH * 4) for n in
                                              ["rwkv_w0", "rwkv_a0", "rwkv_k_k", "rwkv_k_a", "rwkv_r_k",
                                               "rwkv_ln_g", "rwkv_ln_b"]]
        for name, nrow in plist:
            if rows + nrow > 128:
                groups.append(cur)
                cur = []
                rows = 0
            cur.append((name, nrow, rows))
            rows += nrow
        groups.append(cur)
        for grp in groups:
            st = self.pt_stage
            P.op("pool", lambda e: e.memset(st.ap, 0.0), outs=[st.v])
            tot = 0
            for name, nrow, r0 in grp:
                ap = self.dram[name]
                if name == "final_norm_g":
                    src = ap.rearrange("(c p) -> c p", p=128)
                elif name == "rwkv_r_k":
                    src = ap.rearrange("l (m two) d -> (l m) (two d)", two=2)
                else:
                    src = ap.rearrange("l (c p) -> (l c) p", p=128)
                P.dma(st[r0:r0 + nrow, :], raw(src))
                self.pcol[name] = col + r0
                tot = r0 + nrow
            bk = self.bank()
            self.transpose(bk[:, 0:128], st.v, self.ident_f)
            self.copy("dve", self.PT[:, col:col + tot], bk[:, 0:tot])
            col += tot
        assert col <= 256
        cv = self.carver()
        stg = [Buf(cv.f32(1024), "xstg%d" % i) for i in range(3)]
        for tt in range(16):
            st = stg[tt % 3]
            P.dma(st.v, raw(self.dram["x"][tt * 128:(tt + 1) * 128, :]))
            tb = tt // 4
            t0 = (tt % 4) * 128
            for h in range(2):
                bk = self.bank()
                for c4 in range(4):
                    c = h * 4 + c4
                    self.transpose(bk[:, c4 * 128:(c4 + 1) * 128], st[:, c * 128:(c + 1) * 128], self.ident_f)
                cs = slice(h * 4, h * 4 + 4)
                dst = View([self.xb[c][tb] for c in range(cs.start, cs.stop)],
                           self.xT_t[:, cs, tb * 512 + t0: tb * 512 + t0 + 128])
                self.copy(self.evac_eng(), dst, View([bk], bk.ap.rearrange("p (c t) -> p c t", c=4)))
        P.barrier()

    def gcol(self, name, l):
        return self.pcol[name] + l * 8

    def rmsnorm(self, tb, gc, hT_view, sq, rstd, out_dtype_bf16=True):
        P = self.P
        xv = self.xview(slice(0, 8), tb)
        xap, sqap = xv.ap, sq.ap
        P.op("act", lambda e: e.activation(out=sqap, in_=xap, func=AF.Square), ins=[xv], outs=[sq.v])
        bk = self.bank()
        for c in range(8):
            self.mm(bk.v, self.ones_b.v, sq[:, c, :], start=(c == 0), stop=(c == 7))
        rap = rstd.ap
        self.rsqrt_act(rstd.v, bk.v, scale=1.0 / D, bias=self.eps_tiles[NORM_EPS])
        for c in range(8):
            xin = self.xb[c][tb]
            o = hT_view[:, c, :]
            oap, iap, gap = o.ap, xin.ap, self.PT.ap[:, gc + c: gc + c + 1]
            eng = "dve"
            P.op(eng, lambda e, oap=oap, iap=iap, gap=gap: e.scalar_tensor_tensor(
                out=oap, in0=iap, scalar=gap, in1=rap, op0=ALU.mult, op1=ALU.mult),
                ins=[xin.v, self.PT.v, rstd.v], outs=[o])

    def eps_ap(self, val):
        return self.eps_tiles[val].ap

    def make_eps(self):
        self.eps_tiles = {}
        for i, val in enumerate([NORM_EPS, LNX_EPS]):
            b = Buf(self.P.sbuf("eps%d" % i, [128, 1], F32)[:], "eps%d" % i)
            self.P.op("pool", lambda e, b=b, val=val: e.memset(b.ap, val), outs=[b.v])
            self.eps_tiles[val] = b

    def ffn(self, l, which):
        P = self.P
        w_in = self.dram["ffn%d_w_in" % which][l]
        w_out = self.dram["ffn%d_w_out" % which][l]
        gc = self.gcol("ffn%d_norm_g" % which, l)
        cv = self.carver()
        hT = [Buf(cv.b16(4096, "p (c t) -> p c t", c=8), "hT%d" % i) for i in range(2)]
        aT = [[Buf(cv.b16(512), "aT%d_%d" % (j, i)) for i in range(2)] for j in range(22)]
        sq = Buf(cv.b16(4096, "p (c t) -> p c t", c=8), "sq")
        rstd = Buf(cv.f32(512), "rstd")
        sg = [Buf(cv.f32(512), "sg%d" % i) for i in range(4)]
        for half in range(2):
            for i in range(2):
                self.rmsnorm(2 * half + i, gc, hT[i].v, sq, rstd)
            for j in range(22):
                wg = self.load_w(w_in[:, j * 128:(j + 1) * 128], 8, 128)
                wu = self.load_w(w_in[:, DFF + j * 128:DFF + (j + 1) * 128], 8, 128)
                for i in range(2):
                    pg = self.bank()
                    pu = self.bank()
                    for k in range(8):
                        self.mm(pg.v, wg[:, k, :], hT[i][:, k, :], start=(k == 0), stop=(k == 7))
                    for k in range(8):
                        self.mm(pu.v, wu[:, k, :], hT[i][:, k, :], start=(k == 0), stop=(k == 7))
                    s = sg[(j * 2 + i) % 4]
                    sap, pgap, puap, aap = s.ap, pg.ap, pu.ap, aT[j][i].ap
                    P.op("act", lambda e, sap=sap, pgap=pgap: e.activation(out=sap, in_=pgap, func=AF.Silu),
                         ins=[pg.v], outs=[s.v])
                    P.op("dve", lambda e, sap=sap, puap=puap, aap=aap: e.tensor_tensor(
                        out=aap, in0=sap, in1=puap, op=ALU.mult), ins=[s.v, pu.v], outs=[aT[j][i].v])
            for c in range(8):
                wo = self.load_w(w_out[:, c * 128:(c + 1) * 128], 22, 128)
                for i in range(2):
                    po = self.bank()
                    for k in range(22):
                        self.mm(po.v, wo[:, k, :], aT[k][i].v, start=(k == 0), stop=(k == 21))
                    xb = self.xb[c][2 * half + i]
                    xap, poap = xb.ap, po.ap
                    P.op("dve", lambda e, xap=xap, poap=poap: e.scalar_tensor_tensor(
                        out=xap, in0=poap, scalar=0.5, in1=xap, op0=ALU.mult, op1=ALU.add),
                        ins=[po.v, xb.v], outs=[xb.v])
        P.barrier()

    def mem_setup(self):
        P = self.P
        self.memT = Buf(P.sbuf("memT", [128, 8, MEM], F32)[:], "memT")
        cv = self.carver()
        stg = [Buf(cv.f32(1024), "mstg%d" % i) for i in range(2)]
        for mt in range(2):
            P.dma(stg[mt].v, raw(self.dram["mem"][mt * 128:(mt + 1) * 128, :]))
            for h in range(2):
                bk = self.bank()
                for c4 in range(4):
                    c = h * 4 + c4
                    self.transpose(bk[:, c4 * 128:(c4 + 1) * 128], stg[mt][:, c * 128:(c + 1) * 128], self.ident_f)
                self.copy(self.evac_eng(), self.memT[:, h * 4:h * 4 + 4, mt * 128:(mt + 1) * 128],
                          View([bk], bk.ap.rearrange("p (c t) -> p c t", c=4)))

    def xattn(self, l):
        P = self.P
        cv = self.carver()
        a16 = cv.b16
        hT = Buf(a16(4096, "p (c t) -> p c t", c=8), "hT")
        sq = Buf(a16(4096, "p (c t) -> p c t", c=8), "sq")
        qT = Buf(a16(4096, "p (c t) -> p c t", c=8), "qT")
        oT = Buf(a16(4096, "p (c t) -> p c t", c=8), "oT")
        mnT = Buf(a16(2048, "p (c t) -> p c t", c=8), "mnT")
        kT = Buf(a16(2048, "p (c t) -> p c t", c=8), "kT")
        vtm = Buf(a16(2048, "p (m n) -> p m n", m=2), "vtm")
        pT = [Buf(a16(512), "pT%d" % i) for i in range(4)]
        rstd = Buf(cv.f32(512), "rstd")
        rden = [Buf(cv.f32(512), "rden%d" % i) for i in range(2)]
        msq = Buf(a16(2048, "p (c t) -> p c t", c=8), "msq")
        mrs = Buf(cv.f32(256), "mrs")
        w_q = self.dram["xattn_w_q"][l]
        w_kv = self.dram["xattn_w_kv"][l]
        w_o = self.dram["xattn_w_o"][l]
        self.dump(self.memT[:, 0, :], 256, "memT0")
        self.dump(self.memT[:, 7, :], 256, "memT7")
        gm = self.gcol("mem_norm_g", l)
        mT = self.memT
        P.op("act", lambda e: e.activation(out=msq.ap, in_=mT.ap, func=AF.Square), ins=[mT.v], outs=[msq.v])
        bk = self.bank()
        for c in range(8):
            self.mm(bk[:, 0:MEM], self.ones_b.v, msq[:, c, :], start=(c == 0), stop=(c == 7))
        self.rsqrt_act(mrs.v, bk[:, 0:MEM], scale=1.0 / D, bias=self.eps_tiles[NORM_EPS])
        for c in range(8):
            oap, iap, gap = mnT.ap[:, c, :], mT.ap[:, c, :], self.PT.ap[:, gm + c:gm + c + 1]
            P.op("dve", lambda e, oap=oap, iap=iap, gap=gap: e.scalar_tensor_tensor(
                out=oap, in0=iap, scalar=gap, in1=mrs.ap, op0=ALU.mult, op1=ALU.mult),
                ins=[mT.v, self.PT.v, mrs.v], outs=[mnT.v])
        self.dump(mrs.v, 256, "mrs")
        self.dump(mnT[:, 0, :], 256, "mnT0")
        for c2 in range(4):
            wk = self.load_w(w_kv[:, c2 * 256:(c2 + 1) * 256], 8, 256)
            for cc in range(2):
                c = c2 * 2 + cc
                bk = self.bank()
                for k in range(8):
                    self.mm(bk[:, 0:MEM], wk[:, k, cc * 128:(cc + 1) * 128], mnT[:, k, :], start=(k == 0), stop=(k == 7))
                self.copy(self.evac_eng(), kT[:, c, :], bk[:, 0:MEM])
        for n4 in range(4):
            wv = self.load_w(w_kv[:, D + n4 * 256:D + (n4 + 1) * 256], 8, 256)
            for mt in range(2):
                bk = self.bank()
                for k in range(8):
                    self.mm(bk[:, 0:256], mnT[:, k, mt * 128:(mt + 1) * 128], wv[:, k, :], start=(k == 0), stop=(k == 7))
                self.copy(self.evac_eng(), vtm[:, mt, n4 * 256:(n4 + 1) * 256], bk[:, 0:256])
        self.dump(kT[:, 0, :], 256, "kT0")
        self.dump(vtm[:, 0, 0:512], 512, "vtm0")
        gx = self.gcol("xattn_norm_g", l)
        for tb in range(4):
            self.rmsnorm(tb, gx, hT.v, sq, rstd)
            for c2 in range(4):
                wq = self.load_w(w_q[:, c2 * 256:(c2 + 1) * 256], 8, 256)
                for cc in range(2):
                    c = c2 * 2 + cc
                    bk = self.bank()
                    for k in range(8):
                        self.mm(bk.v, wq[:, k, cc * 128:(cc + 1) * 128], hT[:, k, :], start=(k == 0), stop=(k == 7))
                    self.copy(self.evac_eng(), qT[:, c, :], bk.v)
            def xscores(h):
                bks = []
                for mt in range(2):
                    bk = self.bank()
                    for dc in range(2):
                        c = 2 * h + dc
                        self.mm(bk.v, kT[:, c, mt * 128:(mt + 1) * 128], qT[:, c, :], start=(dc == 0), stop=(dc == 1))
                    bks.append(bk)
                return bks
            nxt = xscores(0)
            for h in range(4):
                bks = nxt
                pts = []
                for mt in range(2):
                    p = pT[(h * 2 + mt) % 4]
                    self.act(p.v, bks[mt].v, AF.Exp, scale=1.0 / 16.0)
                    pts.append(p)
                if h + 1 < 4:
                    nxt = xscores(h + 1)
                bden = self.bank()
                for mt in range(2):
                    self.mm(bden.v, self.ones_b.v, pts[mt].v, start=(mt == 0), stop=(mt == 1))
                rd = rden[h % 2]
                self.recip_act(rd.v, bden.v)
                for dc in range(2):
                    c = 2 * h + dc
                    bo = self.bank()
                    for mt in range(2):
                        self.mm(bo.v, vtm[:, mt, c * 128:(c + 1) * 128], pts[mt].v, start=(mt == 0), stop=(mt == 1))
                    self.tt("dve", oT[:, c, :], bo.v, rd.v, ALU.mult)
            if tb == 0:
                self.dump(qT[:, 0, :], 512, "qT0")
                self.dump(pT[0].v, 512, "pT0")
                self.dump(rden[0].v, 512, "rden0")
                self.dump(oT[:, 0, :], 512, "oT0")
            for c2 in range(4):
                wo = self.load_w(w_o[:, c2 * 256:(c2 + 1) * 256], 8, 256)
                for cc in range(2):
                    c = c2 * 2 + cc
                    bk = self.bank()
                    for k in range(8):
                        self.mm(bk.v, wo[:, k, cc * 128:(cc + 1) * 128], oT[:, k, :], start=(k == 0), stop=(k == 7))
                    xb = self.xb[c][tb]
                    xap, bap = xb.ap, bk.ap
                    P.op("dve", lambda e, xap=xap, bap=bap: e.tensor_tensor(out=xap, in0=bap, in1=xap, op=ALU.add),
                         ins=[bk.v, xb.v], outs=[xb.v])
        P.barrier()

    def moba_setup(self):
        P = self.P
        cv = self.carver()
        rbs = Buf(cv.f32(32)[0:8, :], "rbs")
        erbT = Buf(cv.f32(8)[0:32, :], "erbT")
        oh = Buf(cv.f32(2048)[0:32, :], "oh")
        ebrow = Buf(cv.b16(2048)[0:8, :], "ebrow")
        self.EBREP = Buf(self.ebrep_t.ap(), "EBREP")
        P.dma(rbs.v, raw(self.dram["rel_bias"]))
        P.dma(oh.v, raw(self.dram["oh_tab"]))
        self.act(rbs.v, rbs.v, AF.Exp)
        bk = self.bank()
        self.transpose(bk[0:32, 0:8], rbs.v, self.ident_f[0:8, 0:8])
        self.copy("dve", erbT.v, bk[0:32, 0:8])
        for c4 in range(4):
            bk = self.bank()
            self.mm(bk[0:8, :], erbT.v, oh[:, c4 * 512:(c4 + 1) * 512], start=True, stop=True)
            self.copy("dve", ebrow[:, c4 * 512:(c4 + 1) * 512], bk[0:8, :])
        srcb = View([ebrow], ebrow.ap.rearrange("h (o c) -> h o c", o=1).to_broadcast([8, 128, 2048]))
        P.dma(self.EBREP.v, srcb)
        self.RB31 = Buf(P.sbuf("RB31", [128, 8], F32)[:], "RB31")
        P.dma(self.RB31.v, raw(self.dram["rel_bias"][:, 31:32].rearrange("h o -> o h").partition_broadcast(128)),
              allow_slow_non_contiguous=True)
        self.IND128 = Buf(P.sbuf("IND", [128, 2048], BF16)[:], "IND128")
        self.IND = View([self.IND128], self.IND128.ap[0:8, :])
        ind = self.IND
        self.memset("pool", self.IND128.v, 0.0)
        self.memset("pool", ind.v, 1.0)
        P.op("pool", lambda e: e.affine_select(out=ind.ap, in_=ind.ap, compare_op=ALU.is_ge, fill=0.0, base=0,
                                               pattern=[[1, 2048]], channel_multiplier=-256), ins=[ind], outs=[ind])
        P.op("pool", lambda e: e.affine_select(out=ind.ap, in_=ind.ap, compare_op=ALU.is_ge, fill=0.0, base=255,
                                               pattern=[[-1, 2048]], channel_multiplier=256), ins=[ind], outs=[ind])
        P.dma(self.IND128[64:72, :], self.IND128[0:8, :])
        self.ELIG = Buf(P.sbuf("ELIG", [128, 8, 8], F32)[:], "ELIG")
        self.OWN = Buf(P.sbuf("OWN", [128, 8, 8], F32)[:], "OWN")
        self.memset("pool", self.ELIG.v, 0.0)
        self.memset("pool", self.OWN.v, 0.0)
        for qb in range(8):
            self.memset("pool", self.ELIG[:, qb, qb:8], -1e30)
            self.memset("pool", self.OWN[:, qb, qb:qb + 1], 1.0)

    def moba(self, l):
        P = self.P
        cv = self.carver()
        yrw = Buf(cv.b16(8192, "p (c t) -> p c t", c=4), "yrw")
        KT = Buf(cv.b16(8192, "p (c t) -> p c t", c=4), "KT")
        VTM = Buf(cv.b16(8192, "p (k n) -> p k n", k=16), "VTM")
        hT = Buf(cv.b16(4096, "p (c t) -> p c t", c=8), "hT")
        sq = Buf(cv.b16(4096, "p (c t) -> p c t", c=8), "sq")
        rstd = Buf(cv.f32(512), "rstd")
        qs = Buf(cv.b16(1024, "p (a t) -> p a t", a=2), "qs")
        qf = Buf(cv.f32(512), "qf")
        ymo = Buf(cv.b16(2048, "p (c t) -> p c t", c=4), "ymo")
        bands = [Buf(cv.b16(1920), "band%d" % i) for i in range(2)]
        NE, NP = 3, 5
        eT = [Buf(cv.f32(512), "eT%d" % i) for i in range(NE)]
        pT = [Buf(cv.b16(512), "pT%d" % i) for i in range(NP)]
        rden = [Buf(cv.f32(512), "rden%d" % i) for i in range(2)]
        mbT = Buf(cv.b16(1024, "p (a t) -> p a t", a=2), "mbT")
        selp = Buf(cv.f32(4 * 72, "p (a n) -> p a n", a=4), "selp")
        kmT = Buf(cv.f32(64, "p (c n) -> p c n", c=4), "kmT")
        gm = Buf(cv.f32(64, "p (a n) -> p a n", a=8), "gm")
        top8 = Buf(cv.f32(64, "p (a n) -> p a n", a=8), "top8")
        sel = Buf(cv.f32(64, "p (a n) -> p a n", a=8), "sel")
        w_in = self.dram["w_mix_in"][l]
        w_out = self.dram["w_mix_out"][l]
        gc = self.gcol("mix_norm_g", l)
        self.memset("pool", kmT.v, 0.0)
        self.memset("pool", selp.v, 0.0)
        self.memset("pool", qs.v, 0.0)
        self.memset("pool", mbT.v, 0.0)
        if (l, "rwkv") not in self.stages:
            self.memset("pool", yrw.v, 0.0)
        import os
        if int(os.environ.get("MOBA_LV", "9")) < 3:
            self.memset("pool", ymo.v, 0.0)
        self.bank_set = [0, 1, 2, 3]
        acc_i = 0
        band_i = 0
        e_i = 0
        p_i = 0
        QOFF = RPROJ
        for tb in range(4):
            self.rmsnorm(tb, gc, hT.v, sq, rstd)
            for m in range(4):
                wq = self.load_w(w_in[:, QOFF + m * 128:QOFF + (m + 1) * 128], 8, 128)
                wk = self.load_w(w_in[:, QOFF + 512 + m * 128:QOFF + 512 + (m + 1) * 128], 8, 128)
                wv = self.load_w(w_in[:, QOFF + 1024 + m * 128:QOFF + 1024 + (m + 1) * 128], 8, 128)
                import os
                SK = os.environ.get("MOBA_SKIP", "").split(",")
                bq = self.bank()
                for k in range(8):
                    self.mm(bq.v, wq[:, k, :], hT[:, k, :], start=(k == 0), stop=(k == 7))
                if "qf" not in SK:
                    self.copy("act", qf.v, bq.v)
                for par in range(2):
                    pr = slice(par * 64, (par + 1) * 64)
                    self.ts("dve", qs[pr, par, :], bq[pr, :], 0.125, ALU.mult)
                bkk = self.bank()
                for k in range(8):
                    self.mm(bkk.v, wk[:, k, :], hT[:, k, :], start=(k == 0), stop=(k == 7))
                self.copy("act", KT[:, m, tb * 512:(tb + 1) * 512], bkk.v)
                for par in range(2):
                    pr = slice(par * 64, (par + 1) * 64)
                    kin = View([bkk], bkk.ap[pr, :].rearrange("p (a t) -> p a t", a=2))
                    kout = kmT[pr, m, par * 8 + 2 * tb:par * 8 + 2 * tb + 2]
                    P.op("dve", lambda e, o=kout.ap, i=kin.ap: e.tensor_reduce(out=o, in_=i, axis=AX.X, op=ALU.add),
                         ins=[kin], outs=[kout])
                bv = self.bank()
                if "v" not in SK:
                    for tt in range(4):
                        for k in range(8):
                            self.mm(bv[:, tt * 128:(tt + 1) * 128], hT[:, k, tt * 128:(tt + 1) * 128], wv[:, k, :],
                                    start=(k == 0), stop=(k == 7))
                    self.copy("dve", VTM[:, tb * 4:tb * 4 + 4, m * 128:(m + 1) * 128],
                              View([bv], bv.ap.rearrange("p (a n) -> p a n", a=4)))
                import os
                LV = int(os.environ.get("MOBA_LV", "9"))
                if LV < 2:
                    continue
                bg = self.bank()
                for tt in range(4):
                    self.mm(bg[:, tt * 16:(tt + 1) * 16], qf[:, tt * 128:(tt + 1) * 128], kmT[:, m, :],
                            start=True, stop=True)
                for qq in range(2):
                    qb = 2 * tb + qq
                    el = View([self.ELIG], self.ELIG.ap[:, qb:qb + 1, :].to_broadcast([128, 4, 8]))
                    self.tt("dve", gm[:, qq * 4:(qq + 1) * 4, :],
                            View([bg], bg.ap[:, qq * 32:(qq + 1) * 32].rearrange("p (a n) -> p a n", a=4)), el, ALU.add)
                GLV = int(os.environ.get("GATE_LV", "9"))
                if GLV < 2:
                    continue
                for a in range(8):
                    P.op("dve", lambda e, o=top8.ap[:, a, :], i=gm.ap[:, a, :]: e.max(out=o, in_=i),
                         ins=[gm.v], outs=[top8.v])
                thr = View([top8], top8.ap[:, :, 2:3].to_broadcast([128, 8, 8]))
                self.tt("dve", sel.v, gm.v, thr, ALU.is_ge)
                for qq in range(2):
                    qb = 2 * tb + qq
                    ow = View([self.OWN], self.OWN.ap[:, qb:qb + 1, :].to_broadcast([128, 4, 8]))
                    self.tt("dve", sel[:, qq * 4:(qq + 1) * 4, :], sel[:, qq * 4:(qq + 1) * 4, :], ow, ALU.max)
                self.ts("dve", sel.v, sel.v, 30000.0, ALU.mult, -30000.0, ALU.add)
                if GLV < 3:
                    continue
                bt = self.bank()
                for tt in range(4):
                    self.transpose(bt[0:8, tt * 128:(tt + 1) * 128], sel[:, tt * 2, :], self.ident_f)
                self.copy("act", mbT[0:8, 0, :], bt[0:8, :])
                self.copy("dve", selp[:, :, 64:72], View([sel], sel.ap.rearrange("p (t two) n -> p t two n", two=2)[:, :, 1, :]))
                bt = self.bank()
                for tt in range(4):
                    self.transpose(bt[0:72, tt * 128:(tt + 1) * 128], selp[:, tt, :], self.ident_f)
                self.copy("act", mbT[64:72, 1, :], bt[64:72, :])
                if LV < 3:
                    continue
                nkt = 4 * (tb + 1)
                for par in range(2):
                    h = 2 * m + par
                    pr = slice(par * 64, (par + 1) * 64)
                    band = bands[band_i % 2]
                    band_i += 1
                    src = bass.AP(self.ebrep_t, h * 128 * 2048 + 128, [[2047, 128], [1, 1920]])
                    P.dma(band.v, View([self.EBREP], src))
                    bo = self.banks[4 + 2 * (acc_i % 2)]
                    bd = self.banks[5 + 2 * (acc_i % 2)]
                    acc_i += 1
                    def scores(kt):
                        bs = self.bank()
                        self.mm(bs.v, KT[:, m, kt * 128:(kt + 1) * 128], qs[:, par, :], start=True, stop=False)
                        self.mm(bs.v, self.IND128[:, kt * 128:(kt + 1) * 128], mbT[:, par, :], start=False, stop=True)
                        return bs
                    LOOK = 3
                    pend = [scores(kt) for kt in range(min(LOOK, nkt))]
                    for kt in range(nkt):
                        bs = pend.pop(0)
                        if kt + LOOK < nkt:
                            pend.append(scores(kt + LOOK))
                        delta = tb * 512 - kt * 128
                        p = pT[p_i % NP]
                        p_i += 1
                        if delta >= 1024:
                            self.act(p.v, bs.v, AF.Exp, bias=self.RB31[:, h:h + 1])
                        else:
                            et = eT[e_i % NE]
                            e_i += 1
                            self.act(et.v, bs.v, AF.Exp)
                            self.tt("dve", p.v, et.v, band[:, delta + 384:delta + 384 + 512], ALU.mult)
                        self.mm(bo.v, VTM[:, kt, m * 128:(m + 1) * 128], p.v, start=(kt == 0), stop=(kt == nkt - 1))
                        self.mm(bd.v, self.ones_b.v, p.v, start=(kt == 0), stop=(kt == nkt - 1))
                    rd = rden[par]
                    self.recip_act(rd[pr, :], bd[pr, :])
                    self.tt("dve", ymo[pr, m, :], bo[pr, :], rd[pr, :], ALU.mult)
            for c2 in range(4):
                wo = self.load_w(w_out[:, c2 * 256:(c2 + 1) * 256], 8, 256)
                for cc in range(2):
                    c = c2 * 2 + cc
                    bk = self.bank()
                    for k in range(8):
                        rhs = yrw[:, k, tb * 512:(tb + 1) * 512] if k < 4 else ymo[:, k - 4, :]
                        self.mm(bk.v, wo[:, k, cc * 128:(cc + 1) * 128], rhs, start=(k == 0), stop=(k == 7))
                    xb = self.xb[c][tb]
                    self.tt("dve", xb.v, bk.v, xb.v, ALU.add)
        self.bank_set = list(range(8))
        P.barrier()

    def rwkv_setup(self):
        P = self.P
        self.BONES = Buf(P.sbuf("BONES", [128, 128], BF16)[:], "BONES")
        self.memset("pool", self.BONES.v, 0.0)
        self.memset("pool", self.BONES[0:64, 0:64], 1.0)
        self.memset("pool", self.BONES[64:128, 64:128], 1.0)
        self.MUs = Buf(P.sbuf("MUs", [64, 64], F32)[:], "MUs")
        self.MUi = Buf(P.sbuf("MUi", [64, 64], F32)[:], "MUi")
        self.MLs = Buf(P.sbuf("MLs", [64, 64], F32)[:], "MLs")
        for mk, op, pat, cm in [(self.MUs, ALU.is_gt, [[1, 64]], -1), (self.MUi, ALU.is_ge, [[1, 64]], -1),
                                (self.MLs, ALU.is_gt, [[-1, 64]], 1)]:
            self.memset("pool", mk.v, 1.0)
            P.op("pool", lambda e, mk=mk, op=op, pat=pat, cm=cm: e.affine_select(
                out=mk.ap, in_=mk.ap, compare_op=op, fill=0.0, base=0, pattern=pat, channel_multiplier=cm),
                ins=[mk], outs=[mk])
        self.RMASK = Buf(P.sbuf("RMASK", [128, 512], F32)[:], "RMASK")
        self.memset("pool", self.RMASK.v, 1.0)
        self.memset("pool", View([self.RMASK], self.RMASK.ap.rearrange("p (c t) -> p c t", t=64)[:, :, 0:1]), 0.0)

    def rwkv(self, l):
        P = self.P
        CD = 0.6065306597126334
        cv = self.carver()
        yrw = Buf(cv.b16(8192, "p (c t) -> p c t", c=4), "yrw")
        hT = Buf(cv.b16(4096, "p (c t) -> p c t", c=8), "hT")
        ra_o = cv.o
        RA = Buf(cv.f32(2048), "RA")
        sq = View([RA], RA.ap.bitcast(BF16).rearrange("p (c t) -> p c t", c=8))
        rstd = Buf(cv.f32(512), "rstd")
        waup = Buf(cv.b16(512), "waup")
        gup = Buf(cv.b16(512), "gup")
        lo12 = Buf(cv.b16(1024, "p (a t) -> p a t", a=2), "lo12")
        sgl = Buf(cv.b16(512), "sgl")
        carry = Buf(cv.f32(16), "carry")
        Hf_ap = cv.f32(512, "p (m i) -> p m i", m=4)
        Hb_ap = cv.b16(512, "p (m i) -> p m i", m=4)
        Hf = Buf(Hf_ap, "Hf")
        Hb = Buf(Hb_ap, "Hb")
        HfB = [[Buf(Hf_ap[p_ * 64:(p_ + 1) * 64, m_, p_ * 64:(p_ + 1) * 64], "Hf%d%d" % (m_, p_)) for p_ in range(2)] for m_ in range(4)]
        HbB = [[Buf(Hb_ap[p_ * 64:(p_ + 1) * 64, m_, p_ * 64:(p_ + 1) * 64], "Hb%d%d" % (m_, p_)) for p_ in range(2)] for m_ in range(4)]
        praw = [Buf(cv.f32(516), "praw%d" % i) for i in range(2)]
        f32t = {}
        f32o = {}
        for nm in ["rf", "kf", "lerp", "sig", "asig", "Lr", "eL", "eLm", "eX", "t1", "kkn", "kp", "bb", "bonus"]:
            f32o[nm] = cv.o
            f32t[nm] = Buf(cv.f32(512), nm)
        f32t["ys"] = f32t["lerp"]
        b16t = {}
        for nm in ["rt", "kt", "bt", "at", "vT", "kh", "bh", "gt", "tmpb"]:
            b16t[nm] = Buf(cv.b16(512), nm)
        khT = Buf(cv.b16(1024, "p (c j) -> p c j", c=8)[0:64], "khT")
        bhT = Buf(cv.b16(1024, "p (c j) -> p c j", c=8)[0:64], "bhT")
        vTM = Buf(cv.b16(1024, "p (c j) -> p c j", c=8)[0:64], "vTM")
        mats = {}
        for nm in ["TTb", "AkT", "ArbT", "ArkT", "ZC"]:
            mats[nm] = Buf(cv.b16(1024, "p (a t) -> p a t", a=16)[0:64], nm)
        inv = {}
        for nm in ["M", "N"]:
            inv[nm] = Buf(cv.b16(1024, "p (a t) -> p a t", a=16)[0:64], nm)

        def reg(o):
            return self.big[0:64, o:o + 512].bitcast(BF16).rearrange("p (a t) -> p a t", a=16)
        inv["M2"] = View([RA], reg(ra_o))
        inv["N2"] = View([RA], reg(ra_o + 512))
        inv["P2"] = View([f32t["rf"]], reg(f32o["rf"]))
        inv["Pm"] = View([f32t["sig"]], reg(f32o["sig"]))
        Zs = Buf(cv.b16(128)[0:64], "Zs")
        Us = Buf(cv.b16(128)[0:64], "Us")
        w_in = self.dram["w_mix_in"][l]
        gc = self.gcol("mix_norm_g", l)
        pc = self.pcol
        PT = self.PT

        def pcolv(name, idx, n_per_layer):
            c = pc[name] + l * n_per_layer + idx
            return PT[:, c:c + 1]
        P.dma(waup[0:64, :], raw(self.dram["rwkv_w_up"][l]), queue="pool")
        P.dma(waup[64:128, :], raw(self.dram["rwkv_a_up"][l]), queue="pool")
        P.dma(gup.v, raw(self.dram["rwkv_g_up"][l]), queue="pool")
        import os
        RLV = int(os.environ.get("RWKV_LV", "9"))
        if RLV < 9:
            self.memset("pool", yrw.v, 0.0)
        self.memset("pool", carry.v, 0.0)
        self.memset("pool", lo12.v, 0.0)
        for m_ in range(4):
            for p_ in range(2):
                self.memset("pool", HfB[m_][p_].v, 0.0)
                self.memset("pool", HbB[m_][p_].v, 0.0)
        self.bank_set = [0, 1, 2, 3, 4]
        BY = self.banks[5]
        pri = 0

        def project_lerp(j, out_view, tb):
            nonlocal pri
            w = self.load_w(w_in[:, j * 128:(j + 1) * 128], 8, 128)
            bk = self.bank()
            for k in range(8):
                self.mm(bk.v, w[:, k, :], hT[:, k, :], start=(k == 0), stop=(k == 7))
            pr_ = praw[pri % 2]
            pri += 1
            self.copy("dve", pr_[:, 0:1], carry[:, j:j + 1])
            self.copy("act", pr_[:, 1:513], bk.v)
            self.copy("dve", carry[:, j:j + 1], pr_[:, 512:513])
            d = f32t["t1"]
            self.tt("dve", d.v, pr_[:, 0:512], pr_[:, 1:513], ALU.subtract)
            self.stt(out_view, d.v, pcolv("rwkv_mu", j, 14), pr_[:, 1:513], ALU.mult, ALU.add)

        for tb in range(4):
            self.rmsnorm(tb, gc, hT.v, sq, rstd)
            lerp = f32t["lerp"]
            project_lerp(12, lerp.v, tb)
            self.act(lo12[0:64, 0, :], lerp[0:64, :], AF.Tanh)
            self.copy("dve", lo12[64:128, 1, :], lerp[64:128, :])
            project_lerp(13, lerp.v, tb)
            self.act(sgl.v, lerp.v, AF.Sigmoid)
            for m in range(4):
                rf, kf, sig, asig, Lr, eL, eLm, eX = (f32t[n] for n in ["rf", "kf", "sig", "asig", "Lr", "eL", "eLm", "eX"])
                t1, kkn, kp, bb, ys, bonus = (f32t[n] for n in ["t1", "kkn", "kp", "bb", "ys", "bonus"])
                rt, kt, bt, at, vT, kh, bh, gt, tmpb = (b16t[n] for n in ["rt", "kt", "bt", "at", "vT", "kh", "bh", "gt", "tmpb"])
                project_lerp(m, rf.v, tb)
                project_lerp(4 + m, kf.v, tb)
                project_lerp(8 + m, lerp.v, tb)
                self.copy("act", vT.v, lerp.v)
                bw = self.bank()
                self.mm(bw.v, waup[:, m * 128:(m + 1) * 128], lo12[:, 0, :], start=True, stop=True)
                self.act(sig.v, bw.v, AF.Sigmoid, bias=pcolv("rwkv_w0", m, 4))
                ba = self.bank()
                self.mm(ba.v, waup[:, m * 128:(m + 1) * 128], lo12[:, 1, :], start=True, stop=True)
                self.act(asig.v, ba.v, AF.Sigmoid, bias=pcolv("rwkv_a0", m, 4))
                bgt = self.bank()
                self.mm(bgt.v, gup[:, m * 128:(m + 1) * 128], sgl.v, start=True, stop=True)
                self.copy("act", gt.v, bgt.v)
                P.op("dve", lambda e, o=Lr.ap, d0=self.RMASK.ap, d1=sig.ap: e.tensor_tensor_scan(
                    out=o, data0=d0, data1=d1, initial=0.0, op0=ALU.mult, op1=ALU.add),
                    ins=[self.RMASK, sig], outs=[Lr])
                self.act(eL.v, Lr.v, AF.Exp, scale=-CD)
                self.act(eLm.v, Lr.v, AF.Exp, scale=CD)
                self.ts("dve", t1.v, kf.v, pcolv("rwkv_k_k", m, 4), ALU.mult)
                self.act(tmpb.v, t1.v, AF.Square)
                bss = self.bank()
                self.mm(bss.v, self.BONES.v, tmpb.v, start=True, stop=True)
                self.ts("dve", kkn.v, bss.v, 1e-24, ALU.max)
                self.rsqrt_act(kkn.v, kkn.v)
                self.tt("dve", kkn.v, t1.v, kkn.v, ALU.mult)
                self.ts("dve", t1.v, asig.v, -1.0, ALU.add, pcolv("rwkv_k_a", m, 4), ALU.mult)
                self.stt(kp.v, t1.v, 1.0, kf.v, ALU.add, ALU.mult)
                self.tt("dve", bb.v, kkn.v, asig.v, ALU.mult)
                self.stt(tmpb.v, rf.v, pcolv("rwkv_r_k", m, 4), kp.v, ALU.mult, ALU.mult)
                bbn = self.bank()
                self.mm(bbn.v, self.BONES.v, tmpb.v, start=True, stop=True)
                self.tt("dve", bonus.v, bbn.v, vT.v, ALU.mult)
                self.tt("dve", rt.v, rf.v, eL.v, ALU.mult)
                self.tt("dve", kt.v, kp.v, eLm.v, ALU.mult)
                self.tt("dve", bt.v, bb.v, eLm.v, ALU.mult)
                self.tt("dve", t1.v, Lr.v, sig.v, ALU.subtract)
                self.act(eX.v, t1.v, AF.Exp, scale=-CD)
                self.stt(at.v, kkn.v, -1.0, eX.v, ALU.mult, ALU.mult)
                lrc = View([Lr], Lr.ap.rearrange("p (c t) -> p c t", t=64)[:, :, 63:64].to_broadcast([128, 8, 64]))
                self.tt("dve", View([t1], t1.ap.rearrange("p (c t) -> p c t", t=64)),
                        View([Lr], Lr.ap.rearrange("p (c t) -> p c t", t=64)), lrc, ALU.subtract)
                self.act(eX.v, t1.v, AF.Exp, scale=CD)
                self.tt("dve", kh.v, kp.v, eX.v, ALU.mult)
                self.tt("dve", bh.v, bb.v, eX.v, ALU.mult)
                if RLV < 2:
                    continue
                for srcb, dstb in [(kh, khT), (bh, bhT), (vT, vTM)]:
                    bk = self.bank()
                    bkb = View([bk], bk.ap.bitcast(BF16))
                    for c in range(8):
                        self.transpose(bkb[0:64, c * 128:(c + 1) * 128], srcb[:, c * 64:(c + 1) * 64], self.ident_b)
                    self.copy("act", dstb.v, View([bk], bk.ap.bitcast(BF16)[0:64, :].rearrange("p (c j) -> p c j", c=8)))
                def r32v(v):
                    return v

                def chunk_mats(lhs, rhs, dst, mask, eng, r32=False):
                    for par in range(2):
                        pr = slice(par * 64, (par + 1) * 64)
                        bk = self.bank()
                        for c in range(8):
                            cs = slice(c * 64, (c + 1) * 64)
                            self.mm(bk[0:64, cs], lhs[pr, cs], rhs[pr, cs], start=True, stop=True)
                        mk = View([mask], mask.ap.rearrange("p (o t) -> p o t", o=1).to_broadcast([64, 8, 64]))
                        dv = dst[:, par * 8:(par + 1) * 8, :]
                        self.tt(eng, r32v(dv) if r32 else dv,
                                View([bk], bk.ap[0:64, :].rearrange("p (c t) -> p c t", c=8)), mk, ALU.mult)
                M, N, Pm, M2, N2, P2 = (inv[n] for n in ["M", "N", "Pm", "M2", "N2", "P2"])
                if RLV < 3:
                    continue
                chunk_mats(bt, at, M, self.MUs, "dve", r32=True)
                chunk_mats(at, bt, N, self.MLs, "dve", r32=True)
                chunk_mats(kt, at, mats["AkT"], self.MUs, "dve")
                chunk_mats(bt, rt, mats["ArbT"], self.MUi, "dve")
                chunk_mats(kt, rt, mats["ArkT"], self.MUi, "dve")
                if RLV < 4:
                    continue
                i64 = View([self.ident_b], self.ident_b.ap[0:64, 0:64].rearrange("p (o t) -> p o t", o=1).to_broadcast([64, 16, 64]))
                self.tt("dve", r32v(Pm.v), M.v, i64, ALU.add)
                for lvl in range(1, 6):
                    if lvl < 5:
                        for hh in range(2):
                            bk = self.bank()
                            for a8 in range(8):
                                a = hh * 8 + a8
                                self.mm(bk[0:64, a8 * 64:(a8 + 1) * 64], N[:, a, :], M[:, a, :], start=True, stop=True)
                            self.copy("act", r32v(M2[:, hh * 8:(hh + 1) * 8, :]),
                                      View([bk], bk.ap[0:64, :].rearrange("p (c t) -> p c t", c=8)))
                    for hh in range(2):
                        bk = self.bank()
                        for a8 in range(8):
                            a = hh * 8 + a8
                            self.mm(bk[0:64, a8 * 64:(a8 + 1) * 64], M[:, a, :], N[:, a, :], start=True, stop=True)
                        self.copy("act", r32v(N2[:, hh * 8:(hh + 1) * 8, :]),
                                  View([bk], bk.ap[0:64, :].rearrange("p (c t) -> p c t", c=8)))
                    for hh in range(2):
                        bk = self.bank()
                        for a8 in range(8):
                            a = hh * 8 + a8
                            self.mm(bk[0:64, a8 * 64:(a8 + 1) * 64], N2[:, a, :], Pm[:, a, :], start=True, stop=True)
                        self.tt("dve", r32v(P2[:, hh * 8:(hh + 1) * 8, :]),
                                View([bk], bk.ap[0:64, :].rearrange("p (c t) -> p c t", c=8)),
                                Pm[:, hh * 8:(hh + 1) * 8, :], ALU.add)
                    M, M2 = M2, M
                    N, N2 = N2, N
                    Pm, P2 = P2, Pm
                TTb = mats["TTb"]
                self.copy("act", TTb.v, Pm.v)
                AkT, ArbT, ArkT = mats["AkT"], mats["ArbT"], mats["ArkT"]
                if RLV < 5:
                    continue
                ZC = mats["ZC"]
                for par in range(2):
                    pr = slice(par * 64, (par + 1) * 64)
                    bk = self.bank()
                    for c in range(8):
                        self.mm(bk[0:64, c * 64:(c + 1) * 64], AkT[:, par * 8 + c, :], vTM[:, c, pr], start=True, stop=True)
                    self.copy("act", ZC[:, par * 8:(par + 1) * 8, :],
                              View([bk], bk.ap[0:64, :].rearrange("p (c t) -> p c t", c=8)))
                BYH = [self.banks[6], self.banks[7]]
                for c in range(8):
                    cs = slice(c * 64, (c + 1) * 64)
                    zb = [self.banks[(c % 2) * 2], self.banks[(c % 2) * 2 + 1]]
                    for par in range(2):
                        pr = slice(par * 64, (par + 1) * 64)
                        self.mm(zb[par][0:64, 0:64], at[pr, cs], HbB[m][par].v, start=True, stop=True)
                    zin = View(zb, self.ps_t[0:64, (c % 2) * 2:(c % 2) * 2 + 2, 0:64])
                    zc = View([ZC], ZC.ap.rearrange("p (h c) t -> p h c t", h=2)[:, :, c, :])
                    self.tt("dve", View([Zs], Zs.ap.rearrange("p (h t) -> p h t", h=2)), zin, zc, ALU.add)
                    bu = self.banks[4]
                    for par in range(2):
                        pr = slice(par * 64, (par + 1) * 64)
                        a = par * 8 + c
                        self.mm(bu[0:64, pr], TTb[:, a, :], Zs[:, pr], start=True, stop=True)
                    self.copy("act", Us.v, bu[0:64, 0:128])
                    for par in range(2):
                        pr = slice(par * 64, (par + 1) * 64)
                        a = par * 8 + c
                        self.mm(BYH[par][pr, cs], HbB[m][par].v, rt[pr, cs], start=True, stop=True)
                        self.mm(BY[pr, cs], Us[:, pr], ArbT[:, a, :], start=True, stop=False)
                        self.mm(BY[pr, cs], vTM[:, c, pr], ArkT[:, a, :], start=False, stop=True)
                    bhh = self.banks[4]
                    for par in range(2):
                        pr = slice(par * 64, (par + 1) * 64)
                        self.mm(bhh[pr, pr], khT[:, c, pr], vTM[:, c, pr], start=True, stop=False)
                        self.mm(bhh[pr, pr], bhT[:, c, pr], Us[:, pr], start=False, stop=True)
                    for par in range(2):
                        pr = slice(par * 64, (par + 1) * 64)
                        wc = eL[pr, c * 64 + 63:c * 64 + 64]
                        self.stt(HfB[m][par].v, HfB[m][par].v, wc, bhh[pr, pr], ALU.mult, ALU.add)
                        self.copy("act", HbB[m][par].v, HfB[m][par].v)
                if RLV < 6:
                    continue
                self.copy("act", ys.v, BY.v)
                for par in range(2):
                    pr = slice(par * 64, (par + 1) * 64)
                    self.tt("dve", ys[pr, :], ys[pr, :], BYH[par][pr, :], ALU.add)
                self.copy("dve", tmpb.v, ys.v)
                bm = self.bank()
                self.mm(bm.v, self.BONES.v, tmpb.v, start=True, stop=True)
                self.stt(ys.v, bm.v, -1.0 / 64.0, ys.v, ALU.mult, ALU.add)
                self.act(tmpb.v, ys.v, AF.Square)
                bvv = self.bank()
                self.mm(bvv.v, self.BONES.v, tmpb.v, start=True, stop=True)
                self.rsqrt_act(t1.v, bvv.v, scale=1.0 / 64.0, bias=self.eps_tiles[LNX_EPS])
                self.tt("dve", ys.v, ys.v, t1.v, ALU.mult)
                self.ts("dve", ys.v, ys.v, pcolv("rwkv_ln_g", m, 4), ALU.mult, pcolv("rwkv_ln_b", m, 4), ALU.add)
                self.tt("dve", ys.v, ys.v, bonus.v, ALU.add)
                self.tt("dve", yrw[:, m, tb * 512:(tb + 1) * 512], ys.v, gt.v, ALU.mult)
        self.bank_set = list(range(8))
        P.barrier()

    def finish(self):
        P = self.P
        cv = self.carver()
        sq = Buf(cv.b16(4096, "p (c t) -> p c t", c=8), "sq")
        rstd = Buf(cv.f32(512), "rstd")
        yT = Buf(cv.f32(4096, "p (c t) -> p c t", c=8), "yT")
        ostg = [Buf(cv.f32(1024), "ostg%d" % i) for i in range(2)]
        self.OUT = Buf(self.out_ap, "OUT")
        gc = self.pcol["final_norm_g"]
        n = 0
        for tb in range(4):
            if self.final_norm:
                self.rmsnorm(tb, gc, yT.v, sq, rstd)
                src = lambda c, t0: yT[:, c, t0:t0 + 128]
            else:
                src = lambda c, t0, tb=tb: View([self.xb[c][tb]], self.xb[c][tb].ap[:, t0:t0 + 128])
            for t4 in range(4):
                st = ostg[n % 2]
                n += 1
                for h in range(2):
                    bk = self.bank()
                    for c4 in range(4):
                        c = h * 4 + c4
                        self.transpose(bk[:, c4 * 128:(c4 + 1) * 128], src(c, t4 * 128), self.ident_f)
                    self.copy(self.evac_eng(), st[:, h * 512:(h + 1) * 512], bk.v)
                r0 = tb * 512 + t4 * 128
                P.dma(self.OUT[r0:r0 + 128, :], st.v)

    def build(self):
        self.alloc()
        self.make_eps()
        self.setup()
        self.mem_setup()
        self.P.barrier()
        self.moba_setup()
        self.rwkv_setup()
        self.P.barrier()
        for l in range(DEPTH):
            for stg in ["ffn1", "rwkv", "moba", "xattn", "ffn2"]:
                if (l, stg) not in self.stages:
                    continue
                if stg == "ffn1":
                    self.ffn(l, 1)
                elif stg == "ffn2":
                    self.ffn(l, 2)
                elif stg == "xattn":
                    self.xattn(l)
                elif stg == "moba":
                    self.moba(l)
                elif stg == "rwkv":
                    self.rwkv(l)
        self.finish()
        st = self.P.emit()
        return self.nc, st


BUCKET_STARTS = [0, 1, 2, 3, 4, 5, 6, 7, 8, 9, 10, 11, 12, 13, 14, 15, 16, 21, 27, 35, 46, 59, 77, 99, 128, 166,
                 216, 280, 363, 470, 609, 790]


def make_oh():
    oh = np.zeros((32, 2048), np.float32)
    for c in range(512, 2048):
        d = c - 512
        b = 0
        for i, st in enumerate(BUCKET_STARTS):
            if d >= st:
                b = i
        oh[b, c] = 1.0
    return oh


ALL_STAGES = [(l, s) for l in range(DEPTH) for s in ["ffn1", "rwkv", "moba", "xattn", "ffn2"]]


def run(inputs, stages, final_norm=True, cores=8, trace=False, debug=False):
    k = K(stages, final_norm, debug)
    nc, st = k.build()
    in_maps = []
    oh = make_oh()
    for b in range(cores):
        m = {}
        for name, shape in PARAM_SPECS:
            a = np.asarray(inputs[name], dtype=np.float32)
            if name in ("x", "mem"):
                a = a[b]
            m[name] = np.ascontiguousarray(a)
        m["oh_tab"] = oh
        in_maps.append(m)
    res = run_bass_kernel_spmd(nc, in_maps, core_ids=list(range(cores)), trace=trace)
    out = np.stack([r["out"] for r in res.results], axis=0)
    if debug:
        return out, res, st, res.results[0]["dbg"]
    return out, res, st


def kernel(**inputs):
    out, _, _ = run(inputs, ALL_STAGES, True, 8)
    return out.astype(np.float32)
```

```python
import numpy as np
from contextlib import ExitStack
import concourse.bass as bass
import concourse.mybir as mybir
from concourse.bass_utils import run_bass_kernel_spmd

F32 = mybir.dt.float32
BF16 = mybir.dt.bfloat16
F32R = mybir.dt.float32r
AF = mybir.ActivationFunctionType
ALU = mybir.AluOpType
AX = mybir.AxisListType

D = 1024
S = 2048
DEPTH = 2
DFF = 2816
RW = 512
RPROJ = 1792
DPROJ = 3328
MEM = 256
RWKV_TWO_LANES = True
NORM_EPS = 1e-6
LNX_EPS = 64e-5


class Buf:
    def __init__(self, ap, name="", psum=False):
        self.ap = ap
        self.name = name
        self.psum = psum
        self.w = None
        self.r = {}
        self.dsem = None
        self.dcnt = 0

    def __getitem__(self, idx):
        return View([self], self.ap[idx])

    @property
    def v(self):
        return View([self], self.ap)


class View:
    def __init__(self, bufs, ap):
        self.bufs = bufs
        self.ap = ap

    @property
    def v(self):
        return self

    def __getitem__(self, idx):
        return View(self.bufs, self.ap[idx])


def raw(ap):
    return View([], ap)


class Instr:
    __slots__ = ("fn", "deps", "signal", "semval", "dma_inc")

    def __init__(self, fn, deps, dma_inc=None):
        self.fn = fn
        self.deps = deps
        self.signal = False
        self.semval = None
        self.dma_inc = dma_inc


ENGS = ["pe", "act", "dve", "pool", "sp"]


class Prog:
    def __init__(self, nc):
        self.nc = nc
        self.q = {e: [] for e in ENGS}
        self.stack = ExitStack()
        self.esem = {}
        self.n_dsem = 0
        self.all_dma_bufs = []

    def sbuf(self, name, shape, dtype):
        return self.stack.enter_context(self.nc.sbuf_tensor(name, list(shape), dtype))

    def psum(self, name, shape, dtype):
        return self.stack.enter_context(self.nc.psum_tensor(name, list(shape), dtype))

    def _get_dsem(self, buf):
        if buf.dsem is None:
            buf.dsem = self.stack.enter_context(self.nc.semaphore("d%d" % self.n_dsem))
            self.n_dsem += 1
            self.all_dma_bufs.append(buf)
        return buf.dsem

    def _deps(self, eng, ins, outs):
        deps = []
        for v in ins:
            for b in v.bufs:
                if b.w is not None:
                    deps.append(b.w)
                if b.psum:
                    for k, t in b.r.items():
                        if k != eng:
                            deps.append(t)
        for v in outs:
            for b in v.bufs:
                if b.w is not None:
                    deps.append(b.w)
                deps.extend(b.r.values())
        if eng == "pe":
            deps = [d for d in deps if not (d[0] == "E" and d[1] == "pe")]
        return deps

    def _mark(self, tok, key, ins, outs):
        for v in ins:
            for b in v.bufs:
                old = b.r.get(key)
                if old is None or old[2] < tok[2]:
                    b.r[key] = tok
        for v in outs:
            for b in v.bufs:
                b.w = tok
                b.r = {}

    def op(self, eng, fn, ins=(), outs=()):
        ins = [x.v if isinstance(x, Buf) else x for x in ins]
        outs = [x.v if isinstance(x, Buf) else x for x in outs]
        deps = self._deps(eng, ins, outs)
        idx = len(self.q[eng])
        self.q[eng].append(Instr(fn, deps))
        tok = ("E", eng, idx)
        self._mark(tok, eng, ins, outs)
        return tok

    def dma(self, out, in_, queue="sp", **kw):
        out = out.v if isinstance(out, Buf) else out
        in_ = in_.v if isinstance(in_, Buf) else in_
        owner = None
        for v in (out, in_):
            for b in v.bufs:
                owner = b
                break
            if owner is not None:
                break
        assert owner is not None
        sem = self._get_dsem(owner)
        owner.dcnt += 1
        val = 16 * owner.dcnt
        deps = self._deps(queue, [in_], [out])
        oap, iap = out.ap, in_.ap
        self.q[queue].append(Instr(lambda e: e.dma_start(out=oap, in_=iap, **kw), deps, dma_inc=sem))
        tok = ("S", sem, val)
        self._mark(tok, ("S", id(sem)), [in_], [out])
        return tok

    def _last_real(self, e):
        for i in range(len(self.q[e]) - 1, -1, -1):
            ins = self.q[e][i]
            if ins.fn is not None and ins.dma_inc is None:
                return ("E", e, i)
        return None

    def _all_toks(self):
        toks = []
        for e in ENGS:
            t = self._last_real(e)
            if t is not None:
                toks.append(t)
        for b in self.all_dma_bufs:
            toks.append(("S", b.dsem, 16 * b.dcnt))
        return toks

    def barrier(self):
        toks = self._all_toks()
        for e in ENGS:
            deps = [t for t in toks if not (t[0] == "E" and t[1] == e)]
            self.q[e].append(Instr(None, deps))

    def emit(self):
        nc = self.nc
        for e in ENGS:
            self.esem[e] = self.stack.enter_context(nc.semaphore("e_" + e))
        self.q["sp"].append(Instr(None, [t for t in self._all_toks() if not (t[0] == "E" and t[1] == "sp")]))
        for e in ENGS:
            for ins in self.q[e]:
                for d in ins.deps:
                    if d[0] == "E":
                        self.q[d[1]][d[2]].signal = True
        for e in ENGS:
            c = 0
            for ins in self.q[e]:
                if ins.signal:
                    assert ins.fn is not None and ins.dma_inc is None
                    c += 1
                    ins.semval = c
        handles = {"pe": "tensor", "act": "scalar", "dve": "vector", "pool": "gpsimd", "sp": "sync"}
        stats = {}
        with nc.Block() as block:
            for e in ENGS:
                def body(eh, e=e):
                    known = {}
                    nw = 0
                    for ins in self.q[e]:
                        waits = {}
                        for d in ins.deps:
                            if d[0] == "E":
                                sem = self.esem[d[1]]
                                val = self.q[d[1]][d[2]].semval
                            else:
                                sem, val = d[1], d[2]
                            k = id(sem)
                            if known.get(k, 0) >= val:
                                continue
                            if k not in waits or waits[k][1] < val:
                                waits[k] = (sem, val)
                        wl = list(waits.values())
                        for k, (sem, val) in waits.items():
                            known[k] = val
                        nw += len(wl)
                        if ins.fn is None:
                            for sem, val in wl:
                                eh.wait_ge(sem, val)
                            continue
                        for sem, val in wl[:-1]:
                            eh.wait_ge(sem, val)
                        bi = ins.fn(eh)
                        if wl:
                            bi._wait_ge(wl[-1][0], wl[-1][1])
                        if ins.dma_inc is not None:
                            bi.then_inc(ins.dma_inc, 16)
                        elif ins.signal:
                            bi.then_inc(self.esem[e], 1)
                    stats[e] = (len(self.q[e]), nw)
                getattr(block, handles[e])(body)
        self.stats = stats
        return stats


PARAM_SPECS = [
    ("x", [S, D]), ("mem", [MEM, D]), ("rel_bias", [8, 32]), ("final_norm_g", [D]),
    ("ffn1_norm_g", [DEPTH, D]), ("ffn1_w_in", [DEPTH, D, 2 * DFF]), ("ffn1_w_out", [DEPTH, DFF, D]),
    ("mix_norm_g", [DEPTH, D]), ("w_mix_in", [DEPTH, D, DPROJ]), ("w_mix_out", [DEPTH, D, D]),
    ("rwkv_mu", [DEPTH, RPROJ]), ("rwkv_w0", [DEPTH, RW]), ("rwkv_w_up", [DEPTH, 64, RW]),
    ("rwkv_a0", [DEPTH, RW]), ("rwkv_a_up", [DEPTH, 64, RW]), ("rwkv_g_up", [DEPTH, 128, RW]),
    ("rwkv_k_k", [DEPTH, RW]), ("rwkv_k_a", [DEPTH, RW]), ("rwkv_r_k", [DEPTH, 8, 64]),
    ("rwkv_ln_g", [DEPTH, RW]), ("rwkv_ln_b", [DEPTH, RW]),
    ("xattn_norm_g", [DEPTH, D]), ("mem_norm_g", [DEPTH, D]),
    ("xattn_w_q", [DEPTH, D, D]), ("xattn_w_kv", [DEPTH, D, 2 * D]), ("xattn_w_o", [DEPTH, D, D]),
    ("ffn2_norm_g", [DEPTH, D]), ("ffn2_w_in", [DEPTH, D, 2 * DFF]), ("ffn2_w_out", [DEPTH, DFF, D]),
]

GAIN_NAMES = ["ffn1_norm_g", "mix_norm_g", "xattn_norm_g", "mem_norm_g", "ffn2_norm_g"]


class Carver:
    def __init__(self, big, nwords):
        self.big = big
        self.n = nwords
        self.o = 0

    def f32(self, n, pat=None, **kw):
        ap = self.big[:, self.o:self.o + n]
        self.o += n
        assert self.o <= self.n, "scratch overflow"
        return ap.rearrange(pat, **kw) if pat else ap

    def b16(self, n, pat=None, **kw):
        w = (n + 1) // 2
        ap = self.big[:, self.o:self.o + w].bitcast(BF16)
        self.o += w
        assert self.o <= self.n, "scratch overflow"
        return ap.rearrange(pat, **kw) if pat else ap


class K:
    def __init__(self, stages, final_norm=True, debug=False):
        self.stages = stages
        self.final_norm = final_norm
        nc = bass.Bass("TRN2", target_bir_lowering=False)
        self.nc = nc
        self.P = Prog(nc)
        self.dram = {}
        for name, shape in PARAM_SPECS:
            self.dram[name] = nc.dram_tensor(name, shape, F32, kind="ExternalInput").ap()
        self.dram["oh_tab"] = nc.dram_tensor("oh_tab", [32, 2048], F32, kind="ExternalInput").ap()
        self.ebrep_t = nc.dram_tensor("ebrep", [8, 128, 2048], BF16, kind="Internal")
        self.out_ap = nc.dram_tensor("out", [S, D], F32, kind="ExternalOutput").ap()
        self.debug = debug
        if debug:
            self.dbg_ap = nc.dram_tensor("dbg", [128, 16 * 512], F32, kind="ExternalOutput").ap()
            self.DBG = Buf(self.dbg_ap, "DBG")
            self.dbg_n = 0
        self.pb_i = 0
        self.bank_set = list(range(8))
        self.ring_i = 0
        self.ev_i = 0

    def alloc(self):
        P = self.P
        self.xT_t = P.sbuf("xT", [128, 8, S], F32)
        self.xb = [[Buf(self.xT_t[:, c, tb * 512:(tb + 1) * 512], "x%d_%d" % (c, tb)) for tb in range(4)]
                   for c in range(8)]
        self.ident_f = Buf(P.sbuf("ident_f", [128, 128], F32)[:], "ident_f")
        self.ident_b = Buf(P.sbuf("ident_b", [128, 128], BF16)[:], "ident_b")
        self.ones_b = Buf(P.sbuf("ones_b", [128, 128], BF16)[:], "ones_b")
        self.pt_stage = Buf(P.sbuf("pt_stage", [128, 128], F32)[:], "pt_stage")
        self.PT = Buf(P.sbuf("PT", [128, 256], F32)[:], "PT")
        self.ps_t = P.psum("ps", [128, 8, 512], F32)
        self.banks = [Buf(self.ps_t[:, i, :], "bank%d" % i, psum=True) for i in range(8)]
        self.NSLOT = 3
        self.ring_t = P.sbuf("ring", [128, self.NSLOT, 22 * 128], BF16)
        self.ring = [Buf(self.ring_t[:, i, :], "ring%d" % i) for i in range(self.NSLOT)]
        self.BIGW = 29 * 1024
        self.big = P.sbuf("big", [128, self.BIGW], F32)

    def carver(self):
        return Carver(self.big, self.BIGW)

    def dump(self, view, n, name=""):
        if not self.debug:
            return
        if not hasattr(self, "dbg_stg"):
            self.dbg_stg = [Buf(self.P.sbuf("dbgs%d" % i, [128, 512], F32)[:], "dbgs%d" % i) for i in range(2)]
        st = self.dbg_stg[self.dbg_n % 2]
        np_ = view.ap.shape[0]
        self.P.op("pool", lambda e: e.memset(st.ap, 0.0), outs=[st.v])
        bp = 0
        self.copy("dve", st[bp:bp + np_, 0:n], view)
        self.P.dma(self.DBG[:, self.dbg_n * 512:(self.dbg_n + 1) * 512], st.v)
        print("dump slot", self.dbg_n, name)
        self.dbg_n += 1

    def act(self, out, in_, func, bias=None, scale=None):
        oap, iap = out.ap, in_.ap
        ins = [in_]
        kw = {}
        if bias is not None:
            if isinstance(bias, (View, Buf)):
                ins.append(bias)
                kw["bias"] = bias.ap
            else:
                kw["bias"] = bias
        if scale is not None:
            if isinstance(scale, (View, Buf)):
                ins.append(scale)
                kw["scale"] = scale.ap
            else:
                kw["scale"] = scale
        self.P.op("act", lambda e: e.activation(out=oap, in_=iap, func=func, **kw), ins=ins, outs=[out])

    def tt(self, eng, out, in0, in1, op):
        oap, a, b = out.ap, in0.ap, in1.ap
        self.P.op(eng, lambda e: e.tensor_tensor(out=oap, in0=a, in1=b, op=op), ins=[in0, in1], outs=[out])

    def ts(self, eng, out, in0, s1, op0, s2=None, op1=None):
        oap, a = out.ap, in0.ap
        ins = [in0]
        v1 = s1
        if isinstance(s1, (View, Buf)):
            ins.append(s1)
            v1 = s1.ap
        v2 = s2
        if isinstance(s2, (View, Buf)):
            ins.append(s2)
            v2 = s2.ap
        if op1 is None:
            self.P.op(eng, lambda e: e.tensor_scalar(out=oap, in0=a, scalar1=v1, scalar2=None, op0=op0), ins=ins, outs=[out])
        else:
            self.P.op(eng, lambda e: e.tensor_scalar(out=oap, in0=a, scalar1=v1, scalar2=v2, op0=op0, op1=op1),
                      ins=ins, outs=[out])

    def stt(self, out, in0, scalar, in1, op0, op1):
        oap, a, b = out.ap, in0.ap, in1.ap
        ins = [in0, in1]
        sv = scalar
        if isinstance(scalar, (View, Buf)):
            ins.append(scalar)
            sv = scalar.ap
        self.P.op("dve", lambda e: e.scalar_tensor_tensor(out=oap, in0=a, scalar=sv, in1=b, op0=op0, op1=op1),
                  ins=ins, outs=[out])

    def rsqrt_act(self, out, in_, scale=1.0, bias=None):
        self.act(out, in_, AF.Ln, bias=bias, scale=scale)
        self.act(out, out, AF.Exp, scale=-0.5)

    def recip_act(self, out, in_):
        self.act(out, in_, AF.Ln)
        self.act(out, out, AF.Exp, scale=-1.0)

    def recip(self, out, in_):
        oap, iap = out.ap, in_.ap
        self.P.op("dve", lambda e: e.reciprocal(out=oap, in_=iap), ins=[in_], outs=[out])

    def memset(self, eng, out, val):
        oap = out.ap
        self.P.op(eng, lambda e: e.memset(oap, val), outs=[out])

    def bank(self):
        bs = self.bank_set
        b = self.banks[bs[self.pb_i % len(bs)]]
        self.pb_i += 1
        return b

    def evac_eng(self):
        self.ev_i += 1
        return "act" if self.ev_i % 2 == 0 else "dve"

    def copy(self, eng, out, in_):
        oap, iap = out.ap, in_.ap
        if eng == "act":
            self.P.op("act", lambda e: e.activation(out=oap, in_=iap, func=AF.Copy), ins=[in_], outs=[out])
        else:
            self.P.op(eng, lambda e: e.tensor_copy(out=oap, in_=iap), ins=[in_], outs=[out])

    def mm(self, out, lhsT, rhs, start, stop, r32=False):
        oap, lap, rap = out.ap, lhsT.ap, rhs.ap
        if r32:
            lap, rap = lap.bitcast(F32R), rap.bitcast(F32R)
        self.P.op("pe", lambda e: e.matmul(oap, lhsT=lap, rhs=rap, start=start, stop=stop),
                  ins=[lhsT, rhs], outs=[out])

    def transpose(self, out, in_, ident):
        oap, iap, idap = out.ap, in_.ap, ident.ap
        self.P.op("pe", lambda e: e.transpose(out=oap, in_=iap, identity=idap), ins=[in_, ident], outs=[out])

    def load_w(self, w_ap, kc, ncols):
        slot = self.ring[self.ring_i % self.NSLOT]
        self.ring_i += 1
        dst = View([slot], slot.ap[:, 0:kc * ncols].rearrange("p (k n) -> p k n", k=kc))
        src = raw(w_ap.rearrange("(k p) n -> p k n", p=128))
        self.P.dma(dst, src, queue="pool")
        return dst

    def xview(self, cs, tb):
        bufs = [self.xb[c][tb] for c in range(cs.start, cs.stop)]
        return View(bufs, self.xT_t[:, cs, tb * 512:(tb + 1) * 512])

    def setup(self):
        P = self.P
        idf, idb, ones = self.ident_f, self.ident_b, self.ones_b
        P.op("pool", lambda e: e.memset(idf.ap, 0.0), outs=[idf.v])
        P.op("pool", lambda e: e.affine_select(out=idf.ap, in_=idf.ap, compare_op=ALU.not_equal, fill=1.0,
                                               base=0, pattern=[[-1, 128]], channel_multiplier=1),
             ins=[idf.v], outs=[idf.v])
        self.copy("dve", idb.v, idf.v)
        P.op("pool", lambda e: e.memset(ones.ap, 1.0), outs=[ones.v])
        self.pcol = {}
        col = 0
        groups = []
        rows = 0
        cur = []
        plist = [(n, DEPTH * 8) for n in GAIN_NAMES] + [("final_norm_g", 8)]
        plist += [("rwkv_mu", DEPTH * 14)] + [(n, DEPTH * 4) for n in
                                              ["rwkv_w0", "rwkv_a0", "rwkv_k_k", "rwkv_k_a", "rwkv_r_k",
                                               "rwkv_ln_g", "rwkv_ln_b"]]
        for name, nrow in plist:
            if rows + nrow > 128:
                groups.append(cur)
                cur = []
                rows = 0
            cur.append((name, nrow, rows))
            rows += nrow
        groups.append(cur)
        for grp in groups:
            st = self.pt_stage
            P.op("pool", lambda e: e.memset(st.ap, 0.0), outs=[st.v])
            tot = 0
            for name, nrow, r0 in grp:
                ap = self.dram[name]
                if name == "final_norm_g":
                    src = ap.rearrange("(c p) -> c p", p=128)
                elif name == "rwkv_r_k":
                    src = ap.rearrange("l (m two) d -> (l m) (two d)", two=2)
                else:
                    src = ap.rearrange("l (c p) -> (l c) p", p=128)
                P.dma(st[r0:r0 + nrow, :], raw(src))
                self.pcol[name] = col + r0
                tot = r0 + nrow
            bk = self.bank()
            self.transpose(bk[:, 0:128], st.v, self.ident_f)
            self.copy("dve", self.PT[:, col:col + tot], bk[:, 0:tot])
            col += tot
        assert col <= 256
        cv = self.carver()
        stg = [Buf(cv.f32(1024), "xstg%d" % i) for i in range(3)]
        for tt in range(16):
            st = stg[tt % 3]
            P.dma(st.v, raw(self.dram["x"][tt * 128:(tt + 1) * 128, :]))
            tb = tt // 4
            t0 = (tt % 4) * 128
            for h in range(2):
                bk = self.bank()
                for c4 in range(4):
                    c = h * 4 + c4
                    self.transpose(bk[:, c4 * 128:(c4 + 1) * 128], st[:, c * 128:(c + 1) * 128], self.ident_f)
                cs = slice(h * 4, h * 4 + 4)
                dst = View([self.xb[c][tb] for c in range(cs.start, cs.stop)],
                           self.xT_t[:, cs, tb * 512 + t0: tb * 512 + t0 + 128])
                self.copy(self.evac_eng(), dst, View([bk], bk.ap.rearrange("p (c t) -> p c t", c=4)))
        P.barrier()

    def gcol(self, name, l):
        return self.pcol[name] + l * 8

    def rmsnorm(self, tb, gc, hT_view, sq, rstd, out_dtype_bf16=True):
        P = self.P
        xv = self.xview(slice(0, 8), tb)
        xap, sqap = xv.ap, sq.ap
        P.op("act", lambda e: e.activation(out=sqap, in_=xap, func=AF.Square), ins=[xv], outs=[sq.v])
        bk = self.bank()
        for c in range(8):
            self.mm(bk.v, self.ones_b.v, sq[:, c, :], start=(c == 0), stop=(c == 7))
        rap = rstd.ap
        self.rsqrt_act(rstd.v, bk.v, scale=1.0 / D, bias=self.eps_tiles[NORM_EPS])
        for c in range(8):
            xin = self.xb[c][tb]
            o = hT_view[:, c, :]
            oap, iap, gap = o.ap, xin.ap, self.PT.ap[:, gc + c: gc + c + 1]
            eng = "dve"
            P.op(eng, lambda e, oap=oap, iap=iap, gap=gap: e.scalar_tensor_tensor(
                out=oap, in0=iap, scalar=gap, in1=rap, op0=ALU.mult, op1=ALU.mult),
                ins=[xin.v, self.PT.v, rstd.v], outs=[o])

    def eps_ap(self, val):
        return self.eps_tiles[val].ap

    def make_eps(self):
        self.eps_tiles = {}
        for i, val in enumerate([NORM_EPS, LNX_EPS]):
            b = Buf(self.P.sbuf("eps%d" % i, [128, 1], F32)[:], "eps%d" % i)
            self.P.op("pool", lambda e, b=b, val=val: e.memset(b.ap, val), outs=[b.v])
            self.eps_tiles[val] = b

    def ffn(self, l, which):
        P = self.P
        w_in = self.dram["ffn%d_w_in" % which][l]
        w_out = self.dram["ffn%d_w_out" % which][l]
        gc = self.gcol("ffn%d_norm_g" % which, l)
        cv = self.carver()
        hT = [Buf(cv.b16(4096, "p (c t) -> p c t", c=8), "hT%d" % i) for i in range(2)]
        aT = [[Buf(cv.b16(512), "aT%d_%d" % (j, i)) for i in range(2)] for j in range(22)]
        sq = Buf(cv.b16(4096, "p (c t) -> p c t", c=8), "sq")
        rstd = Buf(cv.f32(512), "rstd")
        sg = [Buf(cv.f32(512), "sg%d" % i) for i in range(4)]
        for half in range(2):
            for i in range(2):
                self.rmsnorm(2 * half + i, gc, hT[i].v, sq, rstd)
            for j in range(22):
                wg = self.load_w(w_in[:, j * 128:(j + 1) * 128], 8, 128)
                wu = self.load_w(w_in[:, DFF + j * 128:DFF + (j + 1) * 128], 8, 128)
                for i in range(2):
                    pg = self.bank()
                    pu = self.bank()
                    for k in range(8):
                        self.mm(pg.v, wg[:, k, :], hT[i][:, k, :], start=(k == 0), stop=(k == 7))
                    for k in range(8):
                        self.mm(pu.v, wu[:, k, :], hT[i][:, k, :], start=(k == 0), stop=(k == 7))
                    s = sg[(j * 2 + i) % 4]
                    sap, pgap, puap, aap = s.ap, pg.ap, pu.ap, aT[j][i].ap
                    P.op("act", lambda e, sap=sap, pgap=pgap: e.activation(out=sap, in_=pgap, func=AF.Silu),
                         ins=[pg.v], outs=[s.v])
                    P.op("dve", lambda e, sap=sap, puap=puap, aap=aap: e.tensor_tensor(
                        out=aap, in0=sap, in1=puap, op=ALU.mult), ins=[s.v, pu.v], outs=[aT[j][i].v])
            for c in range(8):
                wo = self.load_w(w_out[:, c * 128:(c + 1) * 128], 22, 128)
                for i in range(2):
                    po = self.bank()
                    for k in range(22):
                        self.mm(po.v, wo[:, k, :], aT[k][i].v, start=(k == 0), stop=(k == 21))
                    xb = self.xb[c][2 * half + i]
                    xap, poap = xb.ap, po.ap
                    P.op("dve", lambda e, xap=xap, poap=poap: e.scalar_tensor_tensor(
                        out=xap, in0=poap, scalar=0.5, in1=xap, op0=ALU.mult, op1=ALU.add),
                        ins=[po.v, xb.v], outs=[xb.v])
        P.barrier()

    def mem_setup(self, cv):
        P = self.P
        self.memT = Buf(cv.f32(2048, "p (c t) -> p c t", c=8), "memT")
        stg = [Buf(cv.f32(1024), "mstg%d" % i) for i in range(2)]
        for mt in range(2):
            P.dma(stg[mt].v, raw(self.dram["mem"][mt * 128:(mt + 1) * 128, :]))
            for h in range(2):
                bk = self.bank()
                for c4 in range(4):
                    c = h * 4 + c4
                    self.transpose(bk[:, c4 * 128:(c4 + 1) * 128], stg[mt][:, c * 128:(c + 1) * 128], self.ident_f)
                self.copy(self.evac_eng(), self.memT[:, h * 4:h * 4 + 4, mt * 128:(mt + 1) * 128],
                          View([bk], bk.ap.rearrange("p (c t) -> p c t", c=4)))

    def xattn(self, l):
        P = self.P
        cv = self.carver()
        self.mem_setup(cv)
        a16 = cv.b16
        hT = Buf(a16(4096, "p (c t) -> p c t", c=8), "hT")
        sq = Buf(a16(4096, "p (c t) -> p c t", c=8), "sq")
        qT = Buf(a16(4096, "p (c t) -> p c t", c=8), "qT")
        oT = Buf(a16(4096, "p (c t) -> p c t", c=8), "oT")
        mnT = Buf(a16(2048, "p (c t) -> p c t", c=8), "mnT")
        kT = Buf(a16(2048, "p (c t) -> p c t", c=8), "kT")
        vtm = Buf(a16(2048, "p (m n) -> p m n", m=2), "vtm")
        pT = [Buf(a16(512), "pT%d" % i) for i in range(4)]
        rstd = Buf(cv.f32(512), "rstd")
        rden = [Buf(cv.f32(512), "rden%d" % i) for i in range(2)]
        msq = Buf(a16(2048, "p (c t) -> p c t", c=8), "msq")
        mrs = Buf(cv.f32(256), "mrs")
        w_q = self.dram["xattn_w_q"][l]
        w_kv = self.dram["xattn_w_kv"][l]
        w_o = self.dram["xattn_w_o"][l]
        self.dump(self.memT[:, 0, :], 256, "memT0")
        self.dump(self.memT[:, 7, :], 256, "memT7")
        gm = self.gcol("mem_norm_g", l)
        mT = self.memT
        P.op("act", lambda e: e.activation(out=msq.ap, in_=mT.ap, func=AF.Square), ins=[mT.v], outs=[msq.v])
        bk = self.bank()
        for c in range(8):
            self.mm(bk[:, 0:MEM], self.ones_b.v, msq[:, c, :], start=(c == 0), stop=(c == 7))
        self.rsqrt_act(mrs.v, bk[:, 0:MEM], scale=1.0 / D, bias=self.eps_tiles[NORM_EPS])
        for c in range(8):
            oap, iap, gap = mnT.ap[:, c, :], mT.ap[:, c, :], self.PT.ap[:, gm + c:gm + c + 1]
            P.op("dve", lambda e, oap=oap, iap=iap, gap=gap: e.scalar_tensor_tensor(
                out=oap, in0=iap, scalar=gap, in1=mrs.ap, op0=ALU.mult, op1=ALU.mult),
                ins=[mT.v, self.PT.v, mrs.v], outs=[mnT.v])
        self.dump(mrs.v, 256, "mrs")
        self.dump(mnT[:, 0, :], 256, "mnT0")
        for c2 in range(4):
            wk = self.load_w(w_kv[:, c2 * 256:(c2 + 1) * 256], 8, 256)
            for cc in range(2):
                c = c2 * 2 + cc
                bk = self.bank()
                for k in range(8):
                    self.mm(bk[:, 0:MEM], wk[:, k, cc * 128:(cc + 1) * 128], mnT[:, k, :], start=(k == 0), stop=(k == 7))
                self.copy(self.evac_eng(), kT[:, c, :], bk[:, 0:MEM])
        for n4 in range(4):
            wv = self.load_w(w_kv[:, D + n4 * 256:D + (n4 + 1) * 256], 8, 256)
            for mt in range(2):
                bk = self.bank()
                for k in range(8):
                    self.mm(bk[:, 0:256], mnT[:, k, mt * 128:(mt + 1) * 128], wv[:, k, :], start=(k == 0), stop=(k == 7))
                self.copy(self.evac_eng(), vtm[:, mt, n4 * 256:(n4 + 1) * 256], bk[:, 0:256])
        self.dump(kT[:, 0, :], 256, "kT0")
        self.dump(vtm[:, 0, 0:512], 512, "vtm0")
        gx = self.gcol("xattn_norm_g", l)
        for tb in range(4):
            self.rmsnorm(tb, gx, hT.v, sq, rstd)
            for c2 in range(4):
                wq = self.load_w(w_q[:, c2 * 256:(c2 + 1) * 256], 8, 256)
                for cc in range(2):
                    c = c2 * 2 + cc
                    bk = self.bank()
                    for k in range(8):
                        self.mm(bk.v, wq[:, k, cc * 128:(cc + 1) * 128], hT[:, k, :], start=(k == 0), stop=(k == 7))
                    self.copy(self.evac_eng(), qT[:, c, :], bk.v)
            def xscores(h):
                bks = []
                for mt in range(2):
                    bk = self.bank()
                    for dc in range(2):
                        c = 2 * h + dc
                        self.mm(bk.v, kT[:, c, mt * 128:(mt + 1) * 128], qT[:, c, :], start=(dc == 0), stop=(dc == 1))
                    bks.append(bk)
                return bks
            nxt = xscores(0)
            for h in range(4):
                bks = nxt
                pts = []
                for mt in range(2):
                    p = pT[(h * 2 + mt) % 4]
                    self.act(p.v, bks[mt].v, AF.Exp, scale=1.0 / 16.0)
                    pts.append(p)
                if h + 1 < 4:
                    nxt = xscores(h + 1)
                bden = self.bank()
                for mt in range(2):
                    self.mm(bden.v, self.ones_b.v, pts[mt].v, start=(mt == 0), stop=(mt == 1))
                rd = rden[h % 2]
                self.recip_act(rd.v, bden.v)
                for dc in range(2):
                    c = 2 * h + dc
                    bo = self.bank()
                    for mt in range(2):
                        self.mm(bo.v, vtm[:, mt, c * 128:(c + 1) * 128], pts[mt].v, start=(mt == 0), stop=(mt == 1))
                    self.tt("dve", oT[:, c, :], bo.v, rd.v, ALU.mult)
            if tb == 0:
                self.dump(qT[:, 0, :], 512, "qT0")
                self.dump(pT[0].v, 512, "pT0")
                self.dump(rden[0].v, 512, "rden0")
                self.dump(oT[:, 0, :], 512, "oT0")
            for c2 in range(4):
                wo = self.load_w(w_o[:, c2 * 256:(c2 + 1) * 256], 8, 256)
                for cc in range(2):
                    c = c2 * 2 + cc
                    bk = self.bank()
                    for k in range(8):
                        self.mm(bk.v, wo[:, k, cc * 128:(cc + 1) * 128], oT[:, k, :], start=(k == 0), stop=(k == 7))
                    xb = self.xb[c][tb]
                    xap, bap = xb.ap, bk.ap
                    P.op("dve", lambda e, xap=xap, bap=bap: e.tensor_tensor(out=xap, in0=bap, in1=xap, op=ALU.add),
                         ins=[bk.v, xb.v], outs=[xb.v])
        P.barrier()

    def moba_setup(self):
        P = self.P
        cv = self.carver()
        rbs = Buf(cv.f32(32)[0:8, :], "rbs")
        erbT = Buf(cv.f32(8)[0:32, :], "erbT")
        oh = Buf(cv.f32(2048)[0:32, :], "oh")
        ebrow = Buf(cv.b16(2048)[0:8, :], "ebrow")
        self.EBREP = Buf(self.ebrep_t.ap(), "EBREP")
        P.dma(rbs.v, raw(self.dram["rel_bias"]))
        P.dma(oh.v, raw(self.dram["oh_tab"]))
        self.act(rbs.v, rbs.v, AF.Exp)
        bk = self.bank()
        self.transpose(bk[0:32, 0:8], rbs.v, self.ident_f[0:8, 0:8])
        self.copy("dve", erbT.v, bk[0:32, 0:8])
        for c4 in range(4):
            bk = self.bank()
            self.mm(bk[0:8, :], erbT.v, oh[:, c4 * 512:(c4 + 1) * 512], start=True, stop=True)
            self.copy("dve", ebrow[:, c4 * 512:(c4 + 1) * 512], bk[0:8, :])
        srcb = View([ebrow], ebrow.ap.rearrange("h (o c) -> h o c", o=1).to_broadcast([8, 128, 2048]))
        P.dma(self.EBREP.v, srcb)
        self.RB31 = Buf(P.sbuf("RB31", [128, 8], F32)[:], "RB31")
        P.dma(self.RB31.v, raw(self.dram["rel_bias"][:, 31:32].rearrange("h o -> o h").partition_broadcast(128)),
              allow_slow_non_contiguous=True)
        self.IND128 = Buf(P.sbuf("IND", [128, 2048], BF16)[:], "IND128")
        self.IND = View([self.IND128], self.IND128.ap[0:8, :])
        ind = self.IND
        self.memset("pool", self.IND128.v, 0.0)
        self.memset("pool", ind.v, 1.0)
        P.op("pool", lambda e: e.affine_select(out=ind.ap, in_=ind.ap, compare_op=ALU.is_ge, fill=0.0, base=0,
                                               pattern=[[1, 2048]], channel_multiplier=-256), ins=[ind], outs=[ind])
        P.op("pool", lambda e: e.affine_select(out=ind.ap, in_=ind.ap, compare_op=ALU.is_ge, fill=0.0, base=255,
                                               pattern=[[-1, 2048]], channel_multiplier=256), ins=[ind], outs=[ind])
        P.dma(self.IND128[64:72, :], self.IND128[0:8, :])
        self.ELIG = Buf(P.sbuf("ELIG", [128, 8, 8], F32)[:], "ELIG")
        self.OWN = Buf(P.sbuf("OWN", [128, 8, 8], F32)[:], "OWN")
        self.memset("pool", self.ELIG.v, 0.0)
        self.memset("pool", self.OWN.v, 0.0)
        for qb in range(8):
            self.memset("pool", self.ELIG[:, qb, qb:8], -1e30)
            self.memset("pool", self.OWN[:, qb, qb:qb + 1], 1.0)

    def moba(self, l):
        P = self.P
        cv = self.carver()
        yrw = Buf(cv.b16(8192, "p (c t) -> p c t", c=4), "yrw")
        KT = Buf(cv.b16(8192, "p (c t) -> p c t", c=4), "KT")
        VTM = Buf(cv.b16(8192, "p (k n) -> p k n", k=16), "VTM")
        hT = Buf(cv.b16(4096, "p (c t) -> p c t", c=8), "hT")
        sq = Buf(cv.b16(4096, "p (c t) -> p c t", c=8), "sq")
        rstd = Buf(cv.f32(512), "rstd")
        qs = Buf(cv.b16(1024, "p (a t) -> p a t", a=2), "qs")
        qf = Buf(cv.f32(512), "qf")
        ymo = Buf(cv.b16(2048, "p (c t) -> p c t", c=4), "ymo")
        bands = [Buf(cv.b16(1920), "band%d" % i) for i in range(2)]
        NE, NP = 3, 5
        eT = [Buf(cv.f32(512), "eT%d" % i) for i in range(NE)]
        pT = [Buf(cv.b16(512), "pT%d" % i) for i in range(NP)]
        rden = [Buf(cv.f32(512), "rden%d" % i) for i in range(2)]
        mbT = Buf(cv.b16(1024, "p (a t) -> p a t", a=2), "mbT")
        selp = Buf(cv.f32(4 * 72, "p (a n) -> p a n", a=4), "selp")
        kmT = Buf(cv.f32(64, "p (c n) -> p c n", c=4), "kmT")
        gm = Buf(cv.f32(64, "p (a n) -> p a n", a=8), "gm")
        top8 = Buf(cv.f32(64, "p (a n) -> p a n", a=8), "top8")
        sel = Buf(cv.f32(64, "p (a n) -> p a n", a=8), "sel")
        w_in = self.dram["w_mix_in"][l]
        w_out = self.dram["w_mix_out"][l]
        gc = self.gcol("mix_norm_g", l)
        self.memset("pool", kmT.v, 0.0)
        self.memset("pool", selp.v, 0.0)
        self.memset("pool", qs.v, 0.0)
        self.memset("pool", mbT.v, 0.0)
        if (l, "rwkv") not in self.stages:
            self.memset("pool", yrw.v, 0.0)
        import os
        if int(os.environ.get("MOBA_LV", "9")) < 3:
            self.memset("pool", ymo.v, 0.0)
        self.bank_set = [0, 1, 2, 3]
        acc_i = 0
        band_i = 0
        e_i = 0
        p_i = 0
        QOFF = RPROJ
        for tb in range(4):
            self.rmsnorm(tb, gc, hT.v, sq, rstd)
            for m in range(4):
                wq = self.load_w(w_in[:, QOFF + m * 128:QOFF + (m + 1) * 128], 8, 128)
                wk = self.load_w(w_in[:, QOFF + 512 + m * 128:QOFF + 512 + (m + 1) * 128], 8, 128)
                wv = self.load_w(w_in[:, QOFF + 1024 + m * 128:QOFF + 1024 + (m + 1) * 128], 8, 128)
                import os
                SK = os.environ.get("MOBA_SKIP", "").split(",")
                bq = self.bank()
                for k in range(8):
                    self.mm(bq.v, wq[:, k, :], hT[:, k, :], start=(k == 0), stop=(k == 7))
                if "qf" not in SK:
                    self.copy("act", qf.v, bq.v)
                for par in range(2):
                    pr = slice(par * 64, (par + 1) * 64)
                    self.ts("dve", qs[pr, par, :], bq[pr, :], 0.125, ALU.mult)
                bkk = self.bank()
                for k in range(8):
                    self.mm(bkk.v, wk[:, k, :], hT[:, k, :], start=(k == 0), stop=(k == 7))
                self.copy("act", KT[:, m, tb * 512:(tb + 1) * 512], bkk.v)
                for par in range(2):
                    pr = slice(par * 64, (par + 1) * 64)
                    kin = View([bkk], bkk.ap[pr, :].rearrange("p (a t) -> p a t", a=2))
                    kout = kmT[pr, m, par * 8 + 2 * tb:par * 8 + 2 * tb + 2]
                    P.op("dve", lambda e, o=kout.ap, i=kin.ap: e.tensor_reduce(out=o, in_=i, axis=AX.X, op=ALU.add),
                         ins=[kin], outs=[kout])
                bv = self.bank()
                if "v" not in SK:
                    for tt in range(4):
                        for k in range(8):
                            self.mm(bv[:, tt * 128:(tt + 1) * 128], hT[:, k, tt * 128:(tt + 1) * 128], wv[:, k, :],
                                    start=(k == 0), stop=(k == 7))
                    self.copy("dve", VTM[:, tb * 4:tb * 4 + 4, m * 128:(m + 1) * 128],
                              View([bv], bv.ap.rearrange("p (a n) -> p a n", a=4)))
                import os
                LV = int(os.environ.get("MOBA_LV", "9"))
                if LV < 2:
                    continue
                bg = self.bank()
                for tt in range(4):
                    self.mm(bg[:, tt * 16:(tt + 1) * 16], qf[:, tt * 128:(tt + 1) * 128], kmT[:, m, :],
                            start=True, stop=True)
                for qq in range(2):
                    qb = 2 * tb + qq
                    el = View([self.ELIG], self.ELIG.ap[:, qb:qb + 1, :].to_broadcast([128, 4, 8]))
                    self.tt("dve", gm[:, qq * 4:(qq + 1) * 4, :],
                            View([bg], bg.ap[:, qq * 32:(qq + 1) * 32].rearrange("p (a n) -> p a n", a=4)), el, ALU.add)
                GLV = int(os.environ.get("GATE_LV", "9"))
                if GLV < 2:
                    continue
                for a in range(8):
                    P.op("dve", lambda e, o=top8.ap[:, a, :], i=gm.ap[:, a, :]: e.max(out=o, in_=i),
                         ins=[gm.v], outs=[top8.v])
                thr = View([top8], top8.ap[:, :, 2:3].to_broadcast([128, 8, 8]))
                self.tt("dve", sel.v, gm.v, thr, ALU.is_ge)
                for qq in range(2):
                    qb = 2 * tb + qq
                    ow = View([self.OWN], self.OWN.ap[:, qb:qb + 1, :].to_broadcast([128, 4, 8]))
                    self.tt("dve", sel[:, qq * 4:(qq + 1) * 4, :], sel[:, qq * 4:(qq + 1) * 4, :], ow, ALU.max)
                self.ts("dve", sel.v, sel.v, 30000.0, ALU.mult, -30000.0, ALU.add)
                if GLV < 3:
                    continue
                bt = self.bank()
                for tt in range(4):
                    self.transpose(bt[0:8, tt * 128:(tt + 1) * 128], sel[:, tt * 2, :], self.ident_f)
                self.copy("act", mbT[0:8, 0, :], bt[0:8, :])
                self.copy("dve", selp[:, :, 64:72], View([sel], sel.ap.rearrange("p (t two) n -> p t two n", two=2)[:, :, 1, :]))
                bt = self.bank()
                for tt in range(4):
                    self.transpose(bt[0:72, tt * 128:(tt + 1) * 128], selp[:, tt, :], self.ident_f)
                self.copy("act", mbT[64:72, 1, :], bt[64:72, :])
                if LV < 3:
                    continue
                nkt = 4 * (tb + 1)
                for par in range(2):
                    h = 2 * m + par
                    pr = slice(par * 64, (par + 1) * 64)
                    band = bands[band_i % 2]
                    band_i += 1
                    src = bass.AP(self.ebrep_t, h * 128 * 2048 + 128, [[2047, 128], [1, 1920]])
                    P.dma(band.v, View([self.EBREP], src))
                    bo = self.banks[4 + 2 * (acc_i % 2)]
                    bd = self.banks[5 + 2 * (acc_i % 2)]
                    acc_i += 1
                    def scores(kt):
                        bs = self.bank()
                        self.mm(bs.v, KT[:, m, kt * 128:(kt + 1) * 128], qs[:, par, :], start=True, stop=False)
                        self.mm(bs.v, self.IND128[:, kt * 128:(kt + 1) * 128], mbT[:, par, :], start=False, stop=True)
                        return bs
                    LOOK = 3
                    pend = [scores(kt) for kt in range(min(LOOK, nkt))]
                    for kt in range(nkt):
                        bs = pend.pop(0)
                        if kt + LOOK < nkt:
                            pend.append(scores(kt + LOOK))
                        delta = tb * 512 - kt * 128
                        p = pT[p_i % NP]
                        p_i += 1
                        if delta >= 1024:
                            self.act(p.v, bs.v, AF.Exp, bias=self.RB31[:, h:h + 1])
                        else:
                            et = eT[e_i % NE]
                            e_i += 1
                            self.act(et.v, bs.v, AF.Exp)
                            self.tt("dve", p.v, et.v, band[:, delta + 384:delta + 384 + 512], ALU.mult)
                        self.mm(bo.v, VTM[:, kt, m * 128:(m + 1) * 128], p.v, start=(kt == 0), stop=(kt == nkt - 1))
                        self.mm(bd.v, self.ones_b.v, p.v, start=(kt == 0), stop=(kt == nkt - 1))
                    rd = rden[par]
                    self.recip_act(rd[pr, :], bd[pr, :])
                    self.tt("dve", ymo[pr, m, :], bo[pr, :], rd[pr, :], ALU.mult)
            for c2 in range(4):
                wo = self.load_w(w_out[:, c2 * 256:(c2 + 1) * 256], 8, 256)
                for cc in range(2):
                    c = c2 * 2 + cc
                    bk = self.bank()
                    for k in range(8):
                        rhs = yrw[:, k, tb * 512:(tb + 1) * 512] if k < 4 else ymo[:, k - 4, :]
                        self.mm(bk.v, wo[:, k, cc * 128:(cc + 1) * 128], rhs, start=(k == 0), stop=(k == 7))
                    xb = self.xb[c][tb]
                    self.tt("dve", xb.v, bk.v, xb.v, ALU.add)
        self.bank_set = list(range(8))
        P.barrier()

    def rwkv_setup(self):
        P = self.P
        self.BONES = Buf(P.sbuf("BONES", [128, 128], BF16)[:], "BONES")
        self.memset("pool", self.BONES.v, 0.0)
        self.memset("pool", self.BONES[0:64, 0:64], 1.0)
        self.memset("pool", self.BONES[64:128, 64:128], 1.0)
        self.MUs = Buf(P.sbuf("MUs", [64, 64], F32)[:], "MUs")
        self.MUi = Buf(P.sbuf("MUi", [64, 64], F32)[:], "MUi")
        self.MLs = Buf(P.sbuf("MLs", [64, 64], F32)[:], "MLs")
        for mk, op, pat, cm in [(self.MUs, ALU.is_gt, [[1, 64]], -1), (self.MUi, ALU.is_ge, [[1, 64]], -1),
                                (self.MLs, ALU.is_gt, [[-1, 64]], 1)]:
            self.memset("pool", mk.v, 1.0)
            P.op("pool", lambda e, mk=mk, op=op, pat=pat, cm=cm: e.affine_select(
                out=mk.ap, in_=mk.ap, compare_op=op, fill=0.0, base=0, pattern=pat, channel_multiplier=cm),
                ins=[mk], outs=[mk])
        self.RMASK = Buf(P.sbuf("RMASK", [128, 512], F32)[:], "RMASK")
        self.memset("pool", self.RMASK.v, 1.0)
        self.memset("pool", View([self.RMASK], self.RMASK.ap.rearrange("p (c t) -> p c t", t=64)[:, :, 0:1]), 0.0)

    def rwkv(self, l):
        P = self.P
        CD = 0.6065306597126334
        cv = self.carver()
        yrw = Buf(cv.b16(8192, "p (c t) -> p c t", c=4), "yrw")
        hT = Buf(cv.b16(4096, "p (c t) -> p c t", c=8), "hT")
        ra_o = cv.o
        RA = Buf(cv.f32(2048), "RA")
        sq = View([RA], RA.ap.bitcast(BF16).rearrange("p (c t) -> p c t", c=8))
        rstd = Buf(cv.f32(512), "rstd")
        waup = Buf(cv.b16(512), "waup")
        gup = Buf(cv.b16(512), "gup")
        lo12 = Buf(cv.b16(1024, "p (a t) -> p a t", a=2), "lo12")
        sgl = Buf(cv.b16(512), "sgl")
        carry = Buf(cv.f32(16), "carry")
        Hf_ap = cv.f32(512, "p (m i) -> p m i", m=4)
        Hb_ap = cv.b16(512, "p (m i) -> p m i", m=4)
        Hf = Buf(Hf_ap, "Hf")
        Hb = Buf(Hb_ap, "Hb")
        HfB = [[Buf(Hf_ap[p_ * 64:(p_ + 1) * 64, m_, p_ * 64:(p_ + 1) * 64], "Hf%d%d" % (m_, p_)) for p_ in range(2)] for m_ in range(4)]
        HbB = [[Buf(Hb_ap[p_ * 64:(p_ + 1) * 64, m_, p_ * 64:(p_ + 1) * 64], "Hb%d%d" % (m_, p_)) for p_ in range(2)] for m_ in range(4)]
        praw = [Buf(cv.f32(516), "praw%d" % i) for i in range(2)]
        f32t = {}
        f32o = {}
        for nm in ["rf", "kf", "lerp", "sig", "asig", "Lr", "eL", "eLm", "eX", "t1", "kkn", "kp", "bb", "bonus"]:
            f32o[nm] = cv.o
            f32t[nm] = Buf(cv.f32(512), nm)
        f32t["ys"] = f32t["lerp"]
        b16t = {}
        for nm in ["rt", "kt", "bt", "at", "vT", "kh", "bh", "gt", "tmpb"]:
            b16t[nm] = Buf(cv.b16(512), nm)
        khT = Buf(cv.b16(1024, "p (c j) -> p c j", c=8)[0:64], "khT")
        bhT = Buf(cv.b16(1024, "p (c j) -> p c j", c=8)[0:64], "bhT")
        vTM = Buf(cv.b16(1024, "p (c j) -> p c j", c=8)[0:64], "vTM")
        mats = {}
        for nm in ["TTb", "AkT", "ArbT", "ArkT", "ZC"]:
            mats[nm] = Buf(cv.b16(1024, "p (a t) -> p a t", a=16)[0:64], nm)
        inv = {}
        for nm in ["M", "N"]:
            inv[nm] = Buf(cv.b16(1024, "p (a t) -> p a t", a=16)[0:64], nm)

        def reg(o):
            return self.big[0:64, o:o + 512].bitcast(BF16).rearrange("p (a t) -> p a t", a=16)
        inv["M2"] = View([RA], reg(ra_o))
        inv["N2"] = View([RA], reg(ra_o + 512))
        inv["P2"] = View([f32t["rf"]], reg(f32o["rf"]))
        inv["Pm"] = View([f32t["sig"]], reg(f32o["sig"]))
        Zs = Buf(cv.b16(128)[0:64], "Zs")
        Us = Buf(cv.b16(128)[0:64], "Us")
        w_in = self.dram["w_mix_in"][l]
        gc = self.gcol("mix_norm_g", l)
        pc = self.pcol
        PT = self.PT

        def pcolv(name, idx, n_per_layer):
            c = pc[name] + l * n_per_layer + idx
            return PT[:, c:c + 1]
        P.dma(waup[0:64, :], raw(self.dram["rwkv_w_up"][l]), queue="pool")
        P.dma(waup[64:128, :], raw(self.dram["rwkv_a_up"][l]), queue="pool")
        P.dma(gup.v, raw(self.dram["rwkv_g_up"][l]), queue="pool")
        import os
        RLV = int(os.environ.get("RWKV_LV", "9"))
        if RLV < 9:
            self.memset("pool", yrw.v, 0.0)
        self.memset("pool", carry.v, 0.0)
        self.memset("pool", lo12.v, 0.0)
        for m_ in range(4):
            for p_ in range(2):
                self.memset("pool", HfB[m_][p_].v, 0.0)
                self.memset("pool", HbB[m_][p_].v, 0.0)
        self.bank_set = [0, 1, 2, 3, 4]
        BY = self.banks[5]
        pri = 0

        def project_lerp(j, out_view, tb):
            nonlocal pri
            w = self.load_w(w_in[:, j * 128:(j + 1) * 128], 8, 128)
            bk = self.bank()
            for k in range(8):
                self.mm(bk.v, w[:, k, :], hT[:, k, :], start=(k == 0), stop=(k == 7))
            pr_ = praw[pri % 2]
            pri += 1
            self.copy("dve", pr_[:, 0:1], carry[:, j:j + 1])
            self.copy("act", pr_[:, 1:513], bk.v)
            self.copy("dve", carry[:, j:j + 1], pr_[:, 512:513])
            d = f32t["t1"]
            self.tt("dve", d.v, pr_[:, 0:512], pr_[:, 1:513], ALU.subtract)
            self.stt(out_view, d.v, pcolv("rwkv_mu", j, 14), pr_[:, 1:513], ALU.mult, ALU.add)

        for tb in range(4):
            self.rmsnorm(tb, gc, hT.v, sq, rstd)
            lerp = f32t["lerp"]
            project_lerp(12, lerp.v, tb)
            self.act(lo12[0:64, 0, :], lerp[0:64, :], AF.Tanh)
            self.copy("dve", lo12[64:128, 1, :], lerp[64:128, :])
            project_lerp(13, lerp.v, tb)
            self.act(sgl.v, lerp.v, AF.Sigmoid)
            for m in range(4):
                rf, kf, sig, asig, Lr, eL, eLm, eX = (f32t[n] for n in ["rf", "kf", "sig", "asig", "Lr", "eL", "eLm", "eX"])
                t1, kkn, kp, bb, ys, bonus = (f32t[n] for n in ["t1", "kkn", "kp", "bb", "ys", "bonus"])
                rt, kt, bt, at, vT, kh, bh, gt, tmpb = (b16t[n] for n in ["rt", "kt", "bt", "at", "vT", "kh", "bh", "gt", "tmpb"])
                project_lerp(m, rf.v, tb)
                project_lerp(4 + m, kf.v, tb)
                project_lerp(8 + m, lerp.v, tb)
                self.copy("act", vT.v, lerp.v)
                bw = self.bank()
                self.mm(bw.v, waup[:, m * 128:(m + 1) * 128], lo12[:, 0, :], start=True, stop=True)
                self.act(sig.v, bw.v, AF.Sigmoid, bias=pcolv("rwkv_w0", m, 4))
                ba = self.bank()
                self.mm(ba.v, waup[:, m * 128:(m + 1) * 128], lo12[:, 1, :], start=True, stop=True)
                self.act(asig.v, ba.v, AF.Sigmoid, bias=pcolv("rwkv_a0", m, 4))
                bgt = self.bank()
                self.mm(bgt.v, gup[:, m * 128:(m + 1) * 128], sgl.v, start=True, stop=True)
                self.copy("act", gt.v, bgt.v)
                P.op("dve", lambda e, o=Lr.ap, d0=self.RMASK.ap, d1=sig.ap: e.tensor_tensor_scan(
                    out=o, data0=d0, data1=d1, initial=0.0, op0=ALU.mult, op1=ALU.add),
                    ins=[self.RMASK, sig], outs=[Lr])
                self.act(eL.v, Lr.v, AF.Exp, scale=-CD)
                self.act(eLm.v, Lr.v, AF.Exp, scale=CD)
                self.ts("dve", t1.v, kf.v, pcolv("rwkv_k_k", m, 4), ALU.mult)
                self.act(tmpb.v, t1.v, AF.Square)
                bss = self.bank()
                self.mm(bss.v, self.BONES.v, tmpb.v, start=True, stop=True)
                self.ts("dve", kkn.v, bss.v, 1e-24, ALU.max)
                self.rsqrt_act(kkn.v, kkn.v)
                self.tt("dve", kkn.v, t1.v, kkn.v, ALU.mult)
                self.ts("dve", t1.v, asig.v, -1.0, ALU.add, pcolv("rwkv_k_a", m, 4), ALU.mult)
                self.stt(kp.v, t1.v, 1.0, kf.v, ALU.add, ALU.mult)
                self.tt("dve", bb.v, kkn.v, asig.v, ALU.mult)
                self.stt(tmpb.v, rf.v, pcolv("rwkv_r_k", m, 4), kp.v, ALU.mult, ALU.mult)
                bbn = self.bank()
                self.mm(bbn.v, self.BONES.v, tmpb.v, start=True, stop=True)
                self.tt("dve", bonus.v, bbn.v, vT.v, ALU.mult)
                self.tt("dve", rt.v, rf.v, eL.v, ALU.mult)
                self.tt("dve", kt.v, kp.v, eLm.v, ALU.mult)
                self.tt("dve", bt.v, bb.v, eLm.v, ALU.mult)
                self.tt("dve", t1.v, Lr.v, sig.v, ALU.subtract)
                self.act(eX.v, t1.v, AF.Exp, scale=-CD)
                self.stt(at.v, kkn.v, -1.0, eX.v, ALU.mult, ALU.mult)
                lrc = View([Lr], Lr.ap.rearrange("p (c t) -> p c t", t=64)[:, :, 63:64].to_broadcast([128, 8, 64]))
                self.tt("dve", View([t1], t1.ap.rearrange("p (c t) -> p c t", t=64)),
                        View([Lr], Lr.ap.rearrange("p (c t) -> p c t", t=64)), lrc, ALU.subtract)
                self.act(eX.v, t1.v, AF.Exp, scale=CD)
                self.tt("dve", kh.v, kp.v, eX.v, ALU.mult)
                self.tt("dve", bh.v, bb.v, eX.v, ALU.mult)
                if RLV < 2:
                    continue
                for srcb, dstb in [(kh, khT), (bh, bhT), (vT, vTM)]:
                    bk = self.bank()
                    bkb = View([bk], bk.ap.bitcast(BF16))
                    for c in range(8):
                        self.transpose(bkb[0:64, c * 128:(c + 1) * 128], srcb[:, c * 64:(c + 1) * 64], self.ident_b)
                    self.copy("act", dstb.v, View([bk], bk.ap.bitcast(BF16)[0:64, :].rearrange("p (c j) -> p c j", c=8)))
                def r32v(v):
                    return v

                def chunk_mats(lhs, rhs, dst, mask, eng, r32=False):
                    for par in range(2):
                        pr = slice(par * 64, (par + 1) * 64)
                        bk = self.bank()
                        for c in range(8):
                            cs = slice(c * 64, (c + 1) * 64)
                            self.mm(bk[0:64, cs], lhs[pr, cs], rhs[pr, cs], start=True, stop=True)
                        mk = View([mask], mask.ap.rearrange("p (o t) -> p o t", o=1).to_broadcast([64, 8, 64]))
                        dv = dst[:, par * 8:(par + 1) * 8, :]
                        self.tt(eng, r32v(dv) if r32 else dv,
                                View([bk], bk.ap[0:64, :].rearrange("p (c t) -> p c t", c=8)), mk, ALU.mult)
                M, N, Pm, M2, N2, P2 = (inv[n] for n in ["M", "N", "Pm", "M2", "N2", "P2"])
                if RLV < 3:
                    continue
                chunk_mats(bt, at, M, self.MUs, "dve", r32=True)
                chunk_mats(at, bt, N, self.MLs, "dve", r32=True)
                chunk_mats(kt, at, mats["AkT"], self.MUs, "dve")
                chunk_mats(bt, rt, mats["ArbT"], self.MUi, "dve")
                chunk_mats(kt, rt, mats["ArkT"], self.MUi, "dve")
                if RLV < 4:
                    continue
                i64 = View([self.ident_b], self.ident_b.ap[0:64, 0:64].rearrange("p (o t) -> p o t", o=1).to_broadcast([64, 16, 64]))
                self.tt("dve", r32v(Pm.v), M.v, i64, ALU.add)
                for lvl in range(1, 6):
                    if lvl < 5:
                        for hh in range(2):
                            bk = self.bank()
                            for a8 in range(8):
                                a = hh * 8 + a8
                                self.mm(bk[0:64, a8 * 64:(a8 + 1) * 64], N[:, a, :], M[:, a, :], start=True, stop=True)
                            self.copy("act", r32v(M2[:, hh * 8:(hh + 1) * 8, :]),
                                      View([bk], bk.ap[0:64, :].rearrange("p (c t) -> p c t", c=8)))
                    for hh in range(2):
                        bk = self.bank()
                        for a8 in range(8):
                            a = hh * 8 + a8
                            self.mm(bk[0:64, a8 * 64:(a8 + 1) * 64], M[:, a, :], N[:, a, :], start=True, stop=True)
                        self.copy("act", r32v(N2[:, hh * 8:(hh + 1) * 8, :]),
                                  View([bk], bk.ap[0:64, :].rearrange("p (c t) -> p c t", c=8)))
                    for hh in range(2):
                        bk = self.bank()
                        for a8 in range(8):
                            a = hh * 8 + a8
                            self.mm(bk[0:64, a8 * 64:(a8 + 1) * 64], N2[:, a, :], Pm[:, a, :], start=True, stop=True)
                        self.tt("dve", r32v(P2[:, hh * 8:(hh + 1) * 8, :]),
                                View([bk], bk.ap[0:64, :].rearrange("p (c t) -> p c t", c=8)),
                                Pm[:, hh * 8:(hh + 1) * 8, :], ALU.add)
                    M, M2 = M2, M
                    N, N2 = N2, N
                    Pm, P2 = P2, Pm
                TTb = mats["TTb"]
                self.copy("act", TTb.v, Pm.v)
                AkT, ArbT, ArkT = mats["AkT"], mats["ArbT"], mats["ArkT"]
                if RLV < 5:
                    continue
                ZC = mats["ZC"]
                for par in range(2):
                    pr = slice(par * 64, (par + 1) * 64)
                    bk = self.bank()
                    for c in range(8):
                        self.mm(bk[0:64, c * 64:(c + 1) * 64], AkT[:, par * 8 + c, :], vTM[:, c, pr], start=True, stop=True)
                    self.copy("act", ZC[:, par * 8:(par + 1) * 8, :],
                              View([bk], bk.ap[0:64, :].rearrange("p (c t) -> p c t", c=8)))
                BYH = [self.banks[6], self.banks[7]]
                for c in range(8):
                    cs = slice(c * 64, (c + 1) * 64)
                    zb = [self.banks[(c % 2) * 2], self.banks[(c % 2) * 2 + 1]]
                    for par in range(2):
                        pr = slice(par * 64, (par + 1) * 64)
                        self.mm(zb[par][0:64, 0:64], at[pr, cs], HbB[m][par].v, start=True, stop=True)
                    zin = View(zb, self.ps_t[0:64, (c % 2) * 2:(c % 2) * 2 + 2, 0:64])
                    zc = View([ZC], ZC.ap.rearrange("p (h c) t -> p h c t", h=2)[:, :, c, :])
                    self.tt("dve", View([Zs], Zs.ap.rearrange("p (h t) -> p h t", h=2)), zin, zc, ALU.add)
                    bu = self.banks[4]
                    for par in range(2):
                        pr = slice(par * 64, (par + 1) * 64)
                        a = par * 8 + c
                        self.mm(bu[0:64, pr], TTb[:, a, :], Zs[:, pr], start=True, stop=True)
                    self.copy("act", Us.v, bu[0:64, 0:128])
                    for par in range(2):
                        pr = slice(par * 64, (par + 1) * 64)
                        a = par * 8 + c
                        self.mm(BYH[par][pr, cs], HbB[m][par].v, rt[pr, cs], start=True, stop=True)
                        self.mm(BY[pr, cs], Us[:, pr], ArbT[:, a, :], start=True, stop=False)
                        self.mm(BY[pr, cs], vTM[:, c, pr], ArkT[:, a, :], start=False, stop=True)
                    bhh = self.banks[4]
                    for par in range(2):
                        pr = slice(par * 64, (par + 1) * 64)
                        self.mm(bhh[pr, pr], khT[:, c, pr], vTM[:, c, pr], start=True, stop=False)
                        self.mm(bhh[pr, pr], bhT[:, c, pr], Us[:, pr], start=False, stop=True)
                    for par in range(2):
                        pr = slice(par * 64, (par + 1) * 64)
                        wc = eL[pr, c * 64 + 63:c * 64 + 64]
                        self.stt(HfB[m][par].v, HfB[m][par].v, wc, bhh[pr, pr], ALU.mult, ALU.add)
                        self.copy("act", HbB[m][par].v, HfB[m][par].v)
                if RLV < 6:
                    continue
                self.copy("act", ys.v, BY.v)
                for par in range(2):
                    pr = slice(par * 64, (par + 1) * 64)
                    self.tt("dve", ys[pr, :], ys[pr, :], BYH[par][pr, :], ALU.add)
                self.copy("dve", tmpb.v, ys.v)
                bm = self.bank()
                self.mm(bm.v, self.BONES.v, tmpb.v, start=True, stop=True)
                self.stt(ys.v, bm.v, -1.0 / 64.0, ys.v, ALU.mult, ALU.add)
                self.act(tmpb.v, ys.v, AF.Square)
                bvv = self.bank()
                self.mm(bvv.v, self.BONES.v, tmpb.v, start=True, stop=True)
                self.rsqrt_act(t1.v, bvv.v, scale=1.0 / 64.0, bias=self.eps_tiles[LNX_EPS])
                self.tt("dve", ys.v, ys.v, t1.v, ALU.mult)
                self.ts("dve", ys.v, ys.v, pcolv("rwkv_ln_g", m, 4), ALU.mult, pcolv("rwkv_ln_b", m, 4), ALU.add)
                self.tt("dve", ys.v, ys.v, bonus.v, ALU.add)
                self.tt("dve", yrw[:, m, tb * 512:(tb + 1) * 512], ys.v, gt.v, ALU.mult)
        self.bank_set = list(range(8))
        P.barrier()

    def rwkv2(self, l):
        P = self.P
        CD = 0.6065306597126334
        TW, NCH = 256, 4
        cv = self.carver()
        yrw = Buf(cv.b16(8192, "p (c t) -> p c t", c=4), "yrw")
        hT = Buf(cv.b16(4096, "p (c t) -> p c t", c=8), "hT")
        sq = Buf(cv.b16(4096, "p (c t) -> p c t", c=8), "sq")
        rstd = Buf(cv.f32(512), "rstd")
        waup = Buf(cv.b16(512), "waup")
        gup = Buf(cv.b16(512), "gup")
        lo12 = Buf(cv.b16(1024, "p (a t) -> p a t", a=2), "lo12")
        sgl = Buf(cv.b16(512), "sgl")
        carry = Buf(cv.f32(16), "carry")
        lerpS = Buf(cv.f32(512), "lerpS")
        prawS = Buf(cv.f32(516), "prawS")
        dS = Buf(cv.f32(512), "dS")
        Hf_ap = cv.f32(512, "p (m i) -> p m i", m=4)
        Hb_ap = cv.b16(512, "p (m i) -> p m i", m=4)
        HfB = [[Buf(Hf_ap[p_ * 64:(p_ + 1) * 64, m_, p_ * 64:(p_ + 1) * 64], "Hf%d%d" % (m_, p_)) for p_ in range(2)] for m_ in range(4)]
        HbB = [[Buf(Hb_ap[p_ * 64:(p_ + 1) * 64, m_, p_ * 64:(p_ + 1) * 64], "Hb%d%d" % (m_, p_)) for p_ in range(2)] for m_ in range(4)]
        lanes = []
        for ln in range(2):
            L = {}
            L["praw"] = [Buf(cv.f32(TW + 4), "praw%d_%d" % (ln, i)) for i in range(2)]
            fo = {}
            for nm in ["rf", "kf", "lerp", "sig", "asig", "Lr", "eL", "eLm", "eX", "t1", "kkn", "kp", "bb", "bonus"]:
                fo[nm] = cv.o
                L[nm] = Buf(cv.f32(TW), "%s%d" % (nm, ln))
            L["ys"] = L["lerp"]
            for nm in ["rt", "kt", "bt", "at", "vT", "kh", "bh", "gt", "tmpb"]:
                L[nm] = Buf(cv.b16(TW), "%s%d" % (nm, ln))
            for nm in ["khT", "bhT", "vTM"]:
                L[nm] = Buf(cv.b16(NCH * 128, "p (c j) -> p c j", c=NCH)[0:64], "%s%d" % (nm, ln))
            for nm in ["TTb", "AkT", "ArbT", "ArkT", "ZC", "M", "N", "M2", "N2"]:
                L[nm] = Buf(cv.b16(2 * NCH * 64, "p (a t) -> p a t", a=2 * NCH)[0:64], "%s%d" % (nm, ln))

            def reg(o):
                return self.big[0:64, o:o + NCH * 64].bitcast(BF16).rearrange("p (a t) -> p a t", a=2 * NCH)
            L["P2"] = View([L["rf"]], reg(fo["rf"]))
            L["Pm"] = View([L["sig"]], reg(fo["sig"]))
            L["Zs"] = Buf(cv.b16(128)[0:64], "Zs%d" % ln)
            L["Us"] = Buf(cv.b16(128)[0:64], "Us%d" % ln)
            L["R"] = [4 * ln, 4 * ln + 1]
            L["ri"] = 0
            L["B0"] = self.banks[4 * ln + 2]
            L["B64"] = self.banks[4 * ln + 3]
            L["bi0"] = 4 * ln + 2
            L["pri"] = 0
            lanes.append(L)
        w_in = self.dram["w_mix_in"][l]
        gc = self.gcol("mix_norm_g", l)
        pc = self.pcol
        PT = self.PT

        def pcolv(name, idx, n_per_layer):
            c = pc[name] + l * n_per_layer + idx
            return PT[:, c:c + 1]
        P.dma(waup[0:64, :], raw(self.dram["rwkv_w_up"][l]), queue="pool")
        P.dma(waup[64:128, :], raw(self.dram["rwkv_a_up"][l]), queue="pool")
        P.dma(gup.v, raw(self.dram["rwkv_g_up"][l]), queue="pool")
        self.memset("pool", carry.v, 0.0)
        self.memset("pool", lo12.v, 0.0)
        for m_ in range(4):
            for p_ in range(2):
                self.memset("pool", HfB[m_][p_].v, 0.0)
                self.memset("pool", HbB[m_][p_].v, 0.0)

        def lbank(L):
            b = self.banks[L["R"][L["ri"] % 2]]
            L["ri"] += 1
            return b

        def unit(L, tb, m, hb):
            t0 = hb * TW
            g0 = tb * 512 + t0
            rf, kf, lerp, sig, asig, Lr, eL, eLm, eX = (L[n] for n in ["rf", "kf", "lerp", "sig", "asig", "Lr", "eL", "eLm", "eX"])
            t1, kkn, kp, bb, ys, bonus = (L[n] for n in ["t1", "kkn", "kp", "bb", "ys", "bonus"])
            rt, kt, bt, at, vT, kh, bh, gt, tmpb = (L[n] for n in ["rt", "kt", "bt", "at", "vT", "kh", "bh", "gt", "tmpb"])
            khT, bhT, vTM = L["khT"], L["bhT"], L["vTM"]
            TTb, AkT, ArbT, ArkT, ZC = L["TTb"], L["AkT"], L["ArbT"], L["ArkT"], L["ZC"]
            M, N, Pm, M2, N2, P2 = L["M"], L["N"], L["Pm"], L["M2"], L["N2"], L["P2"]
            Zs, Us = L["Zs"], L["Us"]
            B0, B64 = L["B0"], L["B64"]

            def project_lerp(j, out_view):
                w = self.load_w(w_in[:, j * 128:(j + 1) * 128], 8, 128)
                bk = lbank(L)
                for k in range(8):
                    self.mm(bk[:, 0:TW], w[:, k, :], hT[:, k, t0:t0 + TW], start=(k == 0), stop=(k == 7))
                pr_ = L["praw"][L["pri"] % 2]
                L["pri"] += 1
                self.copy("dve", pr_[:, 0:1], carry[:, j:j + 1])
                self.copy("act", pr_[:, 1:TW + 1], bk[:, 0:TW])
                self.copy("dve", carry[:, j:j + 1], pr_[:, TW:TW + 1])
                self.tt("dve", t1.v, pr_[:, 0:TW], pr_[:, 1:TW + 1], ALU.subtract)
                self.stt(out_view, t1.v, pcolv("rwkv_mu", j, 14), pr_[:, 1:TW + 1], ALU.mult, ALU.add)
            project_lerp(m, rf.v)
            yield
            project_lerp(4 + m, kf.v)
            yield
            project_lerp(8 + m, lerp.v)
            self.copy("act", vT.v, lerp.v)
            yield
            bw = lbank(L)
            self.mm(bw[:, 0:TW], waup[:, m * 128:(m + 1) * 128], lo12[:, 0, t0:t0 + TW], start=True, stop=True)
            self.act(sig.v, bw[:, 0:TW], AF.Sigmoid, bias=pcolv("rwkv_w0", m, 4))
            ba = lbank(L)
            self.mm(ba[:, 0:TW], waup[:, m * 128:(m + 1) * 128], lo12[:, 1, t0:t0 + TW], start=True, stop=True)
            self.act(asig.v, ba[:, 0:TW], AF.Sigmoid, bias=pcolv("rwkv_a0", m, 4))
            yield
            bgt = lbank(L)
            self.mm(bgt[:, 0:TW], gup[:, m * 128:(m + 1) * 128], sgl[:, t0:t0 + TW], start=True, stop=True)
            self.copy("act", gt.v, bgt[:, 0:TW])
            P.op("dve", lambda e, o=Lr.ap, d0=self.RMASK.ap[:, 0:TW], d1=sig.ap: e.tensor_tensor_scan(
                out=o, data0=d0, data1=d1, initial=0.0, op0=ALU.mult, op1=ALU.add), ins=[self.RMASK, sig], outs=[Lr])
            yield
            self.act(eL.v, Lr.v, AF.Exp, scale=-CD)
            self.act(eLm.v, Lr.v, AF.Exp, scale=CD)
            self.ts("dve", t1.v, kf.v, pcolv("rwkv_k_k", m, 4), ALU.mult)
            self.act(tmpb.v, t1.v, AF.Square)
            yield
            bss = lbank(L)
            self.mm(bss[:, 0:TW], self.BONES.v, tmpb.v, start=True, stop=True)
            self.ts("dve", kkn.v, bss[:, 0:TW], 1e-24, ALU.max)
            self.rsqrt_act(kkn.v, kkn.v)
            yield
            self.tt("dve", kkn.v, t1.v, kkn.v, ALU.mult)
            self.ts("dve", t1.v, asig.v, -1.0, ALU.add, pcolv("rwkv_k_a", m, 4), ALU.mult)
            yield
            self.stt(kp.v, t1.v, 1.0, kf.v, ALU.add, ALU.mult)
            self.tt("dve", bb.v, kkn.v, asig.v, ALU.mult)
            yield
            self.stt(tmpb.v, rf.v, pcolv("rwkv_r_k", m, 4), kp.v, ALU.mult, ALU.mult)
            bbn = lbank(L)
            self.mm(bbn[:, 0:TW], self.BONES.v, tmpb.v, start=True, stop=True)
            self.tt("dve", bonus.v, bbn[:, 0:TW], vT.v, ALU.mult)
            yield
            self.tt("dve", rt.v, rf.v, eL.v, ALU.mult)
            self.tt("dve", kt.v, kp.v, eLm.v, ALU.mult)
            yield
            self.tt("dve", bt.v, bb.v, eLm.v, ALU.mult)
            self.tt("dve", t1.v, Lr.v, sig.v, ALU.subtract)
            self.act(eX.v, t1.v, AF.Exp, scale=-CD)
            yield
            self.stt(at.v, kkn.v, -1.0, eX.v, ALU.mult, ALU.mult)
            lrc = View([Lr], Lr.ap.rearrange("p (c t) -> p c t", t=64)[:, :, 63:64].to_broadcast([128, NCH, 64]))
            self.tt("dve", View([t1], t1.ap.rearrange("p (c t) -> p c t", t=64)),
                    View([Lr], Lr.ap.rearrange("p (c t) -> p c t", t=64)), lrc, ALU.subtract)
            self.act(eX.v, t1.v, AF.Exp, scale=CD)
            yield
            self.tt("dve", kh.v, kp.v, eX.v, ALU.mult)
            self.tt("dve", bh.v, bb.v, eX.v, ALU.mult)
            yield
            for srcb, dstb in [(kh, khT), (bh, bhT), (vT, vTM)]:
                bk = lbank(L)
                bkb = View([bk], bk.ap.bitcast(BF16))
                for c in range(NCH):
                    self.transpose(bkb[0:64, c * 128:(c + 1) * 128], srcb[:, c * 64:(c + 1) * 64], self.ident_b)
                self.copy("act", dstb.v, View([bk], bk.ap.bitcast(BF16)[0:64, 0:NCH * 128].rearrange("p (c j) -> p c j", c=NCH)))
                yield

            def chunk_mats(lhs, rhs, dst, mask):
                for par in range(2):
                    pr = slice(par * 64, (par + 1) * 64)
                    bk = lbank(L)
                    for c in range(NCH):
                        cs = slice(c * 64, (c + 1) * 64)
                        self.mm(bk[0:64, cs], lhs[pr, cs], rhs[pr, cs], start=True, stop=True)
                    mk = View([mask], mask.ap.rearrange("p (o t) -> p o t", o=1).to_broadcast([64, NCH, 64]))
                    self.tt("dve", dst[:, par * NCH:(par + 1) * NCH, :],
                            View([bk], bk.ap[0:64, 0:NCH * 64].rearrange("p (c t) -> p c t", c=NCH)), mk, ALU.mult)
            chunk_mats(bt, at, M, self.MUs)
            yield
            chunk_mats(at, bt, N, self.MLs)
            yield
            chunk_mats(kt, at, AkT, self.MUs)
            yield
            chunk_mats(bt, rt, ArbT, self.MUi)
            yield
            chunk_mats(kt, rt, ArkT, self.MUi)
            yield
            NM = 2 * NCH
            i64 = View([self.ident_b], self.ident_b.ap[0:64, 0:64].rearrange("p (o t) -> p o t", o=1).to_broadcast([64, NM, 64]))
            self.tt("dve", Pm.v, M.v, i64, ALU.add)

            def allm(bk):
                return View([bk], bk.ap[0:64, 0:NM * 64].rearrange("p (c t) -> p c t", c=NM))
            for lvl in range(1, 6):
                if lvl < 5:
                    bk = lbank(L)
                    for a in range(NM):
                        self.mm(bk[0:64, a * 64:(a + 1) * 64], N[:, a, :], M[:, a, :], start=True, stop=True)
                    self.copy("act", M2.v, allm(bk))
                bk = lbank(L)
                for a in range(NM):
                    self.mm(bk[0:64, a * 64:(a + 1) * 64], M[:, a, :], N[:, a, :], start=True, stop=True)
                self.copy("dve", N2.v, allm(bk))
                yield
                bk = lbank(L)
                for a in range(NM):
                    self.mm(bk[0:64, a * 64:(a + 1) * 64], N2[:, a, :], Pm[:, a, :], start=True, stop=True)
                self.tt("dve", P2.v, allm(bk), Pm.v, ALU.add)
                yield
                M, M2 = M2, M
                N, N2 = N2, N
                Pm, P2 = P2, Pm
            self.copy("act", TTb.v, Pm.v)
            for par in range(2):
                pr = slice(par * 64, (par + 1) * 64)
                bk = lbank(L)
                for c in range(NCH):
                    self.mm(bk[0:64, c * 64:(c + 1) * 64], AkT[:, par * NCH + c, :], vTM[:, c, pr], start=True, stop=True)
                self.copy("act", ZC[:, par * NCH:(par + 1) * NCH, :],
                          View([bk], bk.ap[0:64, 0:NCH * 64].rearrange("p (c t) -> p c t", c=NCH)))
            yield
            ZB = [B0, B64]
            for c in range(NCH):
                cs = slice(c * 64, (c + 1) * 64)
                for par in range(2):
                    pr = slice(par * 64, (par + 1) * 64)
                    self.mm(ZB[par][0:64, 256:320], at[pr, cs], HbB[m][par].v, start=True, stop=True)
                zin = View(ZB, self.ps_t[0:64, L["bi0"]:L["bi0"] + 2, 256:320])
                zc = View([ZC], ZC.ap.rearrange("p (h c) t -> p h c t", h=2)[:, :, c, :])
                self.tt("dve", View([Zs], Zs.ap.rearrange("p (h t) -> p h t", h=2)), zin, zc, ALU.add)
                yield
                bu = lbank(L)
                for par in range(2):
                    pr = slice(par * 64, (par + 1) * 64)
                    self.mm(bu[0:64, pr], TTb[:, par * NCH + c, :], Zs[:, pr], start=True, stop=True)
                self.copy("act", Us.v, bu[0:64, 0:128])
                yield
                self.mm(B0[0:64, cs], HbB[m][0].v, rt[0:64, cs], start=True, stop=False)
                self.mm(B0[0:64, cs], Us[:, 0:64], ArbT[:, c, :], start=False, stop=False)
                self.mm(B0[0:64, cs], vTM[:, c, 0:64], ArkT[:, c, :], start=False, stop=True)
                self.mm(B64[64:128, cs], HbB[m][1].v, rt[64:128, cs], start=True, stop=True)
                self.mm(B0[64:128, cs], Us[:, 64:128], ArbT[:, NCH + c, :], start=True, stop=False)
                self.mm(B0[64:128, cs], vTM[:, c, 64:128], ArkT[:, NCH + c, :], start=False, stop=True)
                bhh = lbank(L)
                for par in range(2):
                    pr = slice(par * 64, (par + 1) * 64)
                    self.mm(bhh[pr, pr], khT[:, c, pr], vTM[:, c, pr], start=True, stop=False)
                    self.mm(bhh[pr, pr], bhT[:, c, pr], Us[:, pr], start=False, stop=True)
                for par in range(2):
                    pr = slice(par * 64, (par + 1) * 64)
                    wc = eL[pr, c * 64 + 63:c * 64 + 64]
                    self.stt(HfB[m][par].v, HfB[m][par].v, wc, bhh[pr, pr], ALU.mult, ALU.add)
                    self.copy("act", HbB[m][par].v, HfB[m][par].v)
                yield
            self.copy("act", ys.v, B0[:, 0:TW])
            self.tt("dve", ys[64:128, :], ys[64:128, :], B64[64:128, 0:TW], ALU.add)
            self.copy("dve", tmpb.v, ys.v)
            yield
            bm = lbank(L)
            self.mm(bm[:, 0:TW], self.BONES.v, tmpb.v, start=True, stop=True)
            self.stt(ys.v, bm[:, 0:TW], -1.0 / 64.0, ys.v, ALU.mult, ALU.add)
            self.act(tmpb.v, ys.v, AF.Square)
            yield
            bvv = lbank(L)
            self.mm(bvv[:, 0:TW], self.BONES.v, tmpb.v, start=True, stop=True)
            self.rsqrt_act(t1.v, bvv[:, 0:TW], scale=1.0 / 64.0, bias=self.eps_tiles[LNX_EPS])
            yield
            self.tt("dve", ys.v, ys.v, t1.v, ALU.mult)
            self.ts("dve", ys.v, ys.v, pcolv("rwkv_ln_g", m, 4), ALU.mult, pcolv("rwkv_ln_b", m, 4), ALU.add)
            yield
            self.tt("dve", ys.v, ys.v, bonus.v, ALU.add)
            self.tt("dve", yrw[:, m, g0:g0 + TW], ys.v, gt.v, ALU.mult)
            yield

        def lane_gen(L, ln, tb):
            for m in (ln, ln + 2):
                for hb in range(2):
                    yield from unit(L, tb, m, hb)

        def shared_lerp(j, tb):
            w = self.load_w(w_in[:, j * 128:(j + 1) * 128], 8, 128)
            bk = self.banks[0]
            for k in range(8):
                self.mm(bk.v, w[:, k, :], hT[:, k, :], start=(k == 0), stop=(k == 7))
            self.copy("dve", prawS[:, 0:1], carry[:, j:j + 1])
            self.copy("act", prawS[:, 1:513], bk.v)
            self.copy("dve", carry[:, j:j + 1], prawS[:, 512:513])
            self.tt("dve", dS.v, prawS[:, 0:512], prawS[:, 1:513], ALU.subtract)
            self.stt(lerpS.v, dS.v, pcolv("rwkv_mu", j, 14), prawS[:, 1:513], ALU.mult, ALU.add)

        for tb in range(4):
            self.bank_set = [0, 1]
            self.rmsnorm(tb, gc, hT.v, sq, rstd)
            shared_lerp(12, tb)
            self.act(lo12[0:64, 0, :], lerpS[0:64, :], AF.Tanh)
            self.copy("dve", lo12[64:128, 1, :], lerpS[64:128, :])
            shared_lerp(13, tb)
            self.act(sgl.v, lerpS.v, AF.Sigmoid)
            gens = [lane_gen(lanes[0], 0, tb), lane_gen(lanes[1], 1, tb)]
            alive = [True, True]
            while any(alive):
                for i in range(2):
                    if alive[i]:
                        try:
                            next(gens[i])
                        except StopIteration:
                            alive[i] = False
        self.bank_set = list(range(8))
        P.barrier()

    def finish(self):
        P = self.P
        cv = self.carver()
        sq = Buf(cv.b16(4096, "p (c t) -> p c t", c=8), "sq")
        rstd = Buf(cv.f32(512), "rstd")
        yT = Buf(cv.f32(4096, "p (c t) -> p c t", c=8), "yT")
        ostg = [Buf(cv.f32(1024), "ostg%d" % i) for i in range(2)]
        self.OUT = Buf(self.out_ap, "OUT")
        gc = self.pcol["final_norm_g"]
        n = 0
        for tb in range(4):
            if self.final_norm:
                self.rmsnorm(tb, gc, yT.v, sq, rstd)
                src = lambda c, t0: yT[:, c, t0:t0 + 128]
            else:
                src = lambda c, t0, tb=tb: View([self.xb[c][tb]], self.xb[c][tb].ap[:, t0:t0 + 128])
            for t4 in range(4):
                st = ostg[n % 2]
                n += 1
                for h in range(2):
                    bk = self.bank()
                    for c4 in range(4):
                        c = h * 4 + c4
                        self.transpose(bk[:, c4 * 128:(c4 + 1) * 128], src(c, t4 * 128), self.ident_f)
                    self.copy(self.evac_eng(), st[:, h * 512:(h + 1) * 512], bk.v)
                r0 = tb * 512 + t4 * 128
                P.dma(self.OUT[r0:r0 + 128, :], st.v)

    def build(self):
        self.alloc()
        self.make_eps()
        self.setup()
        self.moba_setup()
        self.rwkv_setup()
        self.P.barrier()
        for l in range(DEPTH):
            for stg in ["ffn1", "rwkv", "moba", "xattn", "ffn2"]:
                if (l, stg) not in self.stages:
                    continue
                if stg == "ffn1":
                    self.ffn(l, 1)
                elif stg == "ffn2":
                    self.ffn(l, 2)
                elif stg == "xattn":
                    self.xattn(l)
                elif stg == "moba":
                    self.moba(l)
                elif stg == "rwkv":
                    if RWKV_TWO_LANES:
                        self.rwkv2(l)
                    else:
                        self.rwkv(l)
        self.finish()
        st = self.P.emit()
        return self.nc, st


BUCKET_STARTS = [0, 1, 2, 3, 4, 5, 6, 7, 8, 9, 10, 11, 12, 13, 14, 15, 16, 21, 27, 35, 46, 59, 77, 99, 128, 166,
                 216, 280, 363, 470, 609, 790]


def make_oh():
    oh = np.zeros((32, 2048), np.float32)
    for c in range(512, 2048):
        d = c - 512
        b = 0
        for i, st in enumerate(BUCKET_STARTS):
            if d >= st:
                b = i
        oh[b, c] = 1.0
    return oh


ALL_STAGES = [(l, s) for l in range(DEPTH) for s in ["ffn1", "rwkv", "moba", "xattn", "ffn2"]]


def run(inputs, stages, final_norm=True, cores=8, trace=False, debug=False):
    k = K(stages, final_norm, debug)
    nc, st = k.build()
    in_maps = []
    oh = make_oh()
    for b in range(cores):
        m = {}
        for name, shape in PARAM_SPECS:
            a = np.asarray(inputs[name], dtype=np.float32)
            if name in ("x", "mem"):
                a = a[b]
            m[name] = np.ascontiguousarray(a)
        m["oh_tab"] = oh
        in_maps.append(m)
    res = run_bass_kernel_spmd(nc, in_maps, core_ids=list(range(cores)), trace=trace)
    out = np.stack([r["out"] for r in res.results], axis=0)
    if debug:
        return out, res, st, res.results[0]["dbg"]
    return out, res, st


def kernel(**inputs):
    out, _, _ = run(inputs, ALL_STAGES, True, 8)
    return out.astype(np.float32)
```
